# Optimizing a Trainium2 kernel written in Bass

```python
import jax, jax.numpy as jnp
from jax import lax
import numpy as np

D_MODEL = 1024
BATCH = 1
SEQ = 16384
DEPTH = 1
DEC_BATCH = 8
DEC_SEQ = 32
PAST_LEN = 4096

CHUNK = 64
MIX_WIDTH = D_MODEL
CONV_WIDTH = MIX_WIDTH // 2
CONV_HEADS = 8
CONV_K = 3
CONV_HIST = CONV_K - 1
POOL_WIDTH = MIX_WIDTH - CONV_WIDTH
POOL_WINDOWS = (2, 4, 8, 16)
POOL_GROUPS = len(POOL_WINDOWS)
POOL_GROUP = POOL_WIDTH // POOL_GROUPS
POOL_HIST = max(POOL_WINDOWS) - 1
IN_COLS = 3 * CONV_WIDTH + POOL_WIDTH
D_FF = 4 * D_MODEL
PLE_DIM = 256
EPS = 1e-6

kernel_name = "hybrid_conv_pool_streaming_encoder_step"


def _rms(x, g):
    x32 = x.astype(jnp.float32)
    y = x32 * lax.rsqrt(jnp.mean(x32 * x32, axis=-1, keepdims=True) + EPS) * g.astype(jnp.float32)
    return y.astype(x.dtype)


def _conv_mix(bg, cg, v, hist, w_conv):
    L = v.shape[1]
    h = cg * v
    hp = jnp.concatenate([hist.astype(h.dtype), h], axis=1)
    conv = w_conv[0] * hp[:, :L] + w_conv[1] * hp[:, 1:L + 1] + w_conv[2] * hp[:, 2:]
    return bg * conv, hp[:, -CONV_HIST:]


def _pool_mix(u, hist, start, w_pool, pool_scale):
    Bn, L, _ = u.shape
    up = jnp.concatenate([hist.astype(u.dtype), u], axis=1)
    c = jnp.cumsum(up.astype(jnp.float32), axis=1)
    c = jnp.pad(c, ((0, 0), (1, 0), (0, 0)))
    pos = start + jnp.arange(L)
    means = []
    for g, w in enumerate(POOL_WINDOWS):
        sl = slice(g * POOL_GROUP, (g + 1) * POOL_GROUP)
        s = c[:, POOL_HIST + 1:, sl] - c[:, POOL_HIST + 1 - w:POOL_HIST + 1 - w + L, sl]
        cnt = jnp.minimum(pos + 1, w).astype(jnp.float32)
        means.append(s / cnt[None, :, None])
    mean = jnp.concatenate(means, axis=-1)
    d = (mean - u.astype(jnp.float32)).astype(u.dtype).reshape(Bn, L, POOL_GROUPS, POOL_GROUP)
    y = jnp.einsum('blgc,gcd->blgd', d, w_pool).reshape(Bn, L, POOL_WIDTH) * pool_scale
    return y, up[:, -POOL_HIST:]


def _layer(x, p, conv_hist, pool_hist, start, g_mix, w_in, w_conv, w_pool, pool_scale, w_out,
           g_mlp, w_up, w_down, g_ple, w_ple, w_ple_gate):
    h = _rms(x, g_mix)
    z = h @ w_in
    bg = z[..., :CONV_WIDTH]
    cg = z[..., CONV_WIDTH:2 * CONV_WIDTH]
    v = z[..., 2 * CONV_WIDTH:3 * CONV_WIDTH]
    u = z[..., 3 * CONV_WIDTH:]
    a, new_conv = _conv_mix(bg, cg, v, conv_hist, w_conv)
    b, new_pool = _pool_mix(u, pool_hist, start, w_pool, pool_scale)
    x = x + jnp.concatenate([a, b], axis=-1) @ w_out
    f = jax.nn.relu(_rms(x, g_mlp) @ w_up)
    x = x + (f * f) @ w_down
    x = x + (p @ w_ple) * jax.nn.sigmoid(_rms(x, g_ple) @ w_ple_gate)
    return x, new_conv, new_pool


def setup_inputs(seed: int = 0) -> dict:
    key = jax.random.key(seed)
    ks = jax.random.split(key, 24)
    f32 = jnp.float32
    nrm = lambda k, s, sc: jax.random.normal(k, s, f32) * sc
    gain = lambda k, s: 1.0 + 0.05 * jax.random.normal(k, s, f32)
    return {
        "x_prompt": nrm(ks[0], (BATCH, SEQ, D_MODEL), 1.0),
        "x_sample": nrm(ks[1], (DEC_BATCH, DEC_SEQ, D_MODEL), 1.0),
        "state_conv": nrm(ks[2], (DEPTH, DEC_BATCH, CONV_HIST, CONV_WIDTH), 1.0),
        "state_pool": nrm(ks[3], (DEPTH, DEC_BATCH, POOL_HIST, POOL_WIDTH), 1.0),
        "p_prompt": nrm(ks[4], (DEPTH, BATCH, SEQ, PLE_DIM), 1.0),
        "p_sample": nrm(ks[5], (DEPTH, DEC_BATCH, DEC_SEQ, PLE_DIM), 1.0),
        "g_mix": gain(ks[6], (DEPTH, D_MODEL)),
        "w_in": nrm(ks[7], (DEPTH, D_MODEL, IN_COLS), D_MODEL ** -0.5),
        "w_conv": nrm(ks[8], (DEPTH, CONV_K, CONV_WIDTH), CONV_K ** -0.5),
        "w_pool": nrm(ks[9], (DEPTH, POOL_GROUPS, POOL_GROUP, POOL_GROUP), POOL_GROUP ** -0.5),
        "pool_scale": gain(ks[10], (DEPTH, POOL_WIDTH)),
        "w_out": nrm(ks[11], (DEPTH, MIX_WIDTH, D_MODEL), MIX_WIDTH ** -0.5),
        "g_mlp": gain(ks[12], (DEPTH, D_MODEL)),
        "w_up": nrm(ks[13], (DEPTH, D_MODEL, D_FF), D_MODEL ** -0.5),
        "w_down": nrm(ks[14], (DEPTH, D_FF, D_MODEL), D_FF ** -0.5),
        "g_ple": gain(ks[15], (DEPTH, D_MODEL)),
        "w_ple": nrm(ks[16], (DEPTH, PLE_DIM, D_MODEL), PLE_DIM ** -0.5),
        "w_ple_gate": nrm(ks[17], (DEPTH, D_MODEL, D_MODEL), D_MODEL ** -0.5),
        "g_final": gain(ks[18], (D_MODEL,)),
    }


def reference(x_prompt, x_sample, state_conv, state_pool, p_prompt, p_sample, g_mix, w_in, w_conv,
              w_pool, pool_scale, w_out, g_mlp, w_up, w_down, g_ple, w_ple, w_ple_gate, g_final):
    xp = x_prompt
    xs = x_sample
    conv_p_list, pool_p_list, conv_s_list, pool_s_list = [], [], [], []
    zero_conv = jnp.zeros((x_prompt.shape[0], CONV_HIST, CONV_WIDTH), x_prompt.dtype)
    zero_pool = jnp.zeros((x_prompt.shape[0], POOL_HIST, POOL_WIDTH), x_prompt.dtype)
    for i in range(DEPTH):
        params = (g_mix[i], w_in[i], w_conv[i], w_pool[i], pool_scale[i], w_out[i],
                  g_mlp[i], w_up[i], w_down[i], g_ple[i], w_ple[i], w_ple_gate[i])
        xp, cp, pp = _layer(xp, p_prompt[i], zero_conv, zero_pool, 0, *params)
        xs, cs, ps = _layer(xs, p_sample[i], state_conv[i], state_pool[i], PAST_LEN, *params)
        conv_p_list.append(cp)
        pool_p_list.append(pp)
        conv_s_list.append(cs)
        pool_s_list.append(ps)
    y_prompt = _rms(xp, g_final)
    y_sample = _rms(xs, g_final)
    new_conv_prompt = jnp.stack(conv_p_list)
    new_pool_prompt = jnp.stack(pool_p_list)
    new_conv_sample = jnp.stack(conv_s_list)
    new_pool_sample = jnp.stack(pool_s_list)
    return (y_prompt, y_sample, new_conv_prompt, new_pool_prompt, new_conv_sample, new_pool_sample)
```

```python
import numpy as np
import concourse.bass as bass
import concourse.mybir as mybir
from concourse.bass_utils import run_bass_kernel_spmd

F32 = mybir.dt.float32
BF16 = mybir.dt.bfloat16
AF = mybir.ActivationFunctionType
ALU = mybir.AluOpType

NCORES = 8
D = 1024
NPT = 2048
TS = 512
NSAMP = 32
NHALO = 16
DFF = 4096
NB = 8
FB = DFF // NB
EPS = 1e-6
WINDOWS = (2, 4, 8, 16)

ARENA_BYTES = 205 * 1024
STOP_AFTER_PHASE = 3


class Slot:
    __slots__ = ("w", "r")

    def __init__(self):
        self.w = None
        self.r = []


class Prog:
    ENGS = ("pe", "act", "dve", "pool", "sp")

    def __init__(self, nc):
        self.nc = nc
        self.ops = {e: [] for e in self.ENGS}
        self.sem = {}
        self.cnt = {e: 0 for e in self.ENGS}
        self.waited = {e: {} for e in self.ENGS}
        self.slots = {}
        self.dma_sems = []
        self.sem_objs = {}
        self.nsem = 0
        self.store_tokens = []
        self.pool_dmas = []
        self._cms = []
        for e in ("pe", "act", "dve", "pool"):
            self.sem[e] = self._new_sem("s_" + e)

    def _new_sem(self, name):
        cm = self.nc.semaphore(name)
        h = cm.__enter__()
        self._cms.append(cm)
        self.nsem += 1
        sid = self.nsem
        self.sem_objs[sid] = h
        return sid

    def close(self):
        for cm in reversed(self._cms):
            cm.__exit__(None, None, None)

    def slot(self, key):
        s = self.slots.get(key)
        if s is None:
            s = Slot()
            self.slots[key] = s
        return s

    def _deps(self, reads, writes):
        deps = []
        for k in reads:
            s = self.slot(k)
            if s.w is not None:
                deps.append(s.w)
        for k in writes:
            s = self.slot(k)
            if s.w is not None:
                deps.append(s.w)
            deps.extend(s.r)
        return deps

    def _waits(self, eng, deps):
        best = {}
        for (sid, val, src) in deps:
            if src == eng and eng == "pe":
                continue
            if val > best.get(sid, 0):
                best[sid] = val
        out = []
        wd = self.waited[eng]
        for sid, val in best.items():
            if wd.get(sid, 0) >= val:
                continue
            wd[sid] = val
            out.append((sid, val))
        return out

    def _commit(self, tok, reads, writes):
        for k in reads:
            self.slot(k).r.append(tok)
        for k in writes:
            s = self.slot(k)
            s.w = tok
            s.r = []

    def op(self, eng, fns, reads=(), writes=()):
        if not isinstance(fns, (list, tuple)):
            fns = [fns]
        waits = self._waits(eng, self._deps(reads, writes))
        self.cnt[eng] += 1
        tok = (self.sem[eng], self.cnt[eng], eng)
        self.ops[eng].append((waits, list(fns), (self.sem[eng], 1)))
        self._commit(tok, reads, writes)
        return tok

    def dma(self, queue, fn, reads=(), writes=(), store=False, after=()):
        deps = self._deps(reads, writes) + list(after)
        if queue == "pool":
            if len(self.pool_dmas) >= 4:
                deps.append(self.pool_dmas[-4])
        waits = self._waits(queue, deps)
        sid = self._new_sem("d%d" % self.nsem)
        tok = (sid, 16, "dma")
        self.ops[queue].append((waits, [fn], (sid, 16)))
        self._commit(tok, reads, writes)
        if queue == "pool":
            self.pool_dmas.append(tok)
        if store:
            self.store_tokens.append(tok)
        return tok

    def barrier(self):
        toks = [(self.sem[e], self.cnt[e], e) for e in ("pe", "act", "dve", "pool") if self.cnt[e] > 0]
        toks += self.store_tokens
        for e in self.ENGS:
            waits = self._waits(e, [t for t in toks if not (t[2] == e)])
            if waits:
                self.ops[e].append((waits, [], None))

    def final_wait(self):
        waits = self._waits("sp", self.store_tokens)
        if waits:
            self.ops["sp"].append((waits, [], None))

    def emit(self, block):
        nc = self.nc
        so = self.sem_objs

        def run(engname):
            def body(e):
                for waits, fns, inc in self.ops[engname]:
                    for sid, val in waits:
                        e.wait_ge(so[sid], val)
                    last = None
                    for f in fns:
                        last = f(e)
                    if inc is not None and last is not None:
                        last.then_inc(so[inc[0]], inc[1])
            return body

        block.sync(run("sp"))
        block.gpsimd(run("pool"))
        block.scalar(run("act"))
        block.vector(run("dve"))
        block.tensor(run("pe"))


class Bump:
    def __init__(self, base, limit):
        self.p = base
        self.limit = limit

    def take(self, nbytes):
        nbytes = (nbytes + 63) // 64 * 64
        o = self.p
        self.p += nbytes
        assert self.p <= self.limit, ("SBUF arena overflow", self.p, self.limit)
        return o


def build_program():
    nc = bass.Bass("TRN2", target_bir_lowering=False)

    def din(name, shape):
        return nc.dram_tensor(name, list(shape), F32, kind="ExternalInput").ap()

    def dout(name, shape):
        return nc.dram_tensor(name, list(shape), F32, kind="ExternalOutput").ap()

    xp = din("xp", (NPT, D)); xh = din("xh", (NHALO, D)); xs = din("xs", (NSAMP, D))
    pp = din("pp", (NPT, 256)); psm = din("psm", (NSAMP, 256))
    sconv = din("sconv", (2, 512)); spool = din("spool", (15, 512))
    invcnt = din("invcnt", (128, 64))
    w_in = din("w_in", (D, 2048)); w_out = din("w_out", (D, D)); w_up = din("w_up", (D, DFF))
    w_down = din("w_down", (DFF, D)); w_ple = din("w_ple", (256, D)); w_gate = din("w_gate", (D, D))
    w_pool = din("w_pool", (4, 128, 128))
    g_mix = din("g_mix", (D,)); g_mlp = din("g_mlp", (D,)); g_ple = din("g_ple", (D,)); g_final = din("g_final", (D,))
    w_conv = din("w_conv", (3, 512)); pool_scale = din("pool_scale", (512,))
    y_p = dout("y_p", (NPT, D)); y_s = dout("y_s", (NSAMP, D))
    ncp = dout("ncp", (2, 512)); npp = dout("npp", (15, 512)); ncs = dout("ncs", (2, 512)); nps = dout("nps", (15, 512))

    arena_cm = nc.sbuf_tensor("arena", [128, ARENA_BYTES // 4], F32)
    arena = arena_cm.__enter__()
    ps_cms = [nc.psum_tensor("ps%d" % i, [128, 512], F32) for i in range(8)]
    PS = [cm.__enter__() for cm in ps_cms]
    pg = Prog(nc)

    def vf32(off, n):
        return arena[:, off // 4: off // 4 + n]

    def vbf(off, n):
        return arena[:, off // 4: off // 4 + n // 2].bitcast(BF16)

    def ps_bf(bank):
        return PS[bank][:, 0:512].bitcast(BF16).rearrange("p (k n) -> p k n", k=2)

    pers = Bump(0, ARENA_BYTES)
    o = pers.take(17 * 4096); xr = vf32(o, 17 * 1024).rearrange("p (s d) -> p s d", s=17)
    o = pers.take(256); identb = vbf(o, 128)
    o = pers.take(512); identf = vf32(o, 128)
    o = pers.take(64); ccol = vf32(o, 16)
    o = pers.take(64); epsc = vf32(o, 1)
    o = pers.take(64); mhalf = vf32(o, 8)
    o = pers.take(256); invc = vf32(o, 64).rearrange("p (g n) -> p g n", g=4)
    NSTAT = 8
    o = pers.take(NSTAT * 96); stat = vf32(o, NSTAT * 24)
    o = pers.take(4096); gbc_a = vf32(o, 1024)
    o = pers.take(8192); up_b0 = vbf(o, 4096).rearrange("p (k n) -> p k n", k=8)
    o = pers.take(8192); dn_b0 = vbf(o, 4096).rearrange("p (k n) -> p k n", k=4)
    R0 = pers.p

    TILES = [("halo", NHALO, 1, NHALO, 15)] + [("p%d" % t, TS, 4, 128, 4 * t) for t in range(4)] + [("smp", NSAMP, 1, NSAMP, 16)]
    MAIN = TILES[1:]

    def xkey(slot):
        return ("x", slot)

    statflip = [0]

    def nstage_a_steps(tile, gbc, gkey, junk, hb, hbk="hb"):
        name, ntok, nsub, rows, xs0 = tile
        k = statflip[0]; statflip[0] = (statflip[0] + 1) % NSTAT
        ssq = stat[:, k * 24: k * 24 + 8]; std = stat[:, k * 24 + 8: k * 24 + 16]; rstd = stat[:, k * 24 + 16: k * 24 + 24]
        skey = ("stat", k)
        steps = []
        for s in range(nsub):
            steps.append(lambda s=s: pg.op("act", lambda e: e.activation(out=junk[:rows, :], in_=xr[:rows, xs0 + s, :], func=AF.Square,
                                                                        accum_out=ssq[:rows, s:s + 1]),
                                           reads=[xkey(xs0 + s)], writes=[("junk",), skey]))

        def powstep():
            pg.op("pool", lambda e: e.tensor_scalar(out=std[:rows, 0:nsub], in0=ssq[:rows, 0:nsub], scalar1=1.0 / D, scalar2=EPS,
                                                    op0=ALU.mult, op1=ALU.add), reads=[skey], writes=[skey])
            pg.op("pool", lambda e: e.tensor_tensor(out=rstd[:rows, 0:nsub], in0=std[:rows, 0:nsub], in1=mhalf[:rows, 0:nsub],
                                                    op=ALU.pow), reads=[skey, ("const",)], writes=[skey])
        steps.append(powstep)
        for s in range(nsub):
            steps.append(lambda s=s: pg.op("dve", lambda e: e.scalar_tensor_tensor(out=hb[:rows, s, :], in0=xr[:rows, xs0 + s, :],
                                                                                  scalar=rstd[:rows, s:s + 1], in1=gbc[:rows, :],
                                                                                  op0=ALU.mult, op1=ALU.mult),
                                           reads=[xkey(xs0 + s), skey, gkey], writes=[(hbk, s)]))
        return steps

    def nstage_a(tile, gbc, gkey, junk, hb, hbk="hb"):
        for st in nstage_a_steps(tile, gbc, gkey, junk, hb, hbk):
            st()

    def nstage_b_steps(tile, hb, hT, hTkey, tbanks, hbk="hb"):
        name, ntok, nsub, rows, xs0 = tile
        steps = []
        for cg in range(4):
            def step(cg=cg):
                bank = tbanks[cg % 2]
                tv = ps_bf(bank)
                fns = []
                for s in range(nsub):
                    for c in (2 * cg, 2 * cg + 1):
                        fns.append(lambda e, s=s, c=c, tv=tv: e.transpose(out=tv[:, c % 2, s * 128: s * 128 + rows],
                                                                          in_=hb[:rows, s, c * 128:(c + 1) * 128],
                                                                          identity=identb[:rows, :rows]))
                pg.op("pe", fns, reads=[(hbk, s) for s in range(nsub)] + [("const",)], writes=[("ps", bank)])
                pg.op("act", lambda e, tv=tv: e.activation(out=hT[:, 2 * cg:2 * cg + 2, 0:ntok], in_=tv[:, :, 0:ntok], func=AF.Copy),
                      reads=[("ps", bank)], writes=[hTkey])
            steps.append(step)
        return steps

    def nstage_b(tile, hb, hT, hTkey, tbanks, hbk="hb", split_evac=False):
        for st in nstage_b_steps(tile, hb, hT, hTkey, tbanks, hbk):
            st()

    def nstage(tile, gbc, gkey, junk, hb, hT, hTkey, tbanks):
        nstage_a(tile, gbc, gkey, junk, hb)
        nstage_b(tile, hb, hT, hTkey, tbanks)

    def setup_consts(crow):
        pg.op("pool", lambda e: e.memset(epsc, EPS), writes=[("const",)])
        pg.op("pool", lambda e: e.memset(mhalf, -0.5), writes=[("const",)])
        pg.op("pool", lambda e: e.iota(identf, pattern=[[1, 128]], base=0, channel_multiplier=-1,
                                       allow_small_or_imprecise_dtypes=True), writes=[("identf",)])
        pg.op("dve", lambda e: e.tensor_scalar(out=identf, in0=identf, scalar1=0.0, scalar2=None, op0=ALU.is_equal),
              reads=[("identf",)], writes=[("identf",)])
        pg.op("dve", lambda e: e.tensor_copy(out=identb, in_=identf), reads=[("identf",)], writes=[("const",)])
        pg.dma("sp", lambda e: e.dma_start(out=invc, in_=invcnt.rearrange("p (g n) -> p g n", g=4)), writes=[("invc",)])
        pg.dma("sp", lambda e: e.dma_start(out=crow[0:12, :], in_=w_conv.rearrange("k (c p) -> (k c) p", p=128)),
               writes=[("crow",)])
        pg.dma("sp", lambda e: e.dma_start(out=crow[12:16, :], in_=pool_scale.rearrange("(c p) -> c p", p=128)),
               writes=[("crow",)])
        pg.op("pe", lambda e: e.transpose(out=PS[7][:, 0:16], in_=crow[0:16, :], identity=identf[0:16, 0:16]),
              reads=[("crow",), ("identf",)], writes=[("ps", 7)])
        pg.op("dve", lambda e: e.tensor_copy(out=ccol, in_=PS[7][:, 0:16]), reads=[("ps", 7)], writes=[("ccol",)])

    r2 = Bump(R0, ARENA_BYTES)
    o = r2.take(4096); wple = vbf(o, 2048).rearrange("p (k n) -> p k n", k=2)
    o = r2.take(16384); wgate = vbf(o, 8192).rearrange("p (k n) -> p k n", k=8)
    W3END = r2.p
    o = r2.take(8192); up_b1 = vbf(o, 4096).rearrange("p (k n) -> p k n", k=8)
    o = r2.take(8192); dn_b1 = vbf(o, 4096).rearrange("p (k n) -> p k n", k=4)
    o = r2.take(8192); fT = vbf(o, 4096).rearrange("p (w k n) -> p w k n", w=2, k=4)
    o = r2.take(4096); rtmp = vf32(o, 1024).rearrange("p (s n) -> p s n", s=2)
    o = r2.take(1024)
    o = r2.take(2048); junk2 = vbf(o, 1024); JUNK2_OFF = o
    o = r2.take(16384); hb2 = vbf(o, 8192).rearrange("p (w s d) -> p w s d", w=2, s=4); HB2_OFF = o
    o = r2.take(8 * 2080 * 2); h2T = vbf(o, 8 * 2080).rearrange("p (k n) -> p k n", k=8)
    R2END = r2.p

    r3 = Bump(W3END, ARENA_BYTES)
    o = r3.take(4096); gbc_b = vf32(o, 1024)
    o = r3.take(16384); hT3 = vbf(o, 8192).rearrange("p (w k n) -> p w k n", w=2, k=8)
    o = r3.take(4096); pb = vbf(o, 2048).rearrange("p (w s n) -> p w s n", w=2, s=4)
    o = r3.take(2048); pT = vbf(o, 1024).rearrange("p (k n) -> p k n", k=2)
    assert r3.p <= JUNK2_OFF
    r3.p = JUNK2_OFF
    o = r3.take(2048); junk3 = vbf(o, 1024)
    o = r3.take(16384); hb3 = vbf(o, 8192).rearrange("p (w s d) -> p w s d", w=2, s=4); assert o == HB2_OFF
    o = r3.take(8192); sg = vf32(o, 2048).rearrange("p (s n) -> p s n", s=4)
    o = r3.take(8192); ptmp = vf32(o, 2048).rearrange("p (s n) -> p s n", s=4)
    o = r3.take(16384); ytile = vf32(o, 4096).rearrange("p (s n) -> p s n", s=4)

    r1 = Bump(R0, ARENA_BYTES)
    o = r1.take(32768); win = vbf(o, 16384).rearrange("p (k n) -> p k n", k=8)
    o = r1.take(16384); wout = vbf(o, 8192).rearrange("p (k n) -> p k n", k=8)
    o = r1.take(1024); wpl = vbf(o, 512).rearrange("p (g n) -> p g n", g=4)
    o = r1.take(2048); junk1 = vbf(o, 1024); assert o == JUNK2_OFF
    o = r1.take(8192); hb1 = vbf(o, 4096).rearrange("p (s d) -> p s d", s=4); assert o == HB2_OFF
    o = r1.take(8192); hT1 = vbf(o, 4096).rearrange("p (k n) -> p k n", k=8)
    o = r1.take(4096); vS = vf32(o, 1024).rearrange("p (s n) -> p s n", s=2)
    o = r1.take(4096); acc = vf32(o, 1024).rearrange("p (s n) -> p s n", s=2)
    o = r1.take(4 * 514 * 4); hcv = vf32(o, 4 * 514).rearrange("p (c n) -> p c n", c=4)
    o = r1.take(4 * 527 * 4); U = vf32(o, 4 * 527).rearrange("p (c n) -> p c n", c=4)
    o = r1.take(2 * 528 * 4); Stmp = vf32(o, 2 * 528).rearrange("p (s n) -> p s n", s=2)
    o = r1.take(4096); dT = vbf(o, 2048).rearrange("p (s n) -> p s n", s=4)
    o = r1.take(8192); mix = vbf(o, 4096).rearrange("p (k n) -> p k n", k=8)
    o = r1.take(64); d16 = vf32(o, 16)
    o = r1.take(4 * 17 * 4); so_col = vf32(o, 68).rearrange("p (c n) -> p c n", c=4)
    o = r1.take(2048); so_row = vf32(o, 512)
    o = r1.take(2048); srow = vf32(o, 512)
    o = r1.take(512); crow = vf32(o, 128)

    setup_consts(crow)
    pg.dma("sp", lambda e: e.dma_start(out=xr[0:NHALO, 15, :], in_=xh[:, :]), writes=[xkey(15)])
    pg.dma("sp", lambda e: e.dma_start(out=gbc_a, in_=g_mix.partition_broadcast(128)), writes=[("gbc_a",)])
    pg.dma("pool", lambda e: e.dma_start(out=wpl, in_=w_pool.rearrange("g c d -> c g d")), writes=[("wpl",)])
    for kk in range(4):
        pg.dma("pool", lambda e, kk=kk: e.dma_start(out=win[:, 2 * kk:2 * kk + 2, :],
                                                    in_=w_in[256 * kk:256 * (kk + 1), :].rearrange("(k p) n -> p k n", p=128)),
               writes=[("win", kk)])
    pg.dma("sp", lambda e: e.dma_start(out=xr[:, 0:4, :], in_=xp[0:TS, :].rearrange("(s p) d -> p s d", p=128)),
           writes=[xkey(s) for s in range(4)])
    pg.op("pool", lambda e: e.memset(srow[0:32, :], 0.0), writes=[("srow",)])
    pg.dma("sp", lambda e: e.dma_start(out=srow[0:2, :], in_=sconv[:, :]), writes=[("srow",)])
    pg.dma("sp", lambda e: e.dma_start(out=srow[2:17, :], in_=spool[:, :]), writes=[("srow",)])

    WIN_KEYS = [("win", kk) for kk in range(4)]
    WOUT_KEYS = [("wout", kk) for kk in range(2)]
    ZB = [2, 3, 4, 0, 1]
    zrot = [0]

    def zbank():
        b = ZB[zrot[0] % len(ZB)]
        zrot[0] += 1
        return b

    def zmm(tile, col0, bank):
        ntok = tile[1]
        fns = []
        for k in range(8):
            fns.append(lambda e, k=k: e.matmul(PS[bank][:, 0:ntok], lhsT=win[:, k, col0:col0 + 128], rhs=hT1[:, k, 0:ntok],
                                               start=(k == 0), stop=(k == 7)))
        pg.op("pe", fns, reads=[("win", kk) for kk in range(4)] + [("hT1",)], writes=[("ps", bank)])

    def pool_sums(tile, c):
        ntok = tile[1]
        nlev = c + 1
        L = ntok + 15
        src = U[:, c, :]
        skey_src = ("U", c)
        for lev in range(nlev):
            sh = 1 << lev
            lo = (1 << (lev + 1)) - 1
            if lev == nlev - 1:
                lo = 15
            dst = Stmp[:, lev % 2, :]
            pg.op("pool", lambda e, dst=dst, src=src, lo=lo, sh=sh, L=L: e.tensor_tensor(
                out=dst[:, lo:L], in0=src[:, lo:L], in1=src[:, lo - sh:L - sh], op=ALU.add),
                reads=[skey_src], writes=[("Stmp", lev % 2)])
            src = dst
            skey_src = ("Stmp", lev % 2)
        return src[:, 15:15 + ntok], skey_src

    def zstage(tile, full=True, first_prompt=False):
        name, ntok, nsub, rows, xs0 = tile
        for c in range(4):
            b = zbank()
            zmm(tile, 1536 + 128 * c, b)
            pg.op("act", lambda e, c=c, b=b: e.activation(out=U[:, c, 15:15 + ntok], in_=PS[b][:, 0:ntok], func=AF.Copy),
                  reads=[("ps", b)], writes=[("U", c)])
            if not full:
                continue
            sums, sk = pool_sums(tile, c)
            oth = 1 - (c % 2)
            tmpd = Stmp[:, oth, 15:15 + ntok]
            invw = 1.0 / WINDOWS[c]
            pg.op("pool", lambda e, sums=sums, tmpd=tmpd, invw=invw: e.tensor_scalar(out=tmpd, in0=sums, scalar1=invw, scalar2=0.0,
                                                                                     op0=ALU.mult, op1=ALU.add),
                  reads=[sk], writes=[("Stmp", oth)])
            if first_prompt:
                pg.op("pool", lambda e, c=c, sums=sums, tmpd=tmpd: e.tensor_tensor(out=tmpd[:, 0:16], in0=sums[:, 0:16],
                                                                                   in1=invc[:, c, :], op=ALU.mult),
                      reads=[sk, ("invc",), ("Stmp", oth)], writes=[("Stmp", oth)])
            pg.op("pool", lambda e, c=c, tmpd=tmpd: e.tensor_tensor(out=dT[:, c, 0:ntok], in0=tmpd, in1=U[:, c, 15:15 + ntok],
                                                                    op=ALU.subtract),
                  reads=[("Stmp", oth), ("U", c)], writes=[("dT", c)])
        for c in range(4):
            slot = c % 2
            b = zbank()
            zmm(tile, 1024 + 128 * c, b)
            pg.op("act", lambda e, b=b, slot=slot: e.activation(out=vS[:, slot, 0:ntok], in_=PS[b][:, 0:ntok], func=AF.Copy),
                  reads=[("ps", b)], writes=[("vS", slot)])
            b = zbank()
            zmm(tile, 512 + 128 * c, b)
            pg.op("dve", lambda e, b=b, c=c, slot=slot: e.tensor_tensor(out=hcv[:, c, 2:2 + ntok], in0=PS[b][:, 0:ntok],
                                                                        in1=vS[:, slot, 0:ntok], op=ALU.mult),
                  reads=[("ps", b), ("vS", slot)], writes=[("hcv", c)])
            if not full:
                continue
            pg.op("dve", lambda e, c=c, slot=slot: e.tensor_scalar(out=acc[:, slot, 0:ntok], in0=hcv[:, c, 0:ntok],
                                                                   scalar1=ccol[:, c:c + 1], scalar2=None, op0=ALU.mult),
                  reads=[("hcv", c), ("ccol",)], writes=[("acc", slot)])
            for tap in (1, 2):
                pg.op("dve", lambda e, c=c, slot=slot, tap=tap: e.scalar_tensor_tensor(
                    out=acc[:, slot, 0:ntok], in0=hcv[:, c, tap:tap + ntok], scalar=ccol[:, 4 * tap + c:4 * tap + c + 1],
                    in1=acc[:, slot, 0:ntok], op0=ALU.mult, op1=ALU.add),
                    reads=[("hcv", c), ("ccol",), ("acc", slot)], writes=[("acc", slot)])
            if c == 3 and full:
                for cc in range(4):
                    b2 = zbank()
                    pg.op("pe", lambda e, cc=cc, b2=b2: e.matmul(PS[b2][:, 0:ntok], lhsT=wpl[:, cc, :], rhs=dT[:, cc, 0:ntok],
                                                                 start=True, stop=True),
                          reads=[("wpl",), ("dT", cc)], writes=[("ps", b2)])
                    pg.op("act", lambda e, cc=cc, b2=b2: e.activation(out=mix[:, 4 + cc, 0:ntok], in_=PS[b2][:, 0:ntok], func=AF.Copy,
                                                                      scale=ccol[:, 12 + cc:13 + cc]),
                          reads=[("ps", b2), ("ccol",)], writes=[("mix", 4 + cc)])
            b = zbank()
            zmm(tile, 128 * c, b)
            pg.op("dve", lambda e, b=b, c=c, slot=slot: e.tensor_tensor(out=mix[:, c, 0:ntok], in0=PS[b][:, 0:ntok],
                                                                        in1=acc[:, slot, 0:ntok], op=ALU.mult),
                  reads=[("ps", b), ("acc", slot)], writes=[("mix", c)])

    def hist_roll(tile):
        ntok = tile[1]
        pg.op("dve", lambda e: e.tensor_copy(out=hcv[:, :, 0:2], in_=hcv[:, :, ntok:ntok + 2]),
              reads=[("hcv", c) for c in range(4)], writes=[("hcv", c) for c in range(4)])
        pg.op("pool", lambda e: e.tensor_copy(out=U[:, :, 0:15], in_=U[:, :, ntok:ntok + 15]),
              reads=[("U", c) for c in range(4)], writes=[("U", c) for c in range(4)])

    OB = [5, 6, 7, 2, 3, 4]
    orot = [0]

    def wout_stage(tile, pre_group=None, post_group=None, tail=()):
        name, ntok, nsub, rows, xs0 = tile
        pre_group = pre_group or {}
        post_group = post_group or {}
        gi = -1
        for s in range(nsub):
            for hf in range(2):
                gi += 1
                for st in pre_group.get(gi, ()):
                    st()
                b = OB[orot[0] % len(OB)]; orot[0] += 1
                fns = []
                korder = (4, 5, 6, 7, 0, 1, 2, 3)
                for ki, k in enumerate(korder):
                    fns.append(lambda e, k=k, ki=ki, s=s, hf=hf, b=b: e.matmul(PS[b][0:rows, :], lhsT=mix[:, k, s * 128:s * 128 + rows],
                                                                               rhs=wout[:, k, hf * 512:(hf + 1) * 512],
                                                                               start=(ki == 0), stop=(ki == 7)))
                pg.op("pe", fns, reads=[("mix", k) for k in range(8)] + WOUT_KEYS, writes=[("ps", b)])
                pg.op("dve", lambda e, s=s, hf=hf, b=b: e.tensor_tensor(out=xr[:rows, xs0 + s, hf * 512:(hf + 1) * 512],
                                                                        in0=xr[:rows, xs0 + s, hf * 512:(hf + 1) * 512],
                                                                        in1=PS[b][0:rows, :], op=ALU.add),
                      reads=[("ps", b), xkey(xs0 + s)], writes=[xkey(xs0 + s)])
                for st in post_group.get(gi, ()):
                    st()
        for st in tail:
            st()

    def state_out(tile, dconv, dpool):
        ntok = tile[1]
        pg.op("dve", lambda e: e.tensor_copy(out=so_col[:, :, 0:2], in_=hcv[:, :, ntok:ntok + 2]),
              reads=[("hcv", c) for c in range(4)], writes=[("so_col",)])
        pg.op("dve", lambda e: e.tensor_copy(out=so_col[:, :, 2:17], in_=U[:, :, ntok:ntok + 15]),
              reads=[("U", c) for c in range(4)], writes=[("so_col",)])
        fns = [lambda e, c=c: e.transpose(out=PS[7][0:17, c * 128:(c + 1) * 128], in_=so_col[:, c, :], identity=identf[:, :])
               for c in range(4)]
        pg.op("pe", fns, reads=[("so_col",), ("identf",)], writes=[("ps", 7)])
        pg.op("act", lambda e: e.activation(out=so_row[0:17, :], in_=PS[7][0:17, :], func=AF.Copy),
              reads=[("ps", 7)], writes=[("so_row",)])
        pg.dma("sp", lambda e: e.dma_start(out=dconv[:, :], in_=so_row[0:2, :]), reads=[("so_row",)], store=True)
        pg.dma("sp", lambda e: e.dma_start(out=dpool[:, :], in_=so_row[2:17, :]), reads=[("so_row",)], store=True)

    def state_in():
        fns = [lambda e, c=c: e.transpose(out=PS[7][:, c * 18:(c + 1) * 18], in_=srow[0:18, c * 128:(c + 1) * 128],
                                          identity=identf[0:18, 0:18]) for c in range(4)]
        pg.op("pe", fns, reads=[("srow",), ("identf",)], writes=[("ps", 7)])
        pv = PS[7][:, 0:72].rearrange("p (c n) -> p c n", c=4)
        pg.op("dve", lambda e: e.tensor_copy(out=hcv[:, :, 0:2], in_=pv[:, :, 0:2]),
              reads=[("ps", 7)] + [("hcv", c) for c in range(4)], writes=[("hcv", c) for c in range(4)])
        pg.op("dve", lambda e: e.tensor_copy(out=U[:, :, 0:15], in_=pv[:, :, 2:17]),
              reads=[("ps", 7)] + [("U", c) for c in range(4)], writes=[("U", c) for c in range(4)])

    def nst1_a(tile):
        nstage_a(tile, gbc_a, ("gbc_a",), junk1, hb1)

    def nst1_a_steps(tile):
        return nstage_a_steps(tile, gbc_a, ("gbc_a",), junk1, hb1)

    def nst1_b(tile):
        nstage_b(tile, hb1, hT1, ("hT1",), (0, 1))

    def nst3_a_steps(i, tile):
        return nstage_a_steps(tile, gbc_a, ("gbc_a",), junk3, hb3[:, i % 2, :, :], hbk="hb2_%d" % (i % 2))

    def nst3_b(i, tile):
        nstage_b(tile, hb3[:, i % 2, :, :], hT3[:, i % 2, :, :], ("hT3", i % 2), (0, 1), hbk="hb2_%d" % (i % 2))

    nsteps = {}

    def get_n(k):
        if k not in nsteps:
            st = nst3_a_steps(k, MAIN[k]); n = MAIN[k][2]
            nsteps[k] = (st[:n], st[n], st[n + 1:])
        return nsteps[k]

    halo = TILES[0]
    nst1_a(halo); nst1_b(halo)
    nst1_a(MAIN[0])
    for kk in range(2):
        pg.dma("pool", lambda e, kk=kk: e.dma_start(out=wout[:, 4 * kk:4 * kk + 4, :],
                                                    in_=w_out[512 * kk:512 * (kk + 1), :].rearrange("(k p) n -> p k n", p=128)),
               writes=[("wout", kk)])
    zstage(halo, full=False)
    halo_done = (pg.sem["pe"], pg.cnt["pe"], "pe")
    hist_roll(halo)
    for t in (1, 2):
        pg.dma("sp", lambda e, t=t: e.dma_start(out=xr[:, 4 * t:4 * t + 4, :],
                                                in_=xp[TS * t:TS * (t + 1), :].rearrange("(s p) d -> p s d", p=128)),
               writes=[xkey(4 * t + s) for s in range(4)], after=[halo_done])
    pg.dma("sp", lambda e: e.dma_start(out=xr[0:NSAMP, 16, :], in_=xs[:, :]), writes=[xkey(16)], after=[halo_done])
    pg.dma("sp", lambda e: e.dma_start(out=xr[:, 12:16, :], in_=xp[TS * 3:TS * 4, :].rearrange("(s p) d -> p s d", p=128)),
           writes=[xkey(12 + s) for s in range(4)])
    pg.dma("pool", lambda e: e.dma_start(out=up_b0, in_=w_up[:, 0:FB].rearrange("(k p) n -> p k n", p=128)), writes=[("up", 0)])
    pg.dma("pool", lambda e: e.dma_start(out=dn_b0, in_=w_down[0:FB, :].rearrange("(k p) n -> p k n", p=128)), writes=[("dn", 0)])

    st1 = {}

    def get1(k):
        if k not in st1:
            st = nst1_a_steps(MAIN[k]); n = MAIN[k][2]
            st1[k] = (st[:n], st[n], st[n + 1:])
        return st1[k]

    nst1_b(MAIN[0])
    for i, tile in enumerate(MAIN):
        nxt = MAIN[i + 1] if i + 1 < len(MAIN) else None
        if nxt is not None and i >= 1:
            for st in get1(i + 1)[2]:
                st()
        if tile[0] == "smp":
            state_in()
        if tile[0] == "p3":
            pre0 = nstage_a_steps(MAIN[0], gbc_a, ("gbc_a",), junk1, hb1, hbk="hb")
            pre1 = nstage_a_steps(MAIN[1], gbc_a, ("gbc_a",), junk1, hb2[:, 1, :, :], hbk="hb2_1")
        zstage(tile, full=True, first_prompt=(i == 0))
        if tile[0] == "smp":
            for st in pre0[MAIN[0][2] + 1:]:
                st()
            pre2 = nstage_a_steps(MAIN[2], gbc_a, ("gbc_a",), junk1, hb2[:, 0, :, :], hbk="hb2_0")
            pre3 = nstage_a_steps(MAIN[3], gbc_a, ("gbc_a",), junk1, hb2[:, 1, :, :], hbk="hb2_1")
            for st in pre2[:MAIN[2][2] + 1] + pre3[:MAIN[3][2] + 1]:
                st()
        pre_g, post_g, tail = {}, {}, []
        ngrp = 2 * tile[2]
        if nxt is not None and i == 0:
            sq, pw, sc = get1(1)
            for q, st in enumerate(sq):
                pre_g.setdefault(min(ngrp - 1, q // 2), []).append(st)
            post_g.setdefault(min(ngrp - 1, 1), []).append(pw)
            for q, st in enumerate(sc):
                post_g.setdefault(min(ngrp - 1, 2 + q), []).append(st)
            tsteps = nstage_b_steps(nxt, hb1, hT1, ("hT1",), (0, 1))
            post_g.setdefault(min(ngrp - 1, 6), []).append(tsteps[0])
            post_g.setdefault(min(ngrp - 1, 7), []).append(tsteps[1])
            tail += tsteps[2:]
        elif nxt is not None:
            tsteps = nstage_b_steps(nxt, hb1, hT1, ("hT1",), (0, 1))
            for q, st in enumerate(tsteps):
                post_g.setdefault(min(ngrp - 1, 2 * q + 1), []).append(st)
        if i + 2 < len(MAIN):
            sq, pw, _ = get1(i + 2)
            for q, st in enumerate(sq):
                pre_g.setdefault(min(ngrp - 1, 2 * q), []).append(st)
            tail.append(pw)
        if tile[0] == "p3":
            n0, n1 = MAIN[0][2], MAIN[1][2]
            for q, st in enumerate(pre0[:n0] + pre1[:n1]):
                pre_g.setdefault(min(ngrp - 1, q), []).append(st)
            tail += [pre0[n0], pre1[n1]]
        wout_stage(tile, pre_g, post_g, tail)
        if nxt is not None and nxt[0] == "smp":
            pg.dma("sp", lambda e: e.dma_start(out=gbc_a, in_=g_mlp.partition_broadcast(128)), writes=[("gbc_a",)])
        if tile[0] == "p3":
            state_out(tile, ncp, npp)
        if tile[0] == "smp":
            state_out(tile, ncs, nps)
        else:
            hist_roll(tile)

    if STOP_AFTER_PHASE >= 2:
        pg.barrier()
        UPB = [up_b0, up_b1]; DNB = [dn_b0, dn_b1]

        tokoff = {}
        off = 0
        for tile in MAIN:
            tokoff[tile[0]] = off
            off += tile[1]

        def load_w2(j):
            pg.dma("pool", lambda e, j=j: e.dma_start(out=UPB[j % 2], in_=w_up[:, j * FB:(j + 1) * FB].rearrange("(k p) n -> p k n", p=128)),
                   writes=[("up", j % 2)])
            pg.dma("pool", lambda e, j=j: e.dma_start(out=DNB[j % 2], in_=w_down[j * FB:(j + 1) * FB, :].rearrange("(k p) n -> p k n", p=128)),
                   writes=[("dn", j % 2)])

        UB = [2, 3, 4]; DB = [5, 6, 7]
        UB4 = [2, 3, 4, 0]; DB4 = [5, 6, 7, 1]
        urot = [0]; drot = [0]

        def up_stage(j, tile, w, post=()):
            name, ntok, nsub, rows, xs0 = tile
            t0 = tokoff[name]
            post = list(post)
            for c in range(4):
                ub = UB if j == 0 else UB4
                b = ub[urot[0] % len(ub)]; urot[0] += 1
                fns = [lambda e, k=k, c=c, b=b: e.matmul(PS[b][:, 0:ntok], lhsT=UPB[j % 2][:, k, c * 128:(c + 1) * 128],
                                                         rhs=h2T[:, k, t0:t0 + ntok], start=(k == 0), stop=(k == 7))
                       for k in range(8)]
                pg.op("pe", fns, reads=[("up", j % 2), ("h2T", name)], writes=[("ps", b)])
                rs = c % 2
                pg.op("act", lambda e, b=b, rs=rs: e.activation(out=rtmp[:, rs, 0:ntok], in_=PS[b][:, 0:ntok], func=AF.Relu),
                      reads=[("ps", b)], writes=[("rtmp", rs)])
                pg.op("dve", lambda e, c=c, rs=rs: e.tensor_tensor(out=fT[:, w % 2, c, 0:ntok], in0=rtmp[:, rs, 0:ntok],
                                                                   in1=rtmp[:, rs, 0:ntok], op=ALU.mult),
                      reads=[("rtmp", rs)], writes=[("fT", w % 2, c)])
                if c < len(post):
                    post[c]()

        def down_stage(j, tile, w):
            name, ntok, nsub, rows, xs0 = tile
            for s in range(nsub):
                for hf in range(2):
                    db = DB if j == 0 else DB4
                    b = db[drot[0] % len(db)]; drot[0] += 1
                    fns = [lambda e, k=k, s=s, hf=hf, b=b: e.matmul(PS[b][0:rows, :], lhsT=fT[:, w % 2, k, s * 128:s * 128 + rows],
                                                                    rhs=DNB[j % 2][:, k, hf * 512:(hf + 1) * 512],
                                                                    start=(k == 0), stop=(k == 3))
                           for k in range(4)]
                    pg.op("pe", fns, reads=[("fT", w % 2, k) for k in range(4)] + [("dn", j % 2)], writes=[("ps", b)])
                    pg.op("dve", lambda e, s=s, hf=hf, b=b: e.tensor_tensor(out=xr[:rows, xs0 + s, hf * 512:(hf + 1) * 512],
                                                                            in0=xr[:rows, xs0 + s, hf * 512:(hf + 1) * 512],
                                                                            in1=PS[b][0:rows, :], op=ALU.add),
                          reads=[("ps", b), xkey(xs0 + s)], writes=[xkey(xs0 + s)])

        pg.dma("pool", lambda e: e.dma_start(out=wple, in_=w_ple.rearrange("(k p) n -> p k n", p=128)), writes=[("wple",)])

        def n2a(ti):
            nstage_a(MAIN[ti], gbc_a, ("gbc_a",), junk2, hb2[:, ti % 2, :, :], hbk="hb2_%d" % (ti % 2))

        def n2b(ti):
            tile = MAIN[ti]
            t0 = tokoff[tile[0]]
            nstage_b(tile, hb2[:, ti % 2, :, :], h2T[:, :, t0:t0 + tile[1]], ("h2T", tile[0]), (0, 1), hbk="hb2_%d" % (ti % 2))

        prev = None
        w = 0
        for j in range(NB):
            if j == 2:
                for kk in range(2):
                    pg.dma("pool", lambda e, kk=kk: e.dma_start(out=wgate[:, 4 * kk:4 * kk + 4, :],
                                                                in_=w_gate[512 * kk:512 * (kk + 1), :].rearrange("(k p) n -> p k n", p=128)),
                           writes=[("wgate", kk)])
            for ti, tile in enumerate(MAIN):
                if j == 0 and ti == 0:
                    for st in pre1[MAIN[1][2] + 1:]:
                        st()
                    n2b(0)
                if j == 0 and ti + 1 < len(MAIN):
                    nt_ = MAIN[ti + 1]
                    t0_ = tokoff[nt_[0]]
                    k_ = (ti + 1) % 2
                    tsteps = nstage_b_steps(nt_, hb2[:, k_, :, :], h2T[:, :, t0_:t0_ + nt_[1]], ("h2T", nt_[0]), (0, 1),
                                            hbk="hb2_%d" % k_)
                    if ti + 2 == 2:
                        ssteps = pre2[MAIN[2][2] + 1:]
                    elif ti + 2 == 3:
                        ssteps = pre3[MAIN[3][2] + 1:]
                    elif ti + 2 < len(MAIN):
                        ssteps = [lambda: n2a(ti + 2)]
                    else:
                        ssteps = []
                    post_ = []
                    for c_ in range(4):
                        def both(c_=c_, tsteps=tsteps, ssteps=ssteps):
                            tsteps[c_]()
                            if c_ < len(ssteps):
                                ssteps[c_]()
                        post_.append(both)
                    up_stage(j, tile, w, post=post_)
                else:
                    up_stage(j, tile, w)
                if prev is not None:
                    down_stage(*prev)
                    if prev[0] == NB - 1 and prev[1] in (MAIN[0], MAIN[1]):
                        kk = 0 if prev[1] is MAIN[0] else 1
                        sq, pw, sc = get_n(kk)
                        for st in sq:
                            st()
                        pw()
                        for st in sc:
                            st()
                if j == 1 and ti == 0:
                    pg.dma("sp", lambda e: e.dma_start(out=gbc_a, in_=g_ple.partition_broadcast(128)), writes=[("gbc_a",)])

                if tile is MAIN[0] and j + 1 < NB:
                    load_w2(j + 1)
                prev = (j, tile, w)
                w += 1
        down_stage(*prev)

    if STOP_AFTER_PHASE >= 3:
        pg.barrier()
        pg.dma("sp", lambda e: e.dma_start(out=gbc_b, in_=g_final.partition_broadcast(128)), writes=[("gbc_b",)])

        def load_p(i, tile):
            name, ntok, nsub, rows, xs0 = tile
            if name == "smp":
                pg.dma("pool", lambda e: e.dma_start(out=pb[0:NSAMP, i % 2, 0, :], in_=psm[:, :]), writes=[("pb", i % 2)])
            else:
                t = int(name[1:])
                pg.dma("pool", lambda e, t=t: e.dma_start(out=pb[:, i % 2, :, :],
                                                          in_=pp[TS * t:TS * (t + 1), :].rearrange("(s p) n -> p s n", p=128)),
                       writes=[("pb", i % 2)])


        yrot = [0]; srot = [0]

        def ple_pre(i, tile):
            name, ntok, nsub, rows, xs0 = tile
            tv = ps_bf(0)
            fns = []
            for s in range(nsub):
                for k in range(2):
                    fns.append(lambda e, s=s, k=k: e.transpose(out=tv[:, k, s * 128:s * 128 + rows],
                                                               in_=pb[:rows, i % 2, s, k * 128:(k + 1) * 128],
                                                               identity=identb[:rows, :rows]))
            pg.op("pe", fns, reads=[("pb", i % 2), ("const",)], writes=[("ps", 0)])
            pg.op("act", lambda e: e.activation(out=pT[:, :, 0:ntok], in_=tv[:, :, 0:ntok], func=AF.Copy),
                  reads=[("ps", 0)], writes=[("pT",)])

        def ple_main(i, tile, bg_act, bg_dve, post_group=None):
            name, ntok, nsub, rows, xs0 = tile
            post_group = post_group or {}
            ngrp = 2 * nsub
            pa = -(-len(bg_act) // ngrp) if bg_act else 0
            pd = -(-len(bg_dve) // ngrp) if bg_dve else 0
            gi = 0
            for s in range(nsub):
                for hf in range(2):
                    for st in bg_act[gi * pa:(gi + 1) * pa]:
                        st()
                    for st in bg_dve[gi * pd:(gi + 1) * pd]:
                        st()
                    gi += 1
                    q = srot[0] % 4; srot[0] += 1
                    bp = (2, 3, 7, 1)[q]; bg = (4, 5, 6, 0)[q]
                    fns = [lambda e, k=k, s=s, hf=hf, bp=bp: e.matmul(PS[bp][0:rows, :], lhsT=pT[:, k, s * 128:s * 128 + rows],
                                                                      rhs=wple[:, k, hf * 512:(hf + 1) * 512],
                                                                      start=(k == 0), stop=(k == 1)) for k in range(2)]
                    pg.op("pe", fns, reads=[("pT",), ("wple",)], writes=[("ps", bp)])
                    fns = [lambda e, k=k, s=s, hf=hf, bg=bg: e.matmul(PS[bg][0:rows, :], lhsT=hT3[:, i % 2, k, s * 128:s * 128 + rows],
                                                                      rhs=wgate[:, k, hf * 512:(hf + 1) * 512],
                                                                      start=(k == 0), stop=(k == 7)) for k in range(8)]
                    pg.op("pe", fns, reads=[("hT3", i % 2), ("wgate", 0), ("wgate", 1)], writes=[("ps", bg)])
                    pg.op("act", lambda e, q=q, bg=bg: e.activation(out=sg[:rows, q, :], in_=PS[bg][0:rows, :], func=AF.Sigmoid),
                          reads=[("ps", bg)], writes=[("sg", q)])
                    pg.op("dve", lambda e, q=q, bp=bp: e.tensor_tensor(out=ptmp[:rows, q, :], in0=PS[bp][0:rows, :],
                                                                       in1=sg[:rows, q, :], op=ALU.mult),
                          reads=[("ps", bp), ("sg", q)], writes=[("ptmp", q)])
                    pg.op("pool", lambda e, s=s, hf=hf, q=q: e.tensor_tensor(out=xr[:rows, xs0 + s, hf * 512:(hf + 1) * 512],
                                                                             in0=xr[:rows, xs0 + s, hf * 512:(hf + 1) * 512],
                                                                             in1=ptmp[:rows, q, :], op=ALU.add),
                          reads=[("ptmp", q), xkey(xs0 + s)], writes=[xkey(xs0 + s)])
                    for st in post_group.get(gi - 1, ()):
                        st()

        def final_steps(i, tile):
            name, ntok, nsub, rows, xs0 = tile
            k = statflip[0]; statflip[0] = (statflip[0] + 1) % NSTAT
            ssq = stat[:, k * 24: k * 24 + 8]; std = stat[:, k * 24 + 8: k * 24 + 16]; rstd = stat[:, k * 24 + 16: k * 24 + 24]
            skey = ("stat", k)
            steps = []
            for s in range(nsub):
                steps.append(lambda s=s: pg.op("act", lambda e: e.activation(out=junk3[:rows, :], in_=xr[:rows, xs0 + s, :], func=AF.Square,
                                                                            accum_out=ssq[:rows, s:s + 1]),
                                               reads=[xkey(xs0 + s)], writes=[("junk",), skey]))

            def powstep():
                pg.op("pool", lambda e: e.tensor_scalar(out=std[:rows, 0:nsub], in0=ssq[:rows, 0:nsub], scalar1=1.0 / D, scalar2=EPS,
                                                        op0=ALU.mult, op1=ALU.add), reads=[skey], writes=[skey])
                pg.op("pool", lambda e: e.tensor_tensor(out=rstd[:rows, 0:nsub], in0=std[:rows, 0:nsub], in1=mhalf[:rows, 0:nsub],
                                                        op=ALU.pow), reads=[skey, ("const",)], writes=[skey])
            steps.append(powstep)

            def ystep(s):
                ys = yrot[0] % 4; yrot[0] += 1
                pg.op("dve", lambda e: e.scalar_tensor_tensor(out=ytile[:rows, ys, :], in0=xr[:rows, xs0 + s, :],
                                                              scalar=rstd[:rows, s:s + 1], in1=gbc_b[:rows, :],
                                                              op0=ALU.mult, op1=ALU.mult),
                      reads=[xkey(xs0 + s), skey, ("gbc_b",)], writes=[("ytile", ys)])
                if name == "smp":
                    pg.dma("sp", lambda e: e.dma_start(out=y_s[:, :], in_=ytile[0:NSAMP, ys, :]), reads=[("ytile", ys)], store=True)
                else:
                    r0 = (xs0 + s) * 128
                    pg.dma("sp", lambda e: e.dma_start(out=y_p[r0:r0 + 128, :], in_=ytile[:, ys, :]), reads=[("ytile", ys)], store=True)
            for s in range(nsub):
                steps.append(lambda s=s: ystep(s))
            return steps

        def merge(a, b):
            out = []
            for j in range(max(len(a), len(b))):
                if j < len(a):
                    out.append(a[j])
                if j < len(b):
                    out.append(b[j])
            return out

        nT = len(MAIN)
        NSUB = [t[2] for t in MAIN]
        fsteps = {}

        def get_f(k):
            if k not in fsteps:
                st = final_steps(k, MAIN[k]); n = NSUB[k]
                fsteps[k] = (st[:n], st[n], st[n + 1:])
            return fsteps[k]

        def run(steps):
            for st in steps:
                st()

        load_p(0, MAIN[0])
        nst3_b(0, MAIN[0]); ple_pre(0, MAIN[0])
        for i, tile in enumerate(MAIN):
            if i + 1 < nT:
                load_p(i + 1, MAIN[i + 1])
            bg_act, bg_dve, pows = [], [], []
            post_g = {}
            ngrp = 2 * tile[2]
            nF = 0
            if i >= 1:
                sq, pwF, yF = get_f(i - 1); bg_act += sq; nF = len(sq)
            if i + 2 < nT:
                sq, pw, _ = get_n(i + 2); bg_act += sq; pows.append(pw)
            if i >= 1 and i + 1 < nT:
                bg_dve += get_n(i + 1)[2]
            tail = []
            if i >= 1 and i == nT - 1:
                bg_act.append(pwF)
                tail += list(yF)
            elif i >= 1:
                pa = -(-len(bg_act) // ngrp)
                gF = (nF - 1) // pa
                post_g.setdefault(gF, []).append(pwF)
                for q, st in enumerate(yF):
                    if gF + 1 + q < ngrp:
                        post_g.setdefault(gF + 1 + q, []).append(st)
                    else:
                        tail.append(st)
            if i + 1 < nT:
                k1 = (i + 1) % 2
                tsteps = nstage_b_steps(MAIN[i + 1], hb3[:, k1, :, :], hT3[:, k1, :, :], ("hT3", k1), (0, 1), hbk="hb2_%d" % k1)
                for q, st in enumerate(tsteps):
                    post_g.setdefault(min(ngrp - 1, max(0, ngrp - 4) + q), []).append(st)
            ple_main(i, tile, bg_act, bg_dve, post_g)
            for pw in pows:
                pw()
            run(tail)
            if i + 1 < nT:
                ple_pre(i + 1, MAIN[i + 1])
        sq, pw, sc = get_f(nT - 1)
        run(sq); pw()
        run(sc)

    if STOP_AFTER_PHASE < 3:
        for t in range(4):
            pg.dma("sp", lambda e, t=t: e.dma_start(out=y_p[TS * t:TS * (t + 1), :].rearrange("(s p) d -> p s d", p=128),
                                                    in_=xr[:, 4 * t:4 * t + 4, :]),
                   reads=[xkey(4 * t + s) for s in range(4)], store=True)
        pg.dma("sp", lambda e: e.dma_start(out=y_s[:, :], in_=xr[0:NSAMP, 16, :]), reads=[xkey(16)], store=True)

    pg.final_wait()
    with nc.Block() as block:
        pg.emit(block)
    pg.close()
    for cm in reversed(ps_cms):
        cm.__exit__(None, None, None)
    arena_cm.__exit__(None, None, None)
    return nc


def _invcnt(core):
    out = np.empty((128, 4, 16), np.float32)
    for g, w in enumerate(WINDOWS):
        if core == 0:
            cnt = np.minimum(np.arange(16) + 1, w).astype(np.float32)
        else:
            cnt = np.full(16, float(w), np.float32)
        out[:, g, :] = (1.0 / cnt)[None, :]
    return out.reshape(128, 64)


def kernel(x_prompt, x_sample, state_conv, state_pool, p_prompt, p_sample, g_mix, w_in, w_conv,
           w_pool, pool_scale, w_out, g_mlp, w_up, w_down, g_ple, w_ple, w_ple_gate, g_final):
    f = lambda a: np.ascontiguousarray(np.asarray(a, dtype=np.float32))
    xpr = f(x_prompt)[0]; xsm = f(x_sample); ppr = f(p_prompt)[0, 0]; psmp = f(p_sample)[0]
    sc = f(state_conv)[0]; spl = f(state_pool)[0]
    shared = {
        "w_in": f(w_in)[0], "w_out": f(w_out)[0], "w_up": f(w_up)[0], "w_down": f(w_down)[0],
        "w_ple": f(w_ple)[0], "w_gate": f(w_ple_gate)[0], "w_pool": f(w_pool)[0],
        "g_mix": f(g_mix)[0], "g_mlp": f(g_mlp)[0], "g_ple": f(g_ple)[0], "g_final": f(g_final),
        "w_conv": f(w_conv)[0], "pool_scale": f(pool_scale)[0],
    }
    in_maps = []
    for c in range(NCORES):
        m = dict(shared)
        m["xp"] = xpr[c * NPT:(c + 1) * NPT]
        m["xh"] = xpr[c * NPT - NHALO:c * NPT] if c > 0 else np.zeros((NHALO, D), np.float32)
        m["xs"] = xsm[c]
        m["pp"] = ppr[c * NPT:(c + 1) * NPT]
        m["psm"] = psmp[c]
        m["sconv"] = sc[c]
        m["spool"] = spl[c]
        m["invcnt"] = _invcnt(c)
        in_maps.append(m)
    nc = build_program()
    res = run_bass_kernel_spmd(nc, in_maps, core_ids=list(range(NCORES)))
    rs = res.results
    y_prompt = np.concatenate([rs[c]["y_p"] for c in range(NCORES)], axis=0)[None]
    y_sample = np.stack([rs[c]["y_s"] for c in range(NCORES)], axis=0)
    ncp = rs[NCORES - 1]["ncp"][None, None]
    npp = rs[NCORES - 1]["npp"][None, None]
    ncs = np.stack([rs[c]["ncs"] for c in range(NCORES)], axis=0)[None]
    nps = np.stack([rs[c]["nps"] for c in range(NCORES)], axis=0)[None]
    return (y_prompt.astype(np.float32), y_sample.astype(np.float32), ncp.astype(np.float32),
            npp.astype(np.float32), ncs.astype(np.float32), nps.astype(np.float32))
```

```python
import numpy as np
import concourse.bass as bass
import concourse.mybir as mybir
from concourse.bass_utils import run_bass_kernel_spmd

F32 = mybir.dt.float32
BF16 = mybir.dt.bfloat16
AF = mybir.ActivationFunctionType
ALU = mybir.AluOpType

NCORES = 8
D = 1024
NPT = 2048
TS = 512
NSAMP = 32
NHALO = 16
DFF = 4096
NB = 8
FB = DFF // NB
EPS = 1e-6
WINDOWS = (2, 4, 8, 16)

ARENA_BYTES = 205 * 1024
STOP_AFTER_PHASE = 3


class Slot:
    __slots__ = ("w", "r")

    def __init__(self):
        self.w = None
        self.r = []


class Prog:
    ENGS = ("pe", "act", "dve", "pool", "sp")

    def __init__(self, nc):
        self.nc = nc
        self.ops = {e: [] for e in self.ENGS}
        self.sem = {}
        self.cnt = {e: 0 for e in self.ENGS}
        self.waited = {e: {} for e in self.ENGS}
        self.slots = {}
        self.dma_sems = []
        self.sem_objs = {}
        self.nsem = 0
        self.store_tokens = []
        self.pool_dmas = []
        self._cms = []
        for e in ("pe", "act", "dve", "pool"):
            self.sem[e] = self._new_sem("s_" + e)

    def _new_sem(self, name):
        cm = self.nc.semaphore(name)
        h = cm.__enter__()
        self._cms.append(cm)
        self.nsem += 1
        sid = self.nsem
        self.sem_objs[sid] = h
        return sid

    def close(self):
        for cm in reversed(self._cms):
            cm.__exit__(None, None, None)

    def slot(self, key):
        s = self.slots.get(key)
        if s is None:
            s = Slot()
            self.slots[key] = s
        return s

    def _deps(self, reads, writes):
        deps = []
        for k in reads:
            s = self.slot(k)
            if s.w is not None:
                deps.append(s.w)
        for k in writes:
            s = self.slot(k)
            if s.w is not None:
                deps.append(s.w)
            deps.extend(s.r)
        return deps

    def _waits(self, eng, deps):
        best = {}
        for (sid, val, src) in deps:
            if src == eng and eng == "pe":
                continue
            if val > best.get(sid, 0):
                best[sid] = val
        out = []
        wd = self.waited[eng]
        for sid, val in best.items():
            if wd.get(sid, 0) >= val:
                continue
            wd[sid] = val
            out.append((sid, val))
        return out

    def _commit(self, tok, reads, writes):
        for k in reads:
            self.slot(k).r.append(tok)
        for k in writes:
            s = self.slot(k)
            s.w = tok
            s.r = []

    def op(self, eng, fns, reads=(), writes=()):
        if not isinstance(fns, (list, tuple)):
            fns = [fns]
        waits = self._waits(eng, self._deps(reads, writes))
        self.cnt[eng] += 1
        tok = (self.sem[eng], self.cnt[eng], eng)
        self.ops[eng].append((waits, list(fns), (self.sem[eng], 1)))
        self._commit(tok, reads, writes)
        return tok

    def dma(self, queue, fn, reads=(), writes=(), store=False, after=()):
        deps = self._deps(reads, writes) + list(after)
        if queue == "pool":
            if len(self.pool_dmas) >= 4:
                deps.append(self.pool_dmas[-4])
        waits = self._waits(queue, deps)
        sid = self._new_sem("d%d" % self.nsem)
        tok = (sid, 16, "dma")
        self.ops[queue].append((waits, [fn], (sid, 16)))
        self._commit(tok, reads, writes)
        if queue == "pool":
            self.pool_dmas.append(tok)
        if store:
            self.store_tokens.append(tok)
        return tok

    def barrier(self):
        toks = [(self.sem[e], self.cnt[e], e) for e in ("pe", "act", "dve", "pool") if self.cnt[e] > 0]
        toks += self.store_tokens
        for e in self.ENGS:
            waits = self._waits(e, [t for t in toks if not (t[2] == e)])
            if waits:
                self.ops[e].append((waits, [], None))

    def final_wait(self):
        waits = self._waits("sp", self.store_tokens)
        if waits:
            self.ops["sp"].append((waits, [], None))

    def emit(self, block):
        nc = self.nc
        so = self.sem_objs

        def run(engname):
            def body(e):
                for waits, fns, inc in self.ops[engname]:
                    for sid, val in waits:
                        e.wait_ge(so[sid], val)
                    last = None
                    for f in fns:
                        last = f(e)
                    if inc is not None and last is not None:
                        last.then_inc(so[inc[0]], inc[1])
            return body

        block.sync(run("sp"))
        block.gpsimd(run("pool"))
        block.scalar(run("act"))
        block.vector(run("dve"))
        block.tensor(run("pe"))


class Bump:
    def __init__(self, base, limit):
        self.p = base
        self.limit = limit

    def take(self, nbytes):
        nbytes = (nbytes + 63) // 64 * 64
        o = self.p
        self.p += nbytes
        assert self.p <= self.limit, ("SBUF arena overflow", self.p, self.limit)
        return o


def build_program():
    nc = bass.Bass("TRN2", target_bir_lowering=False)

    def din(name, shape):
        return nc.dram_tensor(name, list(shape), F32, kind="ExternalInput").ap()

    def dout(name, shape):
        return nc.dram_tensor(name, list(shape), F32, kind="ExternalOutput").ap()

    xp = din("xp", (NPT, D)); xh = din("xh", (NHALO, D)); xs = din("xs", (NSAMP, D))
    pp = din("pp", (NPT, 256)); psm = din("psm", (NSAMP, 256))
    sconv = din("sconv", (2, 512)); spool = din("spool", (15, 512))
    invcnt = din("invcnt", (128, 64))
    w_in = din("w_in", (D, 2048)); w_out = din("w_out", (D, D)); w_up = din("w_up", (D, DFF))
    w_down = din("w_down", (DFF, D)); w_ple = din("w_ple", (256, D)); w_gate = din("w_gate", (D, D))
    w_pool = din("w_pool", (4, 128, 128))
    g_mix = din("g_mix", (D,)); g_mlp = din("g_mlp", (D,)); g_ple = din("g_ple", (D,)); g_final = din("g_final", (D,))
    w_conv = din("w_conv", (3, 512)); pool_scale = din("pool_scale", (512,))
    y_p = dout("y_p", (NPT, D)); y_s = dout("y_s", (NSAMP, D))
    ncp = dout("ncp", (2, 512)); npp = dout("npp", (15, 512)); ncs = dout("ncs", (2, 512)); nps = dout("nps", (15, 512))

    arena_cm = nc.sbuf_tensor("arena", [128, ARENA_BYTES // 4], F32)
    arena = arena_cm.__enter__()
    ps_cms = [nc.psum_tensor("ps%d" % i, [128, 512], F32) for i in range(8)]
    PS = [cm.__enter__() for cm in ps_cms]
    pg = Prog(nc)

    def vf32(off, n):
        return arena[:, off // 4: off // 4 + n]

    def vbf(off, n):
        return arena[:, off // 4: off // 4 + n // 2].bitcast(BF16)

    def ps_bf(bank):
        return PS[bank][:, 0:512].bitcast(BF16).rearrange("p (k n) -> p k n", k=2)

    pers = Bump(0, ARENA_BYTES)
    o = pers.take(17 * 4096); xr = vf32(o, 17 * 1024).rearrange("p (s d) -> p s d", s=17)
    o = pers.take(256); identb = vbf(o, 128)
    o = pers.take(512); identf = vf32(o, 128)
    o = pers.take(64); ccol = vf32(o, 16)
    o = pers.take(64); epsc = vf32(o, 1)
    o = pers.take(64); mhalf = vf32(o, 8)
    o = pers.take(256); invc = vf32(o, 64).rearrange("p (g n) -> p g n", g=4)
    NSTAT = 8
    o = pers.take(NSTAT * 96); stat = vf32(o, NSTAT * 24)
    o = pers.take(4096); gbc_a = vf32(o, 1024)
    o = pers.take(8192); up_b0 = vbf(o, 4096).rearrange("p (k n) -> p k n", k=8)
    o = pers.take(8192); dn_b0 = vbf(o, 4096).rearrange("p (k n) -> p k n", k=4)
    R0 = pers.p

    TILES = [("halo", NHALO, 1, NHALO, 15)] + [("p%d" % t, TS, 4, 128, 4 * t) for t in range(4)] + [("smp", NSAMP, 1, NSAMP, 16)]
    MAIN = TILES[1:]

    def xkey(slot):
        return ("x", slot)

    statflip = [0]

    def nstage_a_steps(tile, gbc, gkey, junk, hb, hbk="hb"):
        name, ntok, nsub, rows, xs0 = tile
        k = statflip[0]; statflip[0] = (statflip[0] + 1) % NSTAT
        ssq = stat[:, k * 24: k * 24 + 8]; std = stat[:, k * 24 + 8: k * 24 + 16]; rstd = stat[:, k * 24 + 16: k * 24 + 24]
        skey = ("stat", k)
        steps = []
        for s in range(nsub):
            steps.append(lambda s=s: pg.op("act", lambda e: e.activation(out=junk[:rows, :], in_=xr[:rows, xs0 + s, :], func=AF.Square,
                                                                        accum_out=ssq[:rows, s:s + 1]),
                                           reads=[xkey(xs0 + s)], writes=[("junk",), skey]))

        def powstep():
            pg.op("pool", lambda e: e.tensor_scalar(out=std[:rows, 0:nsub], in0=ssq[:rows, 0:nsub], scalar1=1.0 / D, scalar2=EPS,
                                                    op0=ALU.mult, op1=ALU.add), reads=[skey], writes=[skey])
            pg.op("pool", lambda e: e.tensor_tensor(out=rstd[:rows, 0:nsub], in0=std[:rows, 0:nsub], in1=mhalf[:rows, 0:nsub],
                                                    op=ALU.pow), reads=[skey, ("const",)], writes=[skey])
        steps.append(powstep)
        for s in range(nsub):
            steps.append(lambda s=s: pg.op("dve", lambda e: e.scalar_tensor_tensor(out=hb[:rows, s, :], in0=xr[:rows, xs0 + s, :],
                                                                                  scalar=rstd[:rows, s:s + 1], in1=gbc[:rows, :],
                                                                                  op0=ALU.mult, op1=ALU.mult),
                                           reads=[xkey(xs0 + s), skey, gkey], writes=[(hbk, s)]))
        return steps

    def nstage_a(tile, gbc, gkey, junk, hb, hbk="hb"):
        for st in nstage_a_steps(tile, gbc, gkey, junk, hb, hbk):
            st()

    def nstage_b_steps(tile, hb, hT, hTkey, tbanks, hbk="hb"):
        name, ntok, nsub, rows, xs0 = tile
        steps = []
        for cg in range(4):
            def step(cg=cg):
                bank = tbanks[cg % 2]
                tv = ps_bf(bank)
                fns = []
                for s in range(nsub):
                    for c in (2 * cg, 2 * cg + 1):
                        fns.append(lambda e, s=s, c=c, tv=tv: e.transpose(out=tv[:, c % 2, s * 128: s * 128 + rows],
                                                                          in_=hb[:rows, s, c * 128:(c + 1) * 128],
                                                                          identity=identb[:rows, :rows]))
                pg.op("pe", fns, reads=[(hbk, s) for s in range(nsub)] + [("const",)], writes=[("ps", bank)])
                pg.op("act", lambda e, tv=tv: e.activation(out=hT[:, 2 * cg:2 * cg + 2, 0:ntok], in_=tv[:, :, 0:ntok], func=AF.Copy),
                      reads=[("ps", bank)], writes=[hTkey])
            steps.append(step)
        return steps

    def nstage_b(tile, hb, hT, hTkey, tbanks, hbk="hb", split_evac=False):
        for st in nstage_b_steps(tile, hb, hT, hTkey, tbanks, hbk):
            st()

    def nstage(tile, gbc, gkey, junk, hb, hT, hTkey, tbanks):
        nstage_a(tile, gbc, gkey, junk, hb)
        nstage_b(tile, hb, hT, hTkey, tbanks)

    def setup_consts(crow):
        pg.op("pool", lambda e: e.memset(epsc, EPS), writes=[("const",)])
        pg.op("pool", lambda e: e.memset(mhalf, -0.5), writes=[("const",)])
        pg.op("pool", lambda e: e.iota(identf, pattern=[[1, 128]], base=0, channel_multiplier=-1,
                                       allow_small_or_imprecise_dtypes=True), writes=[("identf",)])
        pg.op("dve", lambda e: e.tensor_scalar(out=identf, in0=identf, scalar1=0.0, scalar2=None, op0=ALU.is_equal),
              reads=[("identf",)], writes=[("identf",)])
        pg.op("dve", lambda e: e.tensor_copy(out=identb, in_=identf), reads=[("identf",)], writes=[("const",)])
        pg.dma("sp", lambda e: e.dma_start(out=invc, in_=invcnt.rearrange("p (g n) -> p g n", g=4)), writes=[("invc",)])
        pg.dma("sp", lambda e: e.dma_start(out=crow[0:12, :], in_=w_conv.rearrange("k (c p) -> (k c) p", p=128)),
               writes=[("crow",)])
        pg.dma("sp", lambda e: e.dma_start(out=crow[12:16, :], in_=pool_scale.rearrange("(c p) -> c p", p=128)),
               writes=[("crow",)])
        pg.op("pe", lambda e: e.transpose(out=PS[7][:, 0:16], in_=crow[0:16, :], identity=identf[0:16, 0:16]),
              reads=[("crow",), ("identf",)], writes=[("ps", 7)])
        pg.op("dve", lambda e: e.tensor_copy(out=ccol, in_=PS[7][:, 0:16]), reads=[("ps", 7)], writes=[("ccol",)])

    r2 = Bump(R0, ARENA_BYTES)
    o = r2.take(4096); wple = vbf(o, 2048).rearrange("p (k n) -> p k n", k=2)
    o = r2.take(16384); wgate = vbf(o, 8192).rearrange("p (k n) -> p k n", k=8)
    W3END = r2.p
    o = r2.take(8192); up_b1 = vbf(o, 4096).rearrange("p (k n) -> p k n", k=8)
    o = r2.take(8192); dn_b1 = vbf(o, 4096).rearrange("p (k n) -> p k n", k=4)
    o = r2.take(8192); fT = vbf(o, 4096).rearrange("p (w k n) -> p w k n", w=2, k=4)
    o = r2.take(4096); rtmp = vf32(o, 1024).rearrange("p (s n) -> p s n", s=2)
    o = r2.take(1024)
    o = r2.take(2048); junk2 = vbf(o, 1024); JUNK2_OFF = o
    o = r2.take(16384); hb2 = vbf(o, 8192).rearrange("p (w s d) -> p w s d", w=2, s=4); HB2_OFF = o
    o = r2.take(8 * 2080 * 2); h2T = vbf(o, 8 * 2080).rearrange("p (k n) -> p k n", k=8)
    R2END = r2.p

    r3 = Bump(W3END, ARENA_BYTES)
    o = r3.take(4096); gbc_b = vf32(o, 1024)
    o = r3.take(16384); hT3 = vbf(o, 8192).rearrange("p (w k n) -> p w k n", w=2, k=8)
    o = r3.take(4096); pb = vbf(o, 2048).rearrange("p (w s n) -> p w s n", w=2, s=4)
    o = r3.take(2048); pT = vbf(o, 1024).rearrange("p (k n) -> p k n", k=2)
    assert r3.p <= JUNK2_OFF
    r3.p = JUNK2_OFF
    o = r3.take(2048); junk3 = vbf(o, 1024)
    o = r3.take(16384); hb3 = vbf(o, 8192).rearrange("p (w s d) -> p w s d", w=2, s=4); assert o == HB2_OFF
    o = r3.take(6144); sg = vf32(o, 1536).rearrange("p (s n) -> p s n", s=3)
    o = r3.take(6144); ptmp = vf32(o, 1536).rearrange("p (s n) -> p s n", s=3)
    o = r3.take(16384); ytile = vf32(o, 4096).rearrange("p (s n) -> p s n", s=4)

    r1 = Bump(R0, ARENA_BYTES)
    o = r1.take(32768); win = vbf(o, 16384).rearrange("p (k n) -> p k n", k=8)
    o = r1.take(16384); wout = vbf(o, 8192).rearrange("p (k n) -> p k n", k=8)
    o = r1.take(1024); wpl = vbf(o, 512).rearrange("p (g n) -> p g n", g=4)
    o = r1.take(2048); junk1 = vbf(o, 1024); assert o == JUNK2_OFF
    o = r1.take(8192); hb1 = vbf(o, 4096).rearrange("p (s d) -> p s d", s=4); assert o == HB2_OFF
    o = r1.take(8192); hT1 = vbf(o, 4096).rearrange("p (k n) -> p k n", k=8)
    o = r1.take(4096); vS = vf32(o, 1024).rearrange("p (s n) -> p s n", s=2)
    o = r1.take(4096); acc = vf32(o, 1024).rearrange("p (s n) -> p s n", s=2)
    o = r1.take(4 * 514 * 4); hcv = vf32(o, 4 * 514).rearrange("p (c n) -> p c n", c=4)
    o = r1.take(4 * 527 * 4); U = vf32(o, 4 * 527).rearrange("p (c n) -> p c n", c=4)
    o = r1.take(2 * 528 * 4); Stmp = vf32(o, 2 * 528).rearrange("p (s n) -> p s n", s=2)
    o = r1.take(4096); dT = vbf(o, 2048).rearrange("p (s n) -> p s n", s=4)
    o = r1.take(8192); mix = vbf(o, 4096).rearrange("p (k n) -> p k n", k=8)
    o = r1.take(64); d16 = vf32(o, 16)
    o = r1.take(256); hT_halo = vbf(o, 128).rearrange("p (k n) -> p k n", k=8)
    o = r1.take(4 * 17 * 4); so_col = vf32(o, 68).rearrange("p (c n) -> p c n", c=4)
    o = r1.take(2048); so_row = vf32(o, 512)
    o = r1.take(2048); srow = vf32(o, 512)
    o = r1.take(512); crow = vf32(o, 128)

    setup_consts(crow)
    pg.dma("sp", lambda e: e.dma_start(out=xr[0:NHALO, 15, :], in_=xh[:, :]), writes=[xkey(15)])
    pg.dma("sp", lambda e: e.dma_start(out=gbc_a, in_=g_mix.partition_broadcast(128)), writes=[("gbc_a",)])
    pg.dma("pool", lambda e: e.dma_start(out=wpl, in_=w_pool.rearrange("g c d -> c g d")), writes=[("wpl",)])
    for kk in range(4):
        pg.dma("pool", lambda e, kk=kk: e.dma_start(out=win[:, 2 * kk:2 * kk + 2, :],
                                                    in_=w_in[256 * kk:256 * (kk + 1), :].rearrange("(k p) n -> p k n", p=128)),
               writes=[("win", kk)])
    for s_ in range(4):
        pg.dma("sp", lambda e, s_=s_: e.dma_start(out=xr[:, s_, :], in_=xp[128 * s_:128 * (s_ + 1), :]), writes=[xkey(s_)])
    pg.op("pool", lambda e: e.memset(srow[0:32, :], 0.0), writes=[("srow",)])
    pg.dma("sp", lambda e: e.dma_start(out=srow[0:2, :], in_=sconv[:, :]), writes=[("srow",)])
    pg.dma("sp", lambda e: e.dma_start(out=srow[2:17, :], in_=spool[:, :]), writes=[("srow",)])

    WIN_KEYS = [("win", kk) for kk in range(4)]
    WOUT_KEYS = [("wout", kk) for kk in range(2)]
    ZB = [2, 3, 4, 0, 1]
    zrot = [0]

    def zbank():
        b = ZB[zrot[0] % len(ZB)]
        zrot[0] += 1
        return b

    def zmm(tile, col0, bank):
        ntok = tile[1]
        hT, hTkey = (hT_halo, ("hT_halo",)) if tile[0] == "halo" else (hT1, ("hT1",))
        fns = []
        for k in range(8):
            fns.append(lambda e, k=k: e.matmul(PS[bank][:, 0:ntok], lhsT=win[:, k, col0:col0 + 128], rhs=hT[:, k, 0:ntok],
                                               start=(k == 0), stop=(k == 7)))
        pg.op("pe", fns, reads=[("win", kk) for kk in range(4)] + [hTkey], writes=[("ps", bank)])

    def pool_sums(tile, c):
        ntok = tile[1]
        nlev = c + 1
        L = ntok + 15
        src = U[:, c, :]
        skey_src = ("U", c)
        for lev in range(nlev):
            sh = 1 << lev
            lo = (1 << (lev + 1)) - 1
            if lev == nlev - 1:
                lo = 15
            dst = Stmp[:, lev % 2, :]
            pg.op("pool", lambda e, dst=dst, src=src, lo=lo, sh=sh, L=L: e.tensor_tensor(
                out=dst[:, lo:L], in0=src[:, lo:L], in1=src[:, lo - sh:L - sh], op=ALU.add),
                reads=[skey_src], writes=[("Stmp", lev % 2)])
            src = dst
            skey_src = ("Stmp", lev % 2)
        return src[:, 15:15 + ntok], skey_src

    def zstage(tile, full=True, first_prompt=False):
        name, ntok, nsub, rows, xs0 = tile
        for c in range(4):
            b = zbank()
            zmm(tile, 1536 + 128 * c, b)
            pg.op("act", lambda e, c=c, b=b: e.activation(out=U[:, c, 15:15 + ntok], in_=PS[b][:, 0:ntok], func=AF.Copy),
                  reads=[("ps", b)], writes=[("U", c)])
            if not full:
                continue
            sums, sk = pool_sums(tile, c)
            oth = 1 - (c % 2)
            tmpd = Stmp[:, oth, 15:15 + ntok]
            invw = 1.0 / WINDOWS[c]
            pg.op("pool", lambda e, sums=sums, tmpd=tmpd, invw=invw: e.tensor_scalar(out=tmpd, in0=sums, scalar1=invw, scalar2=0.0,
                                                                                     op0=ALU.mult, op1=ALU.add),
                  reads=[sk], writes=[("Stmp", oth)])
            if first_prompt:
                pg.op("pool", lambda e, c=c, sums=sums, tmpd=tmpd: e.tensor_tensor(out=tmpd[:, 0:16], in0=sums[:, 0:16],
                                                                                   in1=invc[:, c, :], op=ALU.mult),
                      reads=[sk, ("invc",), ("Stmp", oth)], writes=[("Stmp", oth)])
            pg.op("pool", lambda e, c=c, tmpd=tmpd: e.tensor_tensor(out=dT[:, c, 0:ntok], in0=tmpd, in1=U[:, c, 15:15 + ntok],
                                                                    op=ALU.subtract),
                  reads=[("Stmp", oth), ("U", c)], writes=[("dT", c)])
        for c in range(4):
            slot = c % 2
            b = zbank()
            zmm(tile, 1024 + 128 * c, b)
            pg.op("act", lambda e, b=b, slot=slot: e.activation(out=vS[:, slot, 0:ntok], in_=PS[b][:, 0:ntok], func=AF.Copy),
                  reads=[("ps", b)], writes=[("vS", slot)])
            b = zbank()
            zmm(tile, 512 + 128 * c, b)
            pg.op("dve", lambda e, b=b, c=c, slot=slot: e.tensor_tensor(out=hcv[:, c, 2:2 + ntok], in0=PS[b][:, 0:ntok],
                                                                        in1=vS[:, slot, 0:ntok], op=ALU.mult),
                  reads=[("ps", b), ("vS", slot)], writes=[("hcv", c)])
            if not full:
                continue
            pg.op("dve", lambda e, c=c, slot=slot: e.tensor_scalar(out=acc[:, slot, 0:ntok], in0=hcv[:, c, 0:ntok],
                                                                   scalar1=ccol[:, c:c + 1], scalar2=None, op0=ALU.mult),
                  reads=[("hcv", c), ("ccol",)], writes=[("acc", slot)])
            for tap in (1, 2):
                pg.op("dve", lambda e, c=c, slot=slot, tap=tap: e.scalar_tensor_tensor(
                    out=acc[:, slot, 0:ntok], in0=hcv[:, c, tap:tap + ntok], scalar=ccol[:, 4 * tap + c:4 * tap + c + 1],
                    in1=acc[:, slot, 0:ntok], op0=ALU.mult, op1=ALU.add),
                    reads=[("hcv", c), ("ccol",), ("acc", slot)], writes=[("acc", slot)])
            if c == 3 and full:
                for cc in range(4):
                    b2 = zbank()
                    pg.op("pe", lambda e, cc=cc, b2=b2: e.matmul(PS[b2][:, 0:ntok], lhsT=wpl[:, cc, :], rhs=dT[:, cc, 0:ntok],
                                                                 start=True, stop=True),
                          reads=[("wpl",), ("dT", cc)], writes=[("ps", b2)])
                    pg.op("act", lambda e, cc=cc, b2=b2: e.activation(out=mix[:, 4 + cc, 0:ntok], in_=PS[b2][:, 0:ntok], func=AF.Copy,
                                                                      scale=ccol[:, 12 + cc:13 + cc]),
                          reads=[("ps", b2), ("ccol",)], writes=[("mix", 4 + cc)])
            b = zbank()
            zmm(tile, 128 * c, b)
            pg.op("dve", lambda e, b=b, c=c, slot=slot: e.tensor_tensor(out=mix[:, c, 0:ntok], in0=PS[b][:, 0:ntok],
                                                                        in1=acc[:, slot, 0:ntok], op=ALU.mult),
                  reads=[("ps", b), ("acc", slot)], writes=[("mix", c)])

    def hist_roll(tile):
        ntok = tile[1]
        pg.op("dve", lambda e: e.tensor_copy(out=hcv[:, :, 0:2], in_=hcv[:, :, ntok:ntok + 2]),
              reads=[("hcv", c) for c in range(4)], writes=[("hcv", c) for c in range(4)])
        pg.op("pool", lambda e: e.tensor_copy(out=U[:, :, 0:15], in_=U[:, :, ntok:ntok + 15]),
              reads=[("U", c) for c in range(4)], writes=[("U", c) for c in range(4)])

    OB = [5, 6, 7, 2, 3, 4]
    orot = [0]

    def wout_stage(tile, pre_group=None, post_group=None, tail=()):
        name, ntok, nsub, rows, xs0 = tile
        pre_group = pre_group or {}
        post_group = post_group or {}
        gi = -1
        for s in range(nsub):
            for hf in range(2):
                gi += 1
                for st in pre_group.get(gi, ()):
                    st()
                b = OB[orot[0] % len(OB)]; orot[0] += 1
                fns = []
                korder = (4, 5, 6, 7, 0, 1, 2, 3)
                for ki, k in enumerate(korder):
                    fns.append(lambda e, k=k, ki=ki, s=s, hf=hf, b=b: e.matmul(PS[b][0:rows, :], lhsT=mix[:, k, s * 128:s * 128 + rows],
                                                                               rhs=wout[:, k, hf * 512:(hf + 1) * 512],
                                                                               start=(ki == 0), stop=(ki == 7)))
                pg.op("pe", fns, reads=[("mix", k) for k in range(8)] + WOUT_KEYS, writes=[("ps", b)])
                pg.op("dve", lambda e, s=s, hf=hf, b=b: e.tensor_tensor(out=xr[:rows, xs0 + s, hf * 512:(hf + 1) * 512],
                                                                        in0=xr[:rows, xs0 + s, hf * 512:(hf + 1) * 512],
                                                                        in1=PS[b][0:rows, :], op=ALU.add),
                      reads=[("ps", b), xkey(xs0 + s)], writes=[xkey(xs0 + s)])
                for st in post_group.get(gi, ()):
                    st()
        for st in tail:
            st()

    def state_out(tile, dconv, dpool):
        ntok = tile[1]
        pg.op("dve", lambda e: e.tensor_copy(out=so_col[:, :, 0:2], in_=hcv[:, :, ntok:ntok + 2]),
              reads=[("hcv", c) for c in range(4)], writes=[("so_col",)])
        pg.op("dve", lambda e: e.tensor_copy(out=so_col[:, :, 2:17], in_=U[:, :, ntok:ntok + 15]),
              reads=[("U", c) for c in range(4)], writes=[("so_col",)])
        fns = [lambda e, c=c: e.transpose(out=PS[7][0:17, c * 128:(c + 1) * 128], in_=so_col[:, c, :], identity=identf[:, :])
               for c in range(4)]
        pg.op("pe", fns, reads=[("so_col",), ("identf",)], writes=[("ps", 7)])
        pg.op("act", lambda e: e.activation(out=so_row[0:17, :], in_=PS[7][0:17, :], func=AF.Copy),
              reads=[("ps", 7)], writes=[("so_row",)])
        pg.dma("sp", lambda e: e.dma_start(out=dconv[:, :], in_=so_row[0:2, :]), reads=[("so_row",)], store=True)
        pg.dma("sp", lambda e: e.dma_start(out=dpool[:, :], in_=so_row[2:17, :]), reads=[("so_row",)], store=True)

    def state_in():
        fns = [lambda e, c=c: e.transpose(out=PS[7][:, c * 18:(c + 1) * 18], in_=srow[0:18, c * 128:(c + 1) * 128],
                                          identity=identf[0:18, 0:18]) for c in range(4)]
        pg.op("pe", fns, reads=[("srow",), ("identf",)], writes=[("ps", 7)])
        pv = PS[7][:, 0:72].rearrange("p (c n) -> p c n", c=4)
        pg.op("dve", lambda e: e.tensor_copy(out=hcv[:, :, 0:2], in_=pv[:, :, 0:2]),
              reads=[("ps", 7)] + [("hcv", c) for c in range(4)], writes=[("hcv", c) for c in range(4)])
        pg.op("dve", lambda e: e.tensor_copy(out=U[:, :, 0:15], in_=pv[:, :, 2:17]),
              reads=[("ps", 7)] + [("U", c) for c in range(4)], writes=[("U", c) for c in range(4)])

    def nst1_a(tile):
        nstage_a(tile, gbc_a, ("gbc_a",), junk1, hb1)

    def nst1_a_steps(tile):
        return nstage_a_steps(tile, gbc_a, ("gbc_a",), junk1, hb1)

    def nst1_b(tile):
        nstage_b(tile, hb1, hT1, ("hT1",), (0, 1))

    def nst3_a_steps(i, tile):
        return nstage_a_steps(tile, gbc_a, ("gbc_a",), junk3, hb3[:, i % 2, :, :], hbk="hb2_%d" % (i % 2))

    def nst3_b(i, tile):
        nstage_b(tile, hb3[:, i % 2, :, :], hT3[:, i % 2, :, :], ("hT3", i % 2), (0, 1), hbk="hb2_%d" % (i % 2))

    nsteps = {}

    def get_n(k):
        if k not in nsteps:
            st = nst3_a_steps(k, MAIN[k]); n = MAIN[k][2]
            nsteps[k] = (st[:n], st[n], st[n + 1:])
        return nsteps[k]

    halo = TILES[0]
    nst1_a(halo)
    nstage_b(halo, hb1, hT_halo, ("hT_halo",), (0, 1))
    nst1_a(MAIN[0])
    nst1_b(MAIN[0])
    for kk in range(2):
        pg.dma("pool", lambda e, kk=kk: e.dma_start(out=wout[:, 4 * kk:4 * kk + 4, :],
                                                    in_=w_out[512 * kk:512 * (kk + 1), :].rearrange("(k p) n -> p k n", p=128)),
               writes=[("wout", kk)])
    zstage(halo, full=False)
    halo_done = (pg.sem["pe"], pg.cnt["pe"], "pe")
    hist_roll(halo)
    for t in (1, 2):
        pg.dma("sp", lambda e, t=t: e.dma_start(out=xr[:, 4 * t:4 * t + 4, :],
                                                in_=xp[TS * t:TS * (t + 1), :].rearrange("(s p) d -> p s d", p=128)),
               writes=[xkey(4 * t + s) for s in range(4)], after=[halo_done])
    pg.dma("sp", lambda e: e.dma_start(out=xr[0:NSAMP, 16, :], in_=xs[:, :]), writes=[xkey(16)], after=[halo_done])
    pg.dma("sp", lambda e: e.dma_start(out=xr[:, 12:16, :], in_=xp[TS * 3:TS * 4, :].rearrange("(s p) d -> p s d", p=128)),
           writes=[xkey(12 + s) for s in range(4)])
    pg.dma("pool", lambda e: e.dma_start(out=up_b0, in_=w_up[:, 0:FB].rearrange("(k p) n -> p k n", p=128)), writes=[("up", 0)])
    pg.dma("pool", lambda e: e.dma_start(out=dn_b0, in_=w_down[0:FB, :].rearrange("(k p) n -> p k n", p=128)), writes=[("dn", 0)])

    st1 = {}

    def get1(k):
        if k not in st1:
            st = nst1_a_steps(MAIN[k]); n = MAIN[k][2]
            st1[k] = (st[:n], st[n], st[n + 1:])
        return st1[k]

    for i, tile in enumerate(MAIN):
        nxt = MAIN[i + 1] if i + 1 < len(MAIN) else None
        if nxt is not None and i >= 1:
            for st in get1(i + 1)[2]:
                st()
        if tile[0] == "smp":
            state_in()
        if tile[0] == "p3":
            pre0 = nstage_a_steps(MAIN[0], gbc_a, ("gbc_a",), junk1, hb1, hbk="hb")
            pre1 = nstage_a_steps(MAIN[1], gbc_a, ("gbc_a",), junk1, hb2[:, 1, :, :], hbk="hb2_1")
        zstage(tile, full=True, first_prompt=(i == 0))
        if tile[0] == "smp":
            for st in pre0[MAIN[0][2] + 1:]:
                st()
            pre2 = nstage_a_steps(MAIN[2], gbc_a, ("gbc_a",), junk1, hb2[:, 0, :, :], hbk="hb2_0")
            pre3 = nstage_a_steps(MAIN[3], gbc_a, ("gbc_a",), junk1, hb2[:, 1, :, :], hbk="hb2_1")
            for st in pre2[:MAIN[2][2] + 1] + pre3[:MAIN[3][2] + 1]:
                st()
        pre_g, post_g, tail = {}, {}, []
        ngrp = 2 * tile[2]
        if nxt is not None and i == 0:
            sq, pw, sc = get1(1)
            for q, st in enumerate(sq):
                pre_g.setdefault(min(ngrp - 1, q // 2), []).append(st)
            post_g.setdefault(min(ngrp - 1, 1), []).append(pw)
            for q, st in enumerate(sc):
                post_g.setdefault(min(ngrp - 1, 2 + q), []).append(st)
            tsteps = nstage_b_steps(nxt, hb1, hT1, ("hT1",), (0, 1))
            post_g.setdefault(min(ngrp - 1, 6), []).append(tsteps[0])
            post_g.setdefault(min(ngrp - 1, 7), []).append(tsteps[1])
            tail += tsteps[2:]
        elif nxt is not None:
            tsteps = nstage_b_steps(nxt, hb1, hT1, ("hT1",), (0, 1))
            for q, st in enumerate(tsteps):
                post_g.setdefault(min(ngrp - 1, 2 * q + 1), []).append(st)
        if i + 2 < len(MAIN):
            sq, pw, _ = get1(i + 2)
            for q, st in enumerate(sq):
                pre_g.setdefault(min(ngrp - 1, 2 * q), []).append(st)
            tail.append(pw)
        if tile[0] == "p3":
            n0, n1 = MAIN[0][2], MAIN[1][2]
            for q, st in enumerate(pre0[:n0] + pre1[:n1]):
                pre_g.setdefault(min(ngrp - 1, q), []).append(st)
            tail += [pre0[n0], pre1[n1]]
        wout_stage(tile, pre_g, post_g, tail)
        if nxt is not None and nxt[0] == "smp":
            pg.dma("sp", lambda e: e.dma_start(out=gbc_a, in_=g_mlp.partition_broadcast(128)), writes=[("gbc_a",)])
        if tile[0] == "p3":
            state_out(tile, ncp, npp)
        if tile[0] == "smp":
            state_out(tile, ncs, nps)
        else:
            hist_roll(tile)

    if STOP_AFTER_PHASE >= 2:
        pg.barrier()
        UPB = [up_b0, up_b1]; DNB = [dn_b0, dn_b1]

        tokoff = {}
        off = 0
        for tile in MAIN:
            tokoff[tile[0]] = off
            off += tile[1]

        def load_w2(j):
            pg.dma("pool", lambda e, j=j: e.dma_start(out=UPB[j % 2], in_=w_up[:, j * FB:(j + 1) * FB].rearrange("(k p) n -> p k n", p=128)),
                   writes=[("up", j % 2)])
            pg.dma("pool", lambda e, j=j: e.dma_start(out=DNB[j % 2], in_=w_down[j * FB:(j + 1) * FB, :].rearrange("(k p) n -> p k n", p=128)),
                   writes=[("dn", j % 2)])

        UB = [2, 3, 4]; DB = [5, 6, 7]
        UB4 = [2, 3, 4, 0]; DB4 = [5, 6, 7, 1]
        urot = [0]; drot = [0]

        def up_stage(j, tile, w, post=()):
            name, ntok, nsub, rows, xs0 = tile
            t0 = tokoff[name]
            post = list(post)
            for c in range(4):
                ub = UB if j == 0 else UB4
                b = ub[urot[0] % len(ub)]; urot[0] += 1
                fns = [lambda e, k=k, c=c, b=b: e.matmul(PS[b][:, 0:ntok], lhsT=UPB[j % 2][:, k, c * 128:(c + 1) * 128],
                                                         rhs=h2T[:, k, t0:t0 + ntok], start=(k == 0), stop=(k == 7))
                       for k in range(8)]
                pg.op("pe", fns, reads=[("up", j % 2), ("h2T", name)], writes=[("ps", b)])
                rs = c % 2
                pg.op("act", lambda e, b=b, rs=rs: e.activation(out=rtmp[:, rs, 0:ntok], in_=PS[b][:, 0:ntok], func=AF.Relu),
                      reads=[("ps", b)], writes=[("rtmp", rs)])
                pg.op("dve", lambda e, c=c, rs=rs: e.tensor_tensor(out=fT[:, w % 2, c, 0:ntok], in0=rtmp[:, rs, 0:ntok],
                                                                   in1=rtmp[:, rs, 0:ntok], op=ALU.mult),
                      reads=[("rtmp", rs)], writes=[("fT", w % 2, c)])
                if c < len(post):
                    post[c]()

        def down_stage(j, tile, w):
            name, ntok, nsub, rows, xs0 = tile
            for s in range(nsub):
                for hf in range(2):
                    db = DB if j == 0 else DB4
                    b = db[drot[0] % len(db)]; drot[0] += 1
                    fns = [lambda e, k=k, s=s, hf=hf, b=b: e.matmul(PS[b][0:rows, :], lhsT=fT[:, w % 2, k, s * 128:s * 128 + rows],
                                                                    rhs=DNB[j % 2][:, k, hf * 512:(hf + 1) * 512],
                                                                    start=(k == 0), stop=(k == 3))
                           for k in range(4)]
                    pg.op("pe", fns, reads=[("fT", w % 2, k) for k in range(4)] + [("dn", j % 2)], writes=[("ps", b)])
                    pg.op("dve", lambda e, s=s, hf=hf, b=b: e.tensor_tensor(out=xr[:rows, xs0 + s, hf * 512:(hf + 1) * 512],
                                                                            in0=xr[:rows, xs0 + s, hf * 512:(hf + 1) * 512],
                                                                            in1=PS[b][0:rows, :], op=ALU.add),
                          reads=[("ps", b), xkey(xs0 + s)], writes=[xkey(xs0 + s)])

        pg.dma("pool", lambda e: e.dma_start(out=wple, in_=w_ple.rearrange("(k p) n -> p k n", p=128)), writes=[("wple",)])

        def n2a(ti):
            nstage_a(MAIN[ti], gbc_a, ("gbc_a",), junk2, hb2[:, ti % 2, :, :], hbk="hb2_%d" % (ti % 2))

        def n2b(ti):
            tile = MAIN[ti]
            t0 = tokoff[tile[0]]
            nstage_b(tile, hb2[:, ti % 2, :, :], h2T[:, :, t0:t0 + tile[1]], ("h2T", tile[0]), (0, 1), hbk="hb2_%d" % (ti % 2))

        prev = None
        w = 0
        for j in range(NB):
            if j == 2:
                for kk in range(2):
                    pg.dma("pool", lambda e, kk=kk: e.dma_start(out=wgate[:, 4 * kk:4 * kk + 4, :],
                                                                in_=w_gate[512 * kk:512 * (kk + 1), :].rearrange("(k p) n -> p k n", p=128)),
                           writes=[("wgate", kk)])
            for ti, tile in enumerate(MAIN):
                if j == 0 and ti == 0:
                    for st in pre1[MAIN[1][2] + 1:]:
                        st()
                    n2b(0)
                if j == 0 and ti + 1 < len(MAIN):
                    nt_ = MAIN[ti + 1]
                    t0_ = tokoff[nt_[0]]
                    k_ = (ti + 1) % 2
                    tsteps = nstage_b_steps(nt_, hb2[:, k_, :, :], h2T[:, :, t0_:t0_ + nt_[1]], ("h2T", nt_[0]), (0, 1),
                                            hbk="hb2_%d" % k_)
                    if ti + 2 == 2:
                        ssteps = pre2[MAIN[2][2] + 1:]
                    elif ti + 2 == 3:
                        ssteps = pre3[MAIN[3][2] + 1:]
                    elif ti + 2 < len(MAIN):
                        ssteps = [lambda: n2a(ti + 2)]
                    else:
                        ssteps = []
                    post_ = []
                    for c_ in range(4):
                        def both(c_=c_, tsteps=tsteps, ssteps=ssteps):
                            tsteps[c_]()
                            if c_ < len(ssteps):
                                ssteps[c_]()
                        post_.append(both)
                    up_stage(j, tile, w, post=post_)
                else:
                    up_stage(j, tile, w)
                if prev is not None:
                    down_stage(*prev)
                    if prev[0] == NB - 1 and prev[1] in (MAIN[0], MAIN[1]):
                        kk = 0 if prev[1] is MAIN[0] else 1
                        sq, pw, sc = get_n(kk)
                        for st in sq:
                            st()
                        pw()
                        for st in sc:
                            st()
                if j == 1 and ti == 0:
                    pg.dma("sp", lambda e: e.dma_start(out=gbc_a, in_=g_ple.partition_broadcast(128)), writes=[("gbc_a",)])

                if tile is MAIN[0] and j + 1 < NB:
                    load_w2(j + 1)
                prev = (j, tile, w)
                w += 1
        down_stage(*prev)

    if STOP_AFTER_PHASE >= 3:
        pg.barrier()
        pg.dma("sp", lambda e: e.dma_start(out=gbc_b, in_=g_final.partition_broadcast(128)), writes=[("gbc_b",)])

        def load_p(i, tile):
            name, ntok, nsub, rows, xs0 = tile
            if name == "smp":
                pg.dma("pool", lambda e: e.dma_start(out=pb[0:NSAMP, i % 2, 0, :], in_=psm[:, :]), writes=[("pb", i % 2)])
            else:
                t = int(name[1:])
                pg.dma("pool", lambda e, t=t: e.dma_start(out=pb[:, i % 2, :, :],
                                                          in_=pp[TS * t:TS * (t + 1), :].rearrange("(s p) n -> p s n", p=128)),
                       writes=[("pb", i % 2)])


        yrot = [0]; srot = [0]

        def ple_pre(i, tile):
            name, ntok, nsub, rows, xs0 = tile
            tv = ps_bf(0)
            fns = []
            for s in range(nsub):
                for k in range(2):
                    fns.append(lambda e, s=s, k=k: e.transpose(out=tv[:, k, s * 128:s * 128 + rows],
                                                               in_=pb[:rows, i % 2, s, k * 128:(k + 1) * 128],
                                                               identity=identb[:rows, :rows]))
            pg.op("pe", fns, reads=[("pb", i % 2), ("const",)], writes=[("ps", 0)])
            pg.op("act", lambda e: e.activation(out=pT[:, :, 0:ntok], in_=tv[:, :, 0:ntok], func=AF.Copy),
                  reads=[("ps", 0)], writes=[("pT",)])

        def ple_main(i, tile, bg_act, bg_dve, post_group=None):
            name, ntok, nsub, rows, xs0 = tile
            post_group = post_group or {}
            ngrp = 2 * nsub
            pa = -(-len(bg_act) // ngrp) if bg_act else 0
            pd = -(-len(bg_dve) // ngrp) if bg_dve else 0
            gi = 0
            for s in range(nsub):
                for hf in range(2):
                    for st in bg_act[gi * pa:(gi + 1) * pa]:
                        st()
                    for st in bg_dve[gi * pd:(gi + 1) * pd]:
                        st()
                    gi += 1
                    q = srot[0] % 3; srot[0] += 1
                    bp = (2, 3, 7)[q]; bg = (4, 5, 6)[q]
                    fns = [lambda e, k=k, s=s, hf=hf, bp=bp: e.matmul(PS[bp][0:rows, :], lhsT=pT[:, k, s * 128:s * 128 + rows],
                                                                      rhs=wple[:, k, hf * 512:(hf + 1) * 512],
                                                                      start=(k == 0), stop=(k == 1)) for k in range(2)]
                    pg.op("pe", fns, reads=[("pT",), ("wple",)], writes=[("ps", bp)])
                    fns = [lambda e, k=k, s=s, hf=hf, bg=bg: e.matmul(PS[bg][0:rows, :], lhsT=hT3[:, i % 2, k, s * 128:s * 128 + rows],
                                                                      rhs=wgate[:, k, hf * 512:(hf + 1) * 512],
                                                                      start=(k == 0), stop=(k == 7)) for k in range(8)]
                    pg.op("pe", fns, reads=[("hT3", i % 2), ("wgate", 0), ("wgate", 1)], writes=[("ps", bg)])
                    pg.op("act", lambda e, q=q, bg=bg: e.activation(out=sg[:rows, q, :], in_=PS[bg][0:rows, :], func=AF.Sigmoid),
                          reads=[("ps", bg)], writes=[("sg", q)])
                    pg.op("dve", lambda e, q=q, bp=bp: e.tensor_tensor(out=ptmp[:rows, q, :], in0=PS[bp][0:rows, :],
                                                                       in1=sg[:rows, q, :], op=ALU.mult),
                          reads=[("ps", bp), ("sg", q)], writes=[("ptmp", q)])
                    pg.op("pool", lambda e, s=s, hf=hf, q=q: e.tensor_tensor(out=xr[:rows, xs0 + s, hf * 512:(hf + 1) * 512],
                                                                             in0=xr[:rows, xs0 + s, hf * 512:(hf + 1) * 512],
                                                                             in1=ptmp[:rows, q, :], op=ALU.add),
                          reads=[("ptmp", q), xkey(xs0 + s)], writes=[xkey(xs0 + s)])
                    for st in post_group.get(gi - 1, ()):
                        st()

        def final_steps(i, tile):
            name, ntok, nsub, rows, xs0 = tile
            k = statflip[0]; statflip[0] = (statflip[0] + 1) % NSTAT
            ssq = stat[:, k * 24: k * 24 + 8]; std = stat[:, k * 24 + 8: k * 24 + 16]; rstd = stat[:, k * 24 + 16: k * 24 + 24]
            skey = ("stat", k)
            steps = []
            for s in range(nsub):
                steps.append(lambda s=s: pg.op("act", lambda e: e.activation(out=junk3[:rows, :], in_=xr[:rows, xs0 + s, :], func=AF.Square,
                                                                            accum_out=ssq[:rows, s:s + 1]),
                                               reads=[xkey(xs0 + s)], writes=[("junk",), skey]))

            def powstep():
                pg.op("pool", lambda e: e.tensor_scalar(out=std[:rows, 0:nsub], in0=ssq[:rows, 0:nsub], scalar1=1.0 / D, scalar2=EPS,
                                                        op0=ALU.mult, op1=ALU.add), reads=[skey], writes=[skey])
                pg.op("pool", lambda e: e.tensor_tensor(out=rstd[:rows, 0:nsub], in0=std[:rows, 0:nsub], in1=mhalf[:rows, 0:nsub],
                                                        op=ALU.pow), reads=[skey, ("const",)], writes=[skey])
            steps.append(powstep)

            def ystep(s):
                ys = yrot[0] % 4; yrot[0] += 1
                pg.op("dve", lambda e: e.scalar_tensor_tensor(out=ytile[:rows, ys, :], in0=xr[:rows, xs0 + s, :],
                                                              scalar=rstd[:rows, s:s + 1], in1=gbc_b[:rows, :],
                                                              op0=ALU.mult, op1=ALU.mult),
                      reads=[xkey(xs0 + s), skey, ("gbc_b",)], writes=[("ytile", ys)])
                if name == "smp":
                    pg.dma("sp", lambda e: e.dma_start(out=y_s[:, :], in_=ytile[0:NSAMP, ys, :]), reads=[("ytile", ys)], store=True)
                else:
                    r0 = (xs0 + s) * 128
                    pg.dma("sp", lambda e: e.dma_start(out=y_p[r0:r0 + 128, :], in_=ytile[:, ys, :]), reads=[("ytile", ys)], store=True)
            for s in range(nsub):
                steps.append(lambda s=s: ystep(s))
            return steps

        def merge(a, b):
            out = []
            for j in range(max(len(a), len(b))):
                if j < len(a):
                    out.append(a[j])
                if j < len(b):
                    out.append(b[j])
            return out

        nT = len(MAIN)
        NSUB = [t[2] for t in MAIN]
        fsteps = {}

        def get_f(k):
            if k not in fsteps:
                st = final_steps(k, MAIN[k]); n = NSUB[k]
                fsteps[k] = (st[:n], st[n], st[n + 1:])
            return fsteps[k]

        def run(steps):
            for st in steps:
                st()

        load_p(0, MAIN[0])
        nst3_b(0, MAIN[0]); ple_pre(0, MAIN[0])
        for i, tile in enumerate(MAIN):
            if i + 1 < nT:
                load_p(i + 1, MAIN[i + 1])
            bg_act, bg_dve, pows = [], [], []
            post_g = {}
            ngrp = 2 * tile[2]
            nF = 0
            if i >= 1:
                sq, pwF, yF = get_f(i - 1); bg_act += sq; nF = len(sq)
            if i + 2 < nT:
                sq, pw, _ = get_n(i + 2); bg_act += sq; pows.append(pw)
            if i >= 1 and i + 1 < nT:
                bg_dve += get_n(i + 1)[2]
            tail = []
            if i >= 1 and i == nT - 1:
                bg_act.append(pwF)
                tail += list(yF)
            elif i >= 1:
                pa = -(-len(bg_act) // ngrp)
                gF = (nF - 1) // pa
                post_g.setdefault(gF, []).append(pwF)
                for q, st in enumerate(yF):
                    if gF + 1 + q < ngrp:
                        post_g.setdefault(gF + 1 + q, []).append(st)
                    else:
                        tail.append(st)
            if i + 1 < nT:
                k1 = (i + 1) % 2
                tsteps = nstage_b_steps(MAIN[i + 1], hb3[:, k1, :, :], hT3[:, k1, :, :], ("hT3", k1), (0, 1), hbk="hb2_%d" % k1)
                for q, st in enumerate(tsteps):
                    post_g.setdefault(min(ngrp - 1, max(0, ngrp - 4) + q), []).append(st)
            ple_main(i, tile, bg_act, bg_dve, post_g)
            for pw in pows:
                pw()
            run(tail)
            if i + 1 < nT:
                ple_pre(i + 1, MAIN[i + 1])
        sq, pw, sc = get_f(nT - 1)
        run(sq); pw()
        run(sc)

    if STOP_AFTER_PHASE < 3:
        for t in range(4):
            pg.dma("sp", lambda e, t=t: e.dma_start(out=y_p[TS * t:TS * (t + 1), :].rearrange("(s p) d -> p s d", p=128),
                                                    in_=xr[:, 4 * t:4 * t + 4, :]),
                   reads=[xkey(4 * t + s) for s in range(4)], store=True)
        pg.dma("sp", lambda e: e.dma_start(out=y_s[:, :], in_=xr[0:NSAMP, 16, :]), reads=[xkey(16)], store=True)

    pg.final_wait()
    with nc.Block() as block:
        pg.emit(block)
    pg.close()
    for cm in reversed(ps_cms):
        cm.__exit__(None, None, None)
    arena_cm.__exit__(None, None, None)
    return nc


def _invcnt(core):
    out = np.empty((128, 4, 16), np.float32)
    for g, w in enumerate(WINDOWS):
        if core == 0:
            cnt = np.minimum(np.arange(16) + 1, w).astype(np.float32)
        else:
            cnt = np.full(16, float(w), np.float32)
        out[:, g, :] = (1.0 / cnt)[None, :]
    return out.reshape(128, 64)


def kernel(x_prompt, x_sample, state_conv, state_pool, p_prompt, p_sample, g_mix, w_in, w_conv,
           w_pool, pool_scale, w_out, g_mlp, w_up, w_down, g_ple, w_ple, w_ple_gate, g_final):
    f = lambda a: np.ascontiguousarray(np.asarray(a, dtype=np.float32))
    xpr = f(x_prompt)[0]; xsm = f(x_sample); ppr = f(p_prompt)[0, 0]; psmp = f(p_sample)[0]
    sc = f(state_conv)[0]; spl = f(state_pool)[0]
    shared = {
        "w_in": f(w_in)[0], "w_out": f(w_out)[0], "w_up": f(w_up)[0], "w_down": f(w_down)[0],
        "w_ple": f(w_ple)[0], "w_gate": f(w_ple_gate)[0], "w_pool": f(w_pool)[0],
        "g_mix": f(g_mix)[0], "g_mlp": f(g_mlp)[0], "g_ple": f(g_ple)[0], "g_final": f(g_final),
        "w_conv": f(w_conv)[0], "pool_scale": f(pool_scale)[0],
    }
    in_maps = []
    for c in range(NCORES):
        m = dict(shared)
        m["xp"] = xpr[c * NPT:(c + 1) * NPT]
        m["xh"] = xpr[c * NPT - NHALO:c * NPT] if c > 0 else np.zeros((NHALO, D), np.float32)
        m["xs"] = xsm[c]
        m["pp"] = ppr[c * NPT:(c + 1) * NPT]
        m["psm"] = psmp[c]
        m["sconv"] = sc[c]
        m["spool"] = spl[c]
        m["invcnt"] = _invcnt(c)
        in_maps.append(m)
    nc = build_program()
    res = run_bass_kernel_spmd(nc, in_maps, core_ids=list(range(NCORES)))
    rs = res.results
    y_prompt = np.concatenate([rs[c]["y_p"] for c in range(NCORES)], axis=0)[None]
    y_sample = np.stack([rs[c]["y_s"] for c in range(NCORES)], axis=0)
    ncp = rs[NCORES - 1]["ncp"][None, None]
    npp = rs[NCORES - 1]["npp"][None, None]
    ncs = np.stack([rs[c]["ncs"] for c in range(NCORES)], axis=0)[None]
    nps = np.stack([rs[c]["nps"] for c in range(NCORES)], axis=0)[None]
    return (y_prompt.astype(np.float32), y_sample.astype(np.float32), ncp.astype(np.float32),
            npp.astype(np.float32), ncs.astype(np.float32), nps.astype(np.float32))
```

```python
import numpy as np
import concourse.bass as bass
import concourse.mybir as mybir
from concourse.bass_utils import run_bass_kernel_spmd

F32 = mybir.dt.float32
BF16 = mybir.dt.bfloat16
AF = mybir.ActivationFunctionType
ALU = mybir.AluOpType

NCORES = 8
D = 1024
NPT = 2048
TS = 512
NSAMP = 32
NHALO = 16
DFF = 4096
NB = 8
FB = DFF // NB
EPS = 1e-6
WINDOWS = (2, 4, 8, 16)

ARENA_BYTES = 205 * 1024
STOP_AFTER_PHASE = 3


class Slot:
    __slots__ = ("w", "r")

    def __init__(self):
        self.w = None
        self.r = []


class Prog:
    ENGS = ("pe", "act", "dve", "pool", "sp")

    def __init__(self, nc):
        self.nc = nc
        self.ops = {e: [] for e in self.ENGS}
        self.sem = {}
        self.cnt = {e: 0 for e in self.ENGS}
        self.waited = {e: {} for e in self.ENGS}
        self.slots = {}
        self.dma_sems = []
        self.sem_objs = {}
        self.nsem = 0
        self.store_tokens = []
        self.pool_dmas = []
        self._cms = []
        for e in ("pe", "act", "dve", "pool"):
            self.sem[e] = self._new_sem("s_" + e)

    def _new_sem(self, name):
        cm = self.nc.semaphore(name)
        h = cm.__enter__()
        self._cms.append(cm)
        self.nsem += 1
        sid = self.nsem
        self.sem_objs[sid] = h
        return sid

    def close(self):
        for cm in reversed(self._cms):
            cm.__exit__(None, None, None)

    def slot(self, key):
        s = self.slots.get(key)
        if s is None:
            s = Slot()
            self.slots[key] = s
        return s

    def _deps(self, reads, writes):
        deps = []
        for k in reads:
            s = self.slot(k)
            if s.w is not None:
                deps.append(s.w)
        for k in writes:
            s = self.slot(k)
            if s.w is not None:
                deps.append(s.w)
            deps.extend(s.r)
        return deps

    def _waits(self, eng, deps):
        best = {}
        for (sid, val, src) in deps:
            if src == eng and eng == "pe":
                continue
            if val > best.get(sid, 0):
                best[sid] = val
        out = []
        wd = self.waited[eng]
        for sid, val in best.items():
            if wd.get(sid, 0) >= val:
                continue
            wd[sid] = val
            out.append((sid, val))
        return out

    def _commit(self, tok, reads, writes):
        for k in reads:
            self.slot(k).r.append(tok)
        for k in writes:
            s = self.slot(k)
            s.w = tok
            s.r = []

    def op(self, eng, fns, reads=(), writes=()):
        if not isinstance(fns, (list, tuple)):
            fns = [fns]
        waits = self._waits(eng, self._deps(reads, writes))
        self.cnt[eng] += 1
        tok = (self.sem[eng], self.cnt[eng], eng)
        self.ops[eng].append((waits, list(fns), (self.sem[eng], 1)))
        self._commit(tok, reads, writes)
        return tok

    def dma(self, queue, fn, reads=(), writes=(), store=False, after=()):
        deps = self._deps(reads, writes) + list(after)
        if queue == "pool":
            if len(self.pool_dmas) >= 4:
                deps.append(self.pool_dmas[-4])
        waits = self._waits(queue, deps)
        sid = self._new_sem("d%d" % self.nsem)
        tok = (sid, 16, "dma")
        self.ops[queue].append((waits, [fn], (sid, 16)))
        self._commit(tok, reads, writes)
        if queue == "pool":
            self.pool_dmas.append(tok)
        if store:
            self.store_tokens.append(tok)
        return tok

    def barrier(self):
        toks = [(self.sem[e], self.cnt[e], e) for e in ("pe", "act", "dve", "pool") if self.cnt[e] > 0]
        toks += self.store_tokens
        for e in self.ENGS:
            waits = self._waits(e, [t for t in toks if not (t[2] == e)])
            if waits:
                self.ops[e].append((waits, [], None))

    def final_wait(self):
        waits = self._waits("sp", self.store_tokens)
        if waits:
            self.ops["sp"].append((waits, [], None))

    def emit(self, block):
        nc = self.nc
        so = self.sem_objs

        def run(engname):
            def body(e):
                for waits, fns, inc in self.ops[engname]:
                    for sid, val in waits:
                        e.wait_ge(so[sid], val)
                    last = None
                    for f in fns:
                        last = f(e)
                    if inc is not None and last is not None:
                        last.then_inc(so[inc[0]], inc[1])
            return body

        block.sync(run("sp"))
        block.gpsimd(run("pool"))
        block.scalar(run("act"))
        block.vector(run("dve"))
        block.tensor(run("pe"))


class Bump:
    def __init__(self, base, limit):
        self.p = base
        self.limit = limit

    def take(self, nbytes):
        nbytes = (nbytes + 63) // 64 * 64
        o = self.p
        self.p += nbytes
        assert self.p <= self.limit, ("SBUF arena overflow", self.p, self.limit)
        return o


def build_program():
    nc = bass.Bass("TRN2", target_bir_lowering=False)

    def din(name, shape):
        return nc.dram_tensor(name, list(shape), F32, kind="ExternalInput").ap()

    def dout(name, shape):
        return nc.dram_tensor(name, list(shape), F32, kind="ExternalOutput").ap()

    xp = din("xp", (NPT, D)); xh = din("xh", (NHALO, D)); xs = din("xs", (NSAMP, D))
    pp = din("pp", (NPT, 256)); psm = din("psm", (NSAMP, 256))
    sconv = din("sconv", (2, 512)); spool = din("spool", (15, 512))
    invcnt = din("invcnt", (128, 64))
    w_in = din("w_in", (D, 2048)); w_out = din("w_out", (D, D)); w_up = din("w_up", (D, DFF))
    w_down = din("w_down", (DFF, D)); w_ple = din("w_ple", (256, D)); w_gate = din("w_gate", (D, D))
    w_pool = din("w_pool", (4, 128, 128))
    g_mix = din("g_mix", (D,)); g_mlp = din("g_mlp", (D,)); g_ple = din("g_ple", (D,)); g_final = din("g_final", (D,))
    w_conv = din("w_conv", (3, 512)); pool_scale = din("pool_scale", (512,))
    y_p = dout("y_p", (NPT, D)); y_s = dout("y_s", (NSAMP, D))
    ncp = dout("ncp", (2, 512)); npp = dout("npp", (15, 512)); ncs = dout("ncs", (2, 512)); nps = dout("nps", (15, 512))

    arena_cm = nc.sbuf_tensor("arena", [128, ARENA_BYTES // 4], F32)
    arena = arena_cm.__enter__()
    ps_cms = [nc.psum_tensor("ps%d" % i, [128, 512], F32) for i in range(8)]
    PS = [cm.__enter__() for cm in ps_cms]
    pg = Prog(nc)

    def vf32(off, n):
        return arena[:, off // 4: off // 4 + n]

    def vbf(off, n):
        return arena[:, off // 4: off // 4 + n // 2].bitcast(BF16)

    def ps_bf(bank):
        return PS[bank][:, 0:512].bitcast(BF16).rearrange("p (k n) -> p k n", k=2)

    pers = Bump(0, ARENA_BYTES)
    o = pers.take(17 * 4096); xr = vf32(o, 17 * 1024).rearrange("p (s d) -> p s d", s=17)
    o = pers.take(256); identb = vbf(o, 128)
    o = pers.take(512); identf = vf32(o, 128)
    o = pers.take(64); ccol = vf32(o, 16)
    o = pers.take(64); epsc = vf32(o, 1)
    o = pers.take(64); mhalf = vf32(o, 8)
    o = pers.take(256); invc = vf32(o, 64).rearrange("p (g n) -> p g n", g=4)
    NSTAT = 8
    o = pers.take(NSTAT * 96); stat = vf32(o, NSTAT * 24)
    o = pers.take(4096); gbc_a = vf32(o, 1024)
    o = pers.take(8192); up_b0 = vbf(o, 4096).rearrange("p (k n) -> p k n", k=8)
    o = pers.take(8192); dn_b0 = vbf(o, 4096).rearrange("p (k n) -> p k n", k=4)
    R0 = pers.p

    TILES = [("halo", NHALO, 1, NHALO, 15)] + [("p%d" % t, TS, 4, 128, 4 * t) for t in range(4)] + [("smp", NSAMP, 1, NSAMP, 16)]
    MAIN = TILES[1:]

    def xkey(slot):
        return ("x", slot)

    statflip = [0]

    def nstage_a_steps(tile, gbc, gkey, junk, hb, hbk="hb"):
        name, ntok, nsub, rows, xs0 = tile
        k = statflip[0]; statflip[0] = (statflip[0] + 1) % NSTAT
        ssq = stat[:, k * 24: k * 24 + 8]; std = stat[:, k * 24 + 8: k * 24 + 16]; rstd = stat[:, k * 24 + 16: k * 24 + 24]
        skey = ("stat", k)
        steps = []
        for s in range(nsub):
            steps.append(lambda s=s: pg.op("act", lambda e: e.activation(out=junk[:rows, :], in_=xr[:rows, xs0 + s, :], func=AF.Square,
                                                                        accum_out=ssq[:rows, s:s + 1]),
                                           reads=[xkey(xs0 + s)], writes=[("junk",), skey]))

        def powstep():
            pg.op("pool", lambda e: e.tensor_scalar(out=std[:rows, 0:nsub], in0=ssq[:rows, 0:nsub], scalar1=1.0 / D, scalar2=EPS,
                                                    op0=ALU.mult, op1=ALU.add), reads=[skey], writes=[skey])
            pg.op("pool", lambda e: e.tensor_tensor(out=rstd[:rows, 0:nsub], in0=std[:rows, 0:nsub], in1=mhalf[:rows, 0:nsub],
                                                    op=ALU.pow), reads=[skey, ("const",)], writes=[skey])
        steps.append(powstep)
        for s in range(nsub):
            steps.append(lambda s=s: pg.op("dve", lambda e: e.scalar_tensor_tensor(out=hb[:rows, s, :], in0=xr[:rows, xs0 + s, :],
                                                                                  scalar=rstd[:rows, s:s + 1], in1=gbc[:rows, :],
                                                                                  op0=ALU.mult, op1=ALU.mult),
                                           reads=[xkey(xs0 + s), skey, gkey], writes=[(hbk, s)]))
        return steps

    def nstage_a(tile, gbc, gkey, junk, hb, hbk="hb"):
        for st in nstage_a_steps(tile, gbc, gkey, junk, hb, hbk):
            st()

    def nstage_b_steps(tile, hb, hT, hTkey, tbanks, hbk="hb"):
        name, ntok, nsub, rows, xs0 = tile
        steps = []
        for cg in range(4):
            def step(cg=cg):
                bank = tbanks[cg % 2]
                tv = ps_bf(bank)
                fns = []
                for s in range(nsub):
                    for c in (2 * cg, 2 * cg + 1):
                        fns.append(lambda e, s=s, c=c, tv=tv: e.transpose(out=tv[:, c % 2, s * 128: s * 128 + rows],
                                                                          in_=hb[:rows, s, c * 128:(c + 1) * 128],
                                                                          identity=identb[:rows, :rows]))
                pg.op("pe", fns, reads=[(hbk, s) for s in range(nsub)] + [("const",)], writes=[("ps", bank)])
                pg.op("act", lambda e, tv=tv: e.activation(out=hT[:, 2 * cg:2 * cg + 2, 0:ntok], in_=tv[:, :, 0:ntok], func=AF.Copy),
                      reads=[("ps", bank)], writes=[hTkey])
            steps.append(step)
        return steps

    def nstage_b(tile, hb, hT, hTkey, tbanks, hbk="hb", split_evac=False):
        for st in nstage_b_steps(tile, hb, hT, hTkey, tbanks, hbk):
            st()

    def nstage(tile, gbc, gkey, junk, hb, hT, hTkey, tbanks):
        nstage_a(tile, gbc, gkey, junk, hb)
        nstage_b(tile, hb, hT, hTkey, tbanks)

    def setup_consts(crow):
        pg.op("pool", lambda e: e.memset(epsc, EPS), writes=[("const",)])
        pg.op("pool", lambda e: e.memset(mhalf, -0.5), writes=[("const",)])
        pg.op("pool", lambda e: e.iota(identf, pattern=[[1, 128]], base=0, channel_multiplier=-1,
                                       allow_small_or_imprecise_dtypes=True), writes=[("identf",)])
        pg.op("dve", lambda e: e.tensor_scalar(out=identf, in0=identf, scalar1=0.0, scalar2=None, op0=ALU.is_equal),
              reads=[("identf",)], writes=[("identf",)])
        pg.op("dve", lambda e: e.tensor_copy(out=identb, in_=identf), reads=[("identf",)], writes=[("const",)])
        pg.dma("sp", lambda e: e.dma_start(out=invc, in_=invcnt.rearrange("p (g n) -> p g n", g=4)), writes=[("invc",)])
        pg.dma("sp", lambda e: e.dma_start(out=crow[0:12, :], in_=w_conv.rearrange("k (c p) -> (k c) p", p=128)),
               writes=[("crow",)])
        pg.dma("sp", lambda e: e.dma_start(out=crow[12:16, :], in_=pool_scale.rearrange("(c p) -> c p", p=128)),
               writes=[("crow",)])
        pg.op("pe", lambda e: e.transpose(out=PS[7][:, 0:16], in_=crow[0:16, :], identity=identf[0:16, 0:16]),
              reads=[("crow",), ("identf",)], writes=[("ps", 7)])
        pg.op("dve", lambda e: e.tensor_copy(out=ccol, in_=PS[7][:, 0:16]), reads=[("ps", 7)], writes=[("ccol",)])

    r2 = Bump(R0, ARENA_BYTES)
    o = r2.take(4096); wple = vbf(o, 2048).rearrange("p (k n) -> p k n", k=2)
    o = r2.take(16384); wgate = vbf(o, 8192).rearrange("p (k n) -> p k n", k=8)
    W3END = r2.p
    o = r2.take(8192); up_b1 = vbf(o, 4096).rearrange("p (k n) -> p k n", k=8)
    o = r2.take(8192); dn_b1 = vbf(o, 4096).rearrange("p (k n) -> p k n", k=4)
    o = r2.take(8192); fT = vbf(o, 4096).rearrange("p (w k n) -> p w k n", w=2, k=4)
    o = r2.take(4096); rtmp = vf32(o, 1024).rearrange("p (s n) -> p s n", s=2)
    o = r2.take(1024)
    o = r2.take(2048); junk2 = vbf(o, 1024); JUNK2_OFF = o
    o = r2.take(16384); hb2 = vbf(o, 8192).rearrange("p (w s d) -> p w s d", w=2, s=4); HB2_OFF = o
    o = r2.take(8 * 2080 * 2); h2T = vbf(o, 8 * 2080).rearrange("p (k n) -> p k n", k=8)
    R2END = r2.p

    r3 = Bump(W3END, ARENA_BYTES)
    o = r3.take(4096); gbc_b = vf32(o, 1024)
    o = r3.take(16384); hT3 = vbf(o, 8192).rearrange("p (w k n) -> p w k n", w=2, k=8)
    o = r3.take(4096); pb = vbf(o, 2048).rearrange("p (w s n) -> p w s n", w=2, s=4)
    o = r3.take(2048); pT = vbf(o, 1024).rearrange("p (k n) -> p k n", k=2)
    assert r3.p <= JUNK2_OFF
    r3.p = JUNK2_OFF
    o = r3.take(2048); junk3 = vbf(o, 1024)
    o = r3.take(16384); hb3 = vbf(o, 8192).rearrange("p (w s d) -> p w s d", w=2, s=4); assert o == HB2_OFF
    o = r3.take(6144); sg = vf32(o, 1536).rearrange("p (s n) -> p s n", s=3)
    o = r3.take(6144); ptmp = vf32(o, 1536).rearrange("p (s n) -> p s n", s=3)
    o = r3.take(16384); ytile = vf32(o, 4096).rearrange("p (s n) -> p s n", s=4)

    r1 = Bump(R0, ARENA_BYTES)
    o = r1.take(32768); win = vbf(o, 16384).rearrange("p (k n) -> p k n", k=8)
    o = r1.take(16384); wout = vbf(o, 8192).rearrange("p (k n) -> p k n", k=8)
    o = r1.take(1024); wpl = vbf(o, 512).rearrange("p (g n) -> p g n", g=4)
    o = r1.take(2048); junk1 = vbf(o, 1024); assert o == JUNK2_OFF
    o = r1.take(8192); hb1 = vbf(o, 4096).rearrange("p (s d) -> p s d", s=4); assert o == HB2_OFF
    o = r1.take(8192); hT1 = vbf(o, 4096).rearrange("p (k n) -> p k n", k=8)
    o = r1.take(4096); vS = vf32(o, 1024).rearrange("p (s n) -> p s n", s=2)
    o = r1.take(4096); acc = vf32(o, 1024).rearrange("p (s n) -> p s n", s=2)
    o = r1.take(4 * 514 * 4); hcv = vf32(o, 4 * 514).rearrange("p (c n) -> p c n", c=4)
    o = r1.take(4 * 527 * 4); U = vf32(o, 4 * 527).rearrange("p (c n) -> p c n", c=4)
    o = r1.take(2 * 528 * 4); Stmp = vf32(o, 2 * 528).rearrange("p (s n) -> p s n", s=2)
    o = r1.take(4096); dT = vbf(o, 2048).rearrange("p (s n) -> p s n", s=4)
    o = r1.take(8192); mix = vbf(o, 4096).rearrange("p (k n) -> p k n", k=8)
    o = r1.take(64); d16 = vf32(o, 16)
    o = r1.take(256); hT_halo = vbf(o, 128).rearrange("p (k n) -> p k n", k=8)
    o = r1.take(4 * 17 * 4); so_col = vf32(o, 68).rearrange("p (c n) -> p c n", c=4)
    o = r1.take(2048); so_row = vf32(o, 512)
    o = r1.take(2048); srow = vf32(o, 512)
    o = r1.take(512); crow = vf32(o, 128)

    setup_consts(crow)
    pg.dma("sp", lambda e: e.dma_start(out=xr[0:NHALO, 15, :], in_=xh[:, :]), writes=[xkey(15)])
    pg.dma("sp", lambda e: e.dma_start(out=gbc_a, in_=g_mix.partition_broadcast(128)), writes=[("gbc_a",)])
    for kk in range(4):
        pg.dma("pool", lambda e, kk=kk: e.dma_start(out=win[:, 2 * kk:2 * kk + 2, :],
                                                    in_=w_in[256 * kk:256 * (kk + 1), :].rearrange("(k p) n -> p k n", p=128)),
               writes=[("win", kk)])
    pg.dma("pool", lambda e: e.dma_start(out=wpl, in_=w_pool.rearrange("g c d -> c g d")), writes=[("wpl",)])
    for s_ in range(4):
        pg.dma("sp", lambda e, s_=s_: e.dma_start(out=xr[:, s_, :], in_=xp[128 * s_:128 * (s_ + 1), :]), writes=[xkey(s_)])
    pg.op("pool", lambda e: e.memset(srow[0:32, :], 0.0), writes=[("srow",)])
    pg.dma("sp", lambda e: e.dma_start(out=srow[0:2, :], in_=sconv[:, :]), writes=[("srow",)])
    pg.dma("sp", lambda e: e.dma_start(out=srow[2:17, :], in_=spool[:, :]), writes=[("srow",)])

    WIN_KEYS = [("win", kk) for kk in range(4)]
    WOUT_KEYS = [("wout", kk) for kk in range(2)]
    ZB = [2, 3, 4, 0, 1]
    zrot = [0]

    def zbank():
        b = ZB[zrot[0] % len(ZB)]
        zrot[0] += 1
        return b

    def zmm(tile, col0, bank):
        ntok = tile[1]
        hT, hTkey = (hT_halo, ("hT_halo",)) if tile[0] == "halo" else (hT1, ("hT1",))
        fns = []
        for k in range(8):
            fns.append(lambda e, k=k: e.matmul(PS[bank][:, 0:ntok], lhsT=win[:, k, col0:col0 + 128], rhs=hT[:, k, 0:ntok],
                                               start=(k == 0), stop=(k == 7)))
        pg.op("pe", fns, reads=[("win", kk) for kk in range(4)] + [hTkey], writes=[("ps", bank)])

    def pool_sums(tile, c):
        ntok = tile[1]
        nlev = c + 1
        L = ntok + 15
        src = U[:, c, :]
        skey_src = ("U", c)
        for lev in range(nlev):
            sh = 1 << lev
            lo = (1 << (lev + 1)) - 1
            if lev == nlev - 1:
                lo = 15
            dst = Stmp[:, lev % 2, :]
            pg.op("pool", lambda e, dst=dst, src=src, lo=lo, sh=sh, L=L: e.tensor_tensor(
                out=dst[:, lo:L], in0=src[:, lo:L], in1=src[:, lo - sh:L - sh], op=ALU.add),
                reads=[skey_src], writes=[("Stmp", lev % 2)])
            src = dst
            skey_src = ("Stmp", lev % 2)
        return src[:, 15:15 + ntok], skey_src

    def zstage(tile, full=True, first_prompt=False):
        name, ntok, nsub, rows, xs0 = tile
        for c in range(4):
            b = zbank()
            zmm(tile, 1536 + 128 * c, b)
            pg.op("act", lambda e, c=c, b=b: e.activation(out=U[:, c, 15:15 + ntok], in_=PS[b][:, 0:ntok], func=AF.Copy),
                  reads=[("ps", b)], writes=[("U", c)])
            if not full:
                continue
            sums, sk = pool_sums(tile, c)
            oth = 1 - (c % 2)
            tmpd = Stmp[:, oth, 15:15 + ntok]
            invw = 1.0 / WINDOWS[c]
            pg.op("pool", lambda e, sums=sums, tmpd=tmpd, invw=invw: e.tensor_scalar(out=tmpd, in0=sums, scalar1=invw, scalar2=0.0,
                                                                                     op0=ALU.mult, op1=ALU.add),
                  reads=[sk], writes=[("Stmp", oth)])
            if first_prompt:
                pg.op("pool", lambda e, c=c, sums=sums, tmpd=tmpd: e.tensor_tensor(out=tmpd[:, 0:16], in0=sums[:, 0:16],
                                                                                   in1=invc[:, c, :], op=ALU.mult),
                      reads=[sk, ("invc",), ("Stmp", oth)], writes=[("Stmp", oth)])
            pg.op("pool", lambda e, c=c, tmpd=tmpd: e.tensor_tensor(out=dT[:, c, 0:ntok], in0=tmpd, in1=U[:, c, 15:15 + ntok],
                                                                    op=ALU.subtract),
                  reads=[("Stmp", oth), ("U", c)], writes=[("dT", c)])
        for c in range(4):
            slot = c % 2
            b = zbank()
            zmm(tile, 1024 + 128 * c, b)
            pg.op("act", lambda e, b=b, slot=slot: e.activation(out=vS[:, slot, 0:ntok], in_=PS[b][:, 0:ntok], func=AF.Copy),
                  reads=[("ps", b)], writes=[("vS", slot)])
            b = zbank()
            zmm(tile, 512 + 128 * c, b)
            pg.op("dve", lambda e, b=b, c=c, slot=slot: e.tensor_tensor(out=hcv[:, c, 2:2 + ntok], in0=PS[b][:, 0:ntok],
                                                                        in1=vS[:, slot, 0:ntok], op=ALU.mult),
                  reads=[("ps", b), ("vS", slot)], writes=[("hcv", c)])
            if not full:
                continue
            pg.op("dve", lambda e, c=c, slot=slot: e.tensor_scalar(out=acc[:, slot, 0:ntok], in0=hcv[:, c, 0:ntok],
                                                                   scalar1=ccol[:, c:c + 1], scalar2=None, op0=ALU.mult),
                  reads=[("hcv", c), ("ccol",)], writes=[("acc", slot)])
            for tap in (1, 2):
                pg.op("dve", lambda e, c=c, slot=slot, tap=tap: e.scalar_tensor_tensor(
                    out=acc[:, slot, 0:ntok], in0=hcv[:, c, tap:tap + ntok], scalar=ccol[:, 4 * tap + c:4 * tap + c + 1],
                    in1=acc[:, slot, 0:ntok], op0=ALU.mult, op1=ALU.add),
                    reads=[("hcv", c), ("ccol",), ("acc", slot)], writes=[("acc", slot)])
            if c == 3 and full:
                for cc in range(4):
                    b2 = zbank()
                    pg.op("pe", lambda e, cc=cc, b2=b2: e.matmul(PS[b2][:, 0:ntok], lhsT=wpl[:, cc, :], rhs=dT[:, cc, 0:ntok],
                                                                 start=True, stop=True),
                          reads=[("wpl",), ("dT", cc)], writes=[("ps", b2)])
                    pg.op("act", lambda e, cc=cc, b2=b2: e.activation(out=mix[:, 4 + cc, 0:ntok], in_=PS[b2][:, 0:ntok], func=AF.Copy,
                                                                      scale=ccol[:, 12 + cc:13 + cc]),
                          reads=[("ps", b2), ("ccol",)], writes=[("mix", 4 + cc)])
            b = zbank()
            zmm(tile, 128 * c, b)
            pg.op("dve", lambda e, b=b, c=c, slot=slot: e.tensor_tensor(out=mix[:, c, 0:ntok], in0=PS[b][:, 0:ntok],
                                                                        in1=acc[:, slot, 0:ntok], op=ALU.mult),
                  reads=[("ps", b), ("acc", slot)], writes=[("mix", c)])

    def hist_roll(tile):
        ntok = tile[1]
        pg.op("dve", lambda e: e.tensor_copy(out=hcv[:, :, 0:2], in_=hcv[:, :, ntok:ntok + 2]),
              reads=[("hcv", c) for c in range(4)], writes=[("hcv", c) for c in range(4)])
        pg.op("pool", lambda e: e.tensor_copy(out=U[:, :, 0:15], in_=U[:, :, ntok:ntok + 15]),
              reads=[("U", c) for c in range(4)], writes=[("U", c) for c in range(4)])

    OB = [5, 6, 7, 2, 3, 4]
    orot = [0]

    def wout_stage(tile, pre_group=None, post_group=None, tail=()):
        name, ntok, nsub, rows, xs0 = tile
        pre_group = pre_group or {}
        post_group = post_group or {}
        gi = -1
        for s in range(nsub):
            for hf in range(2):
                gi += 1
                for st in pre_group.get(gi, ()):
                    st()
                b = OB[orot[0] % len(OB)]; orot[0] += 1
                fns = []
                korder = (4, 5, 6, 7, 0, 1, 2, 3)
                for ki, k in enumerate(korder):
                    fns.append(lambda e, k=k, ki=ki, s=s, hf=hf, b=b: e.matmul(PS[b][0:rows, :], lhsT=mix[:, k, s * 128:s * 128 + rows],
                                                                               rhs=wout[:, k, hf * 512:(hf + 1) * 512],
                                                                               start=(ki == 0), stop=(ki == 7)))
                pg.op("pe", fns, reads=[("mix", k) for k in range(8)] + WOUT_KEYS, writes=[("ps", b)])
                pg.op("dve", lambda e, s=s, hf=hf, b=b: e.tensor_tensor(out=xr[:rows, xs0 + s, hf * 512:(hf + 1) * 512],
                                                                        in0=xr[:rows, xs0 + s, hf * 512:(hf + 1) * 512],
                                                                        in1=PS[b][0:rows, :], op=ALU.add),
                      reads=[("ps", b), xkey(xs0 + s)], writes=[xkey(xs0 + s)])
                for st in post_group.get(gi, ()):
                    st()
        for st in tail:
            st()

    def state_out(tile, dconv, dpool):
        ntok = tile[1]
        pg.op("dve", lambda e: e.tensor_copy(out=so_col[:, :, 0:2], in_=hcv[:, :, ntok:ntok + 2]),
              reads=[("hcv", c) for c in range(4)], writes=[("so_col",)])
        pg.op("dve", lambda e: e.tensor_copy(out=so_col[:, :, 2:17], in_=U[:, :, ntok:ntok + 15]),
              reads=[("U", c) for c in range(4)], writes=[("so_col",)])
        fns = [lambda e, c=c: e.transpose(out=PS[7][0:17, c * 128:(c + 1) * 128], in_=so_col[:, c, :], identity=identf[:, :])
               for c in range(4)]
        pg.op("pe", fns, reads=[("so_col",), ("identf",)], writes=[("ps", 7)])
        pg.op("act", lambda e: e.activation(out=so_row[0:17, :], in_=PS[7][0:17, :], func=AF.Copy),
              reads=[("ps", 7)], writes=[("so_row",)])
        pg.dma("sp", lambda e: e.dma_start(out=dconv[:, :], in_=so_row[0:2, :]), reads=[("so_row",)], store=True)
        pg.dma("sp", lambda e: e.dma_start(out=dpool[:, :], in_=so_row[2:17, :]), reads=[("so_row",)], store=True)

    def state_in():
        fns = [lambda e, c=c: e.transpose(out=PS[7][:, c * 18:(c + 1) * 18], in_=srow[0:18, c * 128:(c + 1) * 128],
                                          identity=identf[0:18, 0:18]) for c in range(4)]
        pg.op("pe", fns, reads=[("srow",), ("identf",)], writes=[("ps", 7)])
        pv = PS[7][:, 0:72].rearrange("p (c n) -> p c n", c=4)
        pg.op("dve", lambda e: e.tensor_copy(out=hcv[:, :, 0:2], in_=pv[:, :, 0:2]),
              reads=[("ps", 7)] + [("hcv", c) for c in range(4)], writes=[("hcv", c) for c in range(4)])
        pg.op("dve", lambda e: e.tensor_copy(out=U[:, :, 0:15], in_=pv[:, :, 2:17]),
              reads=[("ps", 7)] + [("U", c) for c in range(4)], writes=[("U", c) for c in range(4)])

    def nst1_a(tile):
        nstage_a(tile, gbc_a, ("gbc_a",), junk1, hb1)

    def nst1_a_steps(tile):
        return nstage_a_steps(tile, gbc_a, ("gbc_a",), junk1, hb1)

    def nst1_b(tile):
        nstage_b(tile, hb1, hT1, ("hT1",), (0, 1))

    def nst3_a_steps(i, tile):
        return nstage_a_steps(tile, gbc_a, ("gbc_a",), junk3, hb3[:, i % 2, :, :], hbk="hb2_%d" % (i % 2))

    def nst3_b(i, tile):
        nstage_b(tile, hb3[:, i % 2, :, :], hT3[:, i % 2, :, :], ("hT3", i % 2), (0, 1), hbk="hb2_%d" % (i % 2))

    nsteps = {}

    def get_n(k):
        if k not in nsteps:
            st = nst3_a_steps(k, MAIN[k]); n = MAIN[k][2]
            nsteps[k] = (st[:n], st[n], st[n + 1:])
        return nsteps[k]

    halo = TILES[0]
    nst1_a(halo)
    nstage_b(halo, hb1, hT_halo, ("hT_halo",), (0, 1))
    nst1_a(MAIN[0])
    nst1_b(MAIN[0])
    for kk in range(2):
        pg.dma("pool", lambda e, kk=kk: e.dma_start(out=wout[:, 4 * kk:4 * kk + 4, :],
                                                    in_=w_out[512 * kk:512 * (kk + 1), :].rearrange("(k p) n -> p k n", p=128)),
               writes=[("wout", kk)])
    zstage(halo, full=False)
    halo_done = (pg.sem["pe"], pg.cnt["pe"], "pe")
    hist_roll(halo)
    for t in (1, 2):
        pg.dma("sp", lambda e, t=t: e.dma_start(out=xr[:, 4 * t:4 * t + 4, :],
                                                in_=xp[TS * t:TS * (t + 1), :].rearrange("(s p) d -> p s d", p=128)),
               writes=[xkey(4 * t + s) for s in range(4)], after=[halo_done])
    pg.dma("sp", lambda e: e.dma_start(out=xr[0:NSAMP, 16, :], in_=xs[:, :]), writes=[xkey(16)], after=[halo_done])
    pg.dma("sp", lambda e: e.dma_start(out=xr[:, 12:16, :], in_=xp[TS * 3:TS * 4, :].rearrange("(s p) d -> p s d", p=128)),
           writes=[xkey(12 + s) for s in range(4)])
    pg.dma("pool", lambda e: e.dma_start(out=up_b0, in_=w_up[:, 0:FB].rearrange("(k p) n -> p k n", p=128)), writes=[("up", 0)])
    pg.dma("pool", lambda e: e.dma_start(out=dn_b0, in_=w_down[0:FB, :].rearrange("(k p) n -> p k n", p=128)), writes=[("dn", 0)])

    st1 = {}

    def get1(k):
        if k not in st1:
            st = nst1_a_steps(MAIN[k]); n = MAIN[k][2]
            st1[k] = (st[:n], st[n], st[n + 1:])
        return st1[k]

    for i, tile in enumerate(MAIN):
        nxt = MAIN[i + 1] if i + 1 < len(MAIN) else None
        if nxt is not None and i >= 1:
            for st in get1(i + 1)[2]:
                st()
        if tile[0] == "smp":
            state_in()
        if tile[0] == "p3":
            pre0 = nstage_a_steps(MAIN[0], gbc_a, ("gbc_a",), junk1, hb1, hbk="hb")
            pre1 = nstage_a_steps(MAIN[1], gbc_a, ("gbc_a",), junk1, hb2[:, 1, :, :], hbk="hb2_1")
        zstage(tile, full=True, first_prompt=(i == 0))
        if tile[0] == "smp":
            for st in pre0[MAIN[0][2] + 1:]:
                st()
            pre2 = nstage_a_steps(MAIN[2], gbc_a, ("gbc_a",), junk1, hb2[:, 0, :, :], hbk="hb2_0")
            pre3 = nstage_a_steps(MAIN[3], gbc_a, ("gbc_a",), junk1, hb2[:, 1, :, :], hbk="hb2_1")
            for st in pre2[:MAIN[2][2] + 1] + pre3[:MAIN[3][2] + 1]:
                st()
        pre_g, post_g, tail = {}, {}, []
        ngrp = 2 * tile[2]
        if nxt is not None and i == 0:
            sq, pw, sc = get1(1)
            for q, st in enumerate(sq):
                pre_g.setdefault(min(ngrp - 1, q // 2), []).append(st)
            post_g.setdefault(min(ngrp - 1, 1), []).append(pw)
            for q, st in enumerate(sc):
                post_g.setdefault(min(ngrp - 1, 2 + q), []).append(st)
            tsteps = nstage_b_steps(nxt, hb1, hT1, ("hT1",), (0, 1))
            post_g.setdefault(min(ngrp - 1, 6), []).append(tsteps[0])
            post_g.setdefault(min(ngrp - 1, 7), []).append(tsteps[1])
            tail += tsteps[2:]
        elif nxt is not None:
            tsteps = nstage_b_steps(nxt, hb1, hT1, ("hT1",), (0, 1))
            for q, st in enumerate(tsteps):
                post_g.setdefault(min(ngrp - 1, 2 * q + 1), []).append(st)
        if i + 2 < len(MAIN):
            sq, pw, _ = get1(i + 2)
            for q, st in enumerate(sq):
                pre_g.setdefault(min(ngrp - 1, 2 * q), []).append(st)
            tail.append(pw)
        if tile[0] == "p3":
            n0, n1 = MAIN[0][2], MAIN[1][2]
            for q, st in enumerate(pre0[:n0] + pre1[:n1]):
                pre_g.setdefault(min(ngrp - 1, q), []).append(st)
            tail += [pre0[n0], pre1[n1]]
        wout_stage(tile, pre_g, post_g, tail)
        if nxt is not None and nxt[0] == "smp":
            pg.dma("sp", lambda e: e.dma_start(out=gbc_a, in_=g_mlp.partition_broadcast(128)), writes=[("gbc_a",)])
        if tile[0] == "p3":
            state_out(tile, ncp, npp)
        if tile[0] == "smp":
            state_out(tile, ncs, nps)
        else:
            hist_roll(tile)

    if STOP_AFTER_PHASE >= 2:
        pg.barrier()
        UPB = [up_b0, up_b1]; DNB = [dn_b0, dn_b1]

        tokoff = {}
        off = 0
        for tile in MAIN:
            tokoff[tile[0]] = off
            off += tile[1]

        def load_w2(j):
            pg.dma("pool", lambda e, j=j: e.dma_start(out=UPB[j % 2], in_=w_up[:, j * FB:(j + 1) * FB].rearrange("(k p) n -> p k n", p=128)),
                   writes=[("up", j % 2)])
            pg.dma("pool", lambda e, j=j: e.dma_start(out=DNB[j % 2], in_=w_down[j * FB:(j + 1) * FB, :].rearrange("(k p) n -> p k n", p=128)),
                   writes=[("dn", j % 2)])

        UB = [2, 3, 4]; DB = [5, 6, 7]
        UB4 = [2, 3, 4, 0]; DB4 = [5, 6, 7, 1]
        urot = [0]; drot = [0]

        def up_stage(j, tile, w, post=()):
            name, ntok, nsub, rows, xs0 = tile
            t0 = tokoff[name]
            post = list(post)
            for c in range(4):
                ub = UB if j == 0 else UB4
                b = ub[urot[0] % len(ub)]; urot[0] += 1
                fns = [lambda e, k=k, c=c, b=b: e.matmul(PS[b][:, 0:ntok], lhsT=UPB[j % 2][:, k, c * 128:(c + 1) * 128],
                                                         rhs=h2T[:, k, t0:t0 + ntok], start=(k == 0), stop=(k == 7))
                       for k in range(8)]
                pg.op("pe", fns, reads=[("up", j % 2), ("h2T", name)], writes=[("ps", b)])
                rs = c % 2
                pg.op("act", lambda e, b=b, rs=rs: e.activation(out=rtmp[:, rs, 0:ntok], in_=PS[b][:, 0:ntok], func=AF.Relu),
                      reads=[("ps", b)], writes=[("rtmp", rs)])
                pg.op("dve", lambda e, c=c, rs=rs: e.tensor_tensor(out=fT[:, w % 2, c, 0:ntok], in0=rtmp[:, rs, 0:ntok],
                                                                   in1=rtmp[:, rs, 0:ntok], op=ALU.mult),
                      reads=[("rtmp", rs)], writes=[("fT", w % 2, c)])
                if c < len(post):
                    post[c]()

        def down_stage(j, tile, w):
            name, ntok, nsub, rows, xs0 = tile
            for s in range(nsub):
                for hf in range(2):
                    db = DB if j == 0 else DB4
                    b = db[drot[0] % len(db)]; drot[0] += 1
                    fns = [lambda e, k=k, s=s, hf=hf, b=b: e.matmul(PS[b][0:rows, :], lhsT=fT[:, w % 2, k, s * 128:s * 128 + rows],
                                                                    rhs=DNB[j % 2][:, k, hf * 512:(hf + 1) * 512],
                                                                    start=(k == 0), stop=(k == 3))
                           for k in range(4)]
                    pg.op("pe", fns, reads=[("fT", w % 2, k) for k in range(4)] + [("dn", j % 2)], writes=[("ps", b)])
                    pg.op("dve", lambda e, s=s, hf=hf, b=b: e.tensor_tensor(out=xr[:rows, xs0 + s, hf * 512:(hf + 1) * 512],
                                                                            in0=xr[:rows, xs0 + s, hf * 512:(hf + 1) * 512],
                                                                            in1=PS[b][0:rows, :], op=ALU.add),
                          reads=[("ps", b), xkey(xs0 + s)], writes=[xkey(xs0 + s)])

        pg.dma("pool", lambda e: e.dma_start(out=wple, in_=w_ple.rearrange("(k p) n -> p k n", p=128)), writes=[("wple",)])

        def n2a(ti):
            nstage_a(MAIN[ti], gbc_a, ("gbc_a",), junk2, hb2[:, ti % 2, :, :], hbk="hb2_%d" % (ti % 2))

        def n2b(ti):
            tile = MAIN[ti]
            t0 = tokoff[tile[0]]
            nstage_b(tile, hb2[:, ti % 2, :, :], h2T[:, :, t0:t0 + tile[1]], ("h2T", tile[0]), (0, 1), hbk="hb2_%d" % (ti % 2))

        prev = None
        w = 0
        for j in range(NB):
            if j == 2:
                for kk in range(2):
                    pg.dma("pool", lambda e, kk=kk: e.dma_start(out=wgate[:, 4 * kk:4 * kk + 4, :],
                                                                in_=w_gate[512 * kk:512 * (kk + 1), :].rearrange("(k p) n -> p k n", p=128)),
                           writes=[("wgate", kk)])
            for ti, tile in enumerate(MAIN):
                if j == 0 and ti == 0:
                    for st in pre1[MAIN[1][2] + 1:]:
                        st()
                    n2b(0)
                if j == 0 and ti + 1 < len(MAIN):
                    nt_ = MAIN[ti + 1]
                    t0_ = tokoff[nt_[0]]
                    k_ = (ti + 1) % 2
                    tsteps = nstage_b_steps(nt_, hb2[:, k_, :, :], h2T[:, :, t0_:t0_ + nt_[1]], ("h2T", nt_[0]), (0, 1),
                                            hbk="hb2_%d" % k_)
                    if ti + 2 == 2:
                        ssteps = pre2[MAIN[2][2] + 1:]
                    elif ti + 2 == 3:
                        ssteps = pre3[MAIN[3][2] + 1:]
                    elif ti + 2 < len(MAIN):
                        ssteps = [lambda: n2a(ti + 2)]
                    else:
                        ssteps = []
                    post_ = []
                    for c_ in range(4):
                        def both(c_=c_, tsteps=tsteps, ssteps=ssteps):
                            tsteps[c_]()
                            if c_ < len(ssteps):
                                ssteps[c_]()
                        post_.append(both)
                    up_stage(j, tile, w, post=post_)
                else:
                    up_stage(j, tile, w)
                if prev is not None:
                    down_stage(*prev)
                    if prev[0] == NB - 1 and prev[1] in (MAIN[0], MAIN[1]):
                        kk = 0 if prev[1] is MAIN[0] else 1
                        sq, pw, sc = get_n(kk)
                        for st in sq:
                            st()
                        pw()
                        for st in sc:
                            st()
                if j == 1 and ti == 0:
                    pg.dma("sp", lambda e: e.dma_start(out=gbc_a, in_=g_ple.partition_broadcast(128)), writes=[("gbc_a",)])

                if tile is MAIN[0] and j + 1 < NB:
                    load_w2(j + 1)
                prev = (j, tile, w)
                w += 1
        down_stage(*prev)

    if STOP_AFTER_PHASE >= 3:
        pg.barrier()
        pg.dma("sp", lambda e: e.dma_start(out=gbc_b, in_=g_final.partition_broadcast(128)), writes=[("gbc_b",)])

        def load_p(i, tile):
            name, ntok, nsub, rows, xs0 = tile
            if name == "smp":
                pg.dma("pool", lambda e: e.dma_start(out=pb[0:NSAMP, i % 2, 0, :], in_=psm[:, :]), writes=[("pb", i % 2)])
            else:
                t = int(name[1:])
                pg.dma("pool", lambda e, t=t: e.dma_start(out=pb[:, i % 2, :, :],
                                                          in_=pp[TS * t:TS * (t + 1), :].rearrange("(s p) n -> p s n", p=128)),
                       writes=[("pb", i % 2)])


        yrot = [0]; srot = [0]

        def ple_pre(i, tile):
            name, ntok, nsub, rows, xs0 = tile
            tv = ps_bf(0)
            fns = []
            for s in range(nsub):
                for k in range(2):
                    fns.append(lambda e, s=s, k=k: e.transpose(out=tv[:, k, s * 128:s * 128 + rows],
                                                               in_=pb[:rows, i % 2, s, k * 128:(k + 1) * 128],
                                                               identity=identb[:rows, :rows]))
            pg.op("pe", fns, reads=[("pb", i % 2), ("const",)], writes=[("ps", 0)])
            pg.op("act", lambda e: e.activation(out=pT[:, :, 0:ntok], in_=tv[:, :, 0:ntok], func=AF.Copy),
                  reads=[("ps", 0)], writes=[("pT",)])

        def ple_main(i, tile, bg_act, bg_dve, post_group=None):
            name, ntok, nsub, rows, xs0 = tile
            post_group = post_group or {}
            ngrp = 2 * nsub
            pa = -(-len(bg_act) // ngrp) if bg_act else 0
            pd = -(-len(bg_dve) // ngrp) if bg_dve else 0
            gi = 0
            for s in range(nsub):
                for hf in range(2):
                    for st in bg_act[gi * pa:(gi + 1) * pa]:
                        st()
                    for st in bg_dve[gi * pd:(gi + 1) * pd]:
                        st()
                    gi += 1
                    q = srot[0] % 3; srot[0] += 1
                    bp = (2, 3, 7)[q]; bg = (4, 5, 6)[q]
                    fns = [lambda e, k=k, s=s, hf=hf, bp=bp: e.matmul(PS[bp][0:rows, :], lhsT=pT[:, k, s * 128:s * 128 + rows],
                                                                      rhs=wple[:, k, hf * 512:(hf + 1) * 512],
                                                                      start=(k == 0), stop=(k == 1)) for k in range(2)]
                    pg.op("pe", fns, reads=[("pT",), ("wple",)], writes=[("ps", bp)])
                    fns = [lambda e, k=k, s=s, hf=hf, bg=bg: e.matmul(PS[bg][0:rows, :], lhsT=hT3[:, i % 2, k, s * 128:s * 128 + rows],
                                                                      rhs=wgate[:, k, hf * 512:(hf + 1) * 512],
                                                                      start=(k == 0), stop=(k == 7)) for k in range(8)]
                    pg.op("pe", fns, reads=[("hT3", i % 2), ("wgate", 0), ("wgate", 1)], writes=[("ps", bg)])
                    pg.op("act", lambda e, q=q, bg=bg: e.activation(out=sg[:rows, q, :], in_=PS[bg][0:rows, :], func=AF.Sigmoid),
                          reads=[("ps", bg)], writes=[("sg", q)])
                    pg.op("dve", lambda e, q=q, bp=bp: e.tensor_tensor(out=ptmp[:rows, q, :], in0=PS[bp][0:rows, :],
                                                                       in1=sg[:rows, q, :], op=ALU.mult),
                          reads=[("ps", bp), ("sg", q)], writes=[("ptmp", q)])
                    pg.op("pool", lambda e, s=s, hf=hf, q=q: e.tensor_tensor(out=xr[:rows, xs0 + s, hf * 512:(hf + 1) * 512],
                                                                             in0=xr[:rows, xs0 + s, hf * 512:(hf + 1) * 512],
                                                                             in1=ptmp[:rows, q, :], op=ALU.add),
                          reads=[("ptmp", q), xkey(xs0 + s)], writes=[xkey(xs0 + s)])
                    for st in post_group.get(gi - 1, ()):
                        st()

        def final_steps(i, tile):
            name, ntok, nsub, rows, xs0 = tile
            k = statflip[0]; statflip[0] = (statflip[0] + 1) % NSTAT
            ssq = stat[:, k * 24: k * 24 + 8]; std = stat[:, k * 24 + 8: k * 24 + 16]; rstd = stat[:, k * 24 + 16: k * 24 + 24]
            skey = ("stat", k)
            steps = []
            for s in range(nsub):
                steps.append(lambda s=s: pg.op("act", lambda e: e.activation(out=junk3[:rows, :], in_=xr[:rows, xs0 + s, :], func=AF.Square,
                                                                            accum_out=ssq[:rows, s:s + 1]),
                                               reads=[xkey(xs0 + s)], writes=[("junk",), skey]))

            def powstep():
                pg.op("pool", lambda e: e.tensor_scalar(out=std[:rows, 0:nsub], in0=ssq[:rows, 0:nsub], scalar1=1.0 / D, scalar2=EPS,
                                                        op0=ALU.mult, op1=ALU.add), reads=[skey], writes=[skey])
                pg.op("pool", lambda e: e.tensor_tensor(out=rstd[:rows, 0:nsub], in0=std[:rows, 0:nsub], in1=mhalf[:rows, 0:nsub],
                                                        op=ALU.pow), reads=[skey, ("const",)], writes=[skey])
            steps.append(powstep)

            def ystep(s):
                ys = yrot[0] % 4; yrot[0] += 1
                pg.op("dve", lambda e: e.scalar_tensor_tensor(out=ytile[:rows, ys, :], in0=xr[:rows, xs0 + s, :],
                                                              scalar=rstd[:rows, s:s + 1], in1=gbc_b[:rows, :],
                                                              op0=ALU.mult, op1=ALU.mult),
                      reads=[xkey(xs0 + s), skey, ("gbc_b",)], writes=[("ytile", ys)])
                if name == "smp":
                    pg.dma("sp", lambda e: e.dma_start(out=y_s[:, :], in_=ytile[0:NSAMP, ys, :]), reads=[("ytile", ys)], store=True)
                else:
                    r0 = (xs0 + s) * 128
                    pg.dma("sp", lambda e: e.dma_start(out=y_p[r0:r0 + 128, :], in_=ytile[:, ys, :]), reads=[("ytile", ys)], store=True)
            for s in range(nsub):
                steps.append(lambda s=s: ystep(s))
            return steps

        def merge(a, b):
            out = []
            for j in range(max(len(a), len(b))):
                if j < len(a):
                    out.append(a[j])
                if j < len(b):
                    out.append(b[j])
            return out

        nT = len(MAIN)
        NSUB = [t[2] for t in MAIN]
        fsteps = {}

        def get_f(k):
            if k not in fsteps:
                st = final_steps(k, MAIN[k]); n = NSUB[k]
                fsteps[k] = (st[:n], st[n], st[n + 1:])
            return fsteps[k]

        def run(steps):
            for st in steps:
                st()

        load_p(0, MAIN[0])
        nst3_b(0, MAIN[0]); ple_pre(0, MAIN[0])
        for i, tile in enumerate(MAIN):
            if i + 1 < nT:
                load_p(i + 1, MAIN[i + 1])
            bg_act, bg_dve, pows = [], [], []
            post_g = {}
            ngrp = 2 * tile[2]
            nF = 0
            if i >= 1:
                sq, pwF, yF = get_f(i - 1); bg_act += sq; nF = len(sq)
            if i + 2 < nT:
                sq, pw, _ = get_n(i + 2); bg_act += sq; pows.append(pw)
            if i >= 1 and i + 1 < nT:
                bg_dve += get_n(i + 1)[2]
            tail = []
            if i >= 1 and i == nT - 1:
                bg_act.append(pwF)
                tail += list(yF)
            elif i >= 1:
                pa = -(-len(bg_act) // ngrp)
                gF = (nF - 1) // pa
                post_g.setdefault(gF, []).append(pwF)
                for q, st in enumerate(yF):
                    if gF + 1 + q < ngrp:
                        post_g.setdefault(gF + 1 + q, []).append(st)
                    else:
                        tail.append(st)
            if i + 1 < nT:
                k1 = (i + 1) % 2
                tsteps = nstage_b_steps(MAIN[i + 1], hb3[:, k1, :, :], hT3[:, k1, :, :], ("hT3", k1), (0, 1), hbk="hb2_%d" % k1)
                for q, st in enumerate(tsteps):
                    post_g.setdefault(min(ngrp - 1, max(0, ngrp - 4) + q), []).append(st)
            ple_main(i, tile, bg_act, bg_dve, post_g)
            for pw in pows:
                pw()
            run(tail)
            if i + 1 < nT:
                ple_pre(i + 1, MAIN[i + 1])
        sq, pw, sc = get_f(nT - 1)
        run(sq); pw()
        run(sc)

    if STOP_AFTER_PHASE < 3:
        for t in range(4):
            pg.dma("sp", lambda e, t=t: e.dma_start(out=y_p[TS * t:TS * (t + 1), :].rearrange("(s p) d -> p s d", p=128),
                                                    in_=xr[:, 4 * t:4 * t + 4, :]),
                   reads=[xkey(4 * t + s) for s in range(4)], store=True)
        pg.dma("sp", lambda e: e.dma_start(out=y_s[:, :], in_=xr[0:NSAMP, 16, :]), reads=[xkey(16)], store=True)

    pg.final_wait()
    with nc.Block() as block:
        pg.emit(block)
    pg.close()
    for cm in reversed(ps_cms):
        cm.__exit__(None, None, None)
    arena_cm.__exit__(None, None, None)
    return nc


def _invcnt(core):
    out = np.empty((128, 4, 16), np.float32)
    for g, w in enumerate(WINDOWS):
        if core == 0:
            cnt = np.minimum(np.arange(16) + 1, w).astype(np.float32)
        else:
            cnt = np.full(16, float(w), np.float32)
        out[:, g, :] = (1.0 / cnt)[None, :]
    return out.reshape(128, 64)


def kernel(x_prompt, x_sample, state_conv, state_pool, p_prompt, p_sample, g_mix, w_in, w_conv,
           w_pool, pool_scale, w_out, g_mlp, w_up, w_down, g_ple, w_ple, w_ple_gate, g_final):
    f = lambda a: np.ascontiguousarray(np.asarray(a, dtype=np.float32))
    xpr = f(x_prompt)[0]; xsm = f(x_sample); ppr = f(p_prompt)[0, 0]; psmp = f(p_sample)[0]
    sc = f(state_conv)[0]; spl = f(state_pool)[0]
    shared = {
        "w_in": f(w_in)[0], "w_out": f(w_out)[0], "w_up": f(w_up)[0], "w_down": f(w_down)[0],
        "w_ple": f(w_ple)[0], "w_gate": f(w_ple_gate)[0], "w_pool": f(w_pool)[0],
        "g_mix": f(g_mix)[0], "g_mlp": f(g_mlp)[0], "g_ple": f(g_ple)[0], "g_final": f(g_final),
        "w_conv": f(w_conv)[0], "pool_scale": f(pool_scale)[0],
    }
    in_maps = []
    for c in range(NCORES):
        m = dict(shared)
        m["xp"] = xpr[c * NPT:(c + 1) * NPT]
        m["xh"] = xpr[c * NPT - NHALO:c * NPT] if c > 0 else np.zeros((NHALO, D), np.float32)
        m["xs"] = xsm[c]
        m["pp"] = ppr[c * NPT:(c + 1) * NPT]
        m["psm"] = psmp[c]
        m["sconv"] = sc[c]
        m["spool"] = spl[c]
        m["invcnt"] = _invcnt(c)
        in_maps.append(m)
    nc = build_program()
    res = run_bass_kernel_spmd(nc, in_maps, core_ids=list(range(NCORES)))
    rs = res.results
    y_prompt = np.concatenate([rs[c]["y_p"] for c in range(NCORES)], axis=0)[None]
    y_sample = np.stack([rs[c]["y_s"] for c in range(NCORES)], axis=0)
    ncp = rs[NCORES - 1]["ncp"][None, None]
    npp = rs[NCORES - 1]["npp"][None, None]
    ncs = np.stack([rs[c]["ncs"] for c in range(NCORES)], axis=0)[None]
    nps = np.stack([rs[c]["nps"] for c in range(NCORES)], axis=0)[None]
    return (y_prompt.astype(np.float32), y_sample.astype(np.float32), ncp.astype(np.float32),
            npp.astype(np.float32), ncs.astype(np.float32), nps.astype(np.float32))
```

```python
import numpy as np
import concourse.bass as bass
import concourse.mybir as mybir
from concourse.bass_utils import run_bass_kernel_spmd

F32 = mybir.dt.float32
BF16 = mybir.dt.bfloat16
AF = mybir.ActivationFunctionType
ALU = mybir.AluOpType

NCORES = 8
D = 1024
NPT = 2048
TS = 512
NSAMP = 32
NHALO = 16
DFF = 4096
NB = 8
FB = DFF // NB
EPS = 1e-6
WINDOWS = (2, 4, 8, 16)

ARENA_BYTES = 205 * 1024
STOP_AFTER_PHASE = 3


class Slot:
    __slots__ = ("w", "r")

    def __init__(self):
        self.w = None
        self.r = []


class Prog:
    ENGS = ("pe", "act", "dve", "pool", "sp")

    def __init__(self, nc):
        self.nc = nc
        self.ops = {e: [] for e in self.ENGS}
        self.sem = {}
        self.cnt = {e: 0 for e in self.ENGS}
        self.waited = {e: {} for e in self.ENGS}
        self.slots = {}
        self.dma_sems = []
        self.sem_objs = {}
        self.nsem = 0
        self.store_tokens = []
        self.pool_dmas = []
        self._cms = []
        for e in ("pe", "act", "dve", "pool"):
            self.sem[e] = self._new_sem("s_" + e)

    def _new_sem(self, name):
        cm = self.nc.semaphore(name)
        h = cm.__enter__()
        self._cms.append(cm)
        self.nsem += 1
        sid = self.nsem
        self.sem_objs[sid] = h
        return sid

    def close(self):
        for cm in reversed(self._cms):
            cm.__exit__(None, None, None)

    def slot(self, key):
        s = self.slots.get(key)
        if s is None:
            s = Slot()
            self.slots[key] = s
        return s

    def _deps(self, reads, writes):
        deps = []
        for k in reads:
            s = self.slot(k)
            if s.w is not None:
                deps.append(s.w)
        for k in writes:
            s = self.slot(k)
            if s.w is not None:
                deps.append(s.w)
            deps.extend(s.r)
        return deps

    def _waits(self, eng, deps):
        best = {}
        for (sid, val, src) in deps:
            if src == eng and eng == "pe":
                continue
            if val > best.get(sid, 0):
                best[sid] = val
        out = []
        wd = self.waited[eng]
        for sid, val in best.items():
            if wd.get(sid, 0) >= val:
                continue
            wd[sid] = val
            out.append((sid, val))
        return out

    def _commit(self, tok, reads, writes):
        for k in reads:
            self.slot(k).r.append(tok)
        for k in writes:
            s = self.slot(k)
            s.w = tok
            s.r = []

    def op(self, eng, fns, reads=(), writes=()):
        if not isinstance(fns, (list, tuple)):
            fns = [fns]
        waits = self._waits(eng, self._deps(reads, writes))
        self.cnt[eng] += 1
        tok = (self.sem[eng], self.cnt[eng], eng)
        self.ops[eng].append((waits, list(fns), (self.sem[eng], 1)))
        self._commit(tok, reads, writes)
        return tok

    def dma(self, queue, fn, reads=(), writes=(), store=False, after=()):
        deps = self._deps(reads, writes) + list(after)
        if queue == "pool":
            if len(self.pool_dmas) >= 4:
                deps.append(self.pool_dmas[-4])
        waits = self._waits(queue, deps)
        sid = self._new_sem("d%d" % self.nsem)
        tok = (sid, 16, "dma")
        self.ops[queue].append((waits, [fn], (sid, 16)))
        self._commit(tok, reads, writes)
        if queue == "pool":
            self.pool_dmas.append(tok)
        if store:
            self.store_tokens.append(tok)
        return tok

    def barrier(self):
        toks = [(self.sem[e], self.cnt[e], e) for e in ("pe", "act", "dve", "pool") if self.cnt[e] > 0]
        toks += self.store_tokens
        for e in self.ENGS:
            waits = self._waits(e, [t for t in toks if not (t[2] == e)])
            if waits:
                self.ops[e].append((waits, [], None))

    def final_wait(self):
        waits = self._waits("sp", self.store_tokens)
        if waits:
            self.ops["sp"].append((waits, [], None))

    def emit(self, block):
        nc = self.nc
        so = self.sem_objs

        def run(engname):
            def body(e):
                for waits, fns, inc in self.ops[engname]:
                    for sid, val in waits:
                        e.wait_ge(so[sid], val)
                    last = None
                    for f in fns:
                        last = f(e)
                    if inc is not None and last is not None:
                        last.then_inc(so[inc[0]], inc[1])
            return body

        block.sync(run("sp"))
        block.gpsimd(run("pool"))
        block.scalar(run("act"))
        block.vector(run("dve"))
        block.tensor(run("pe"))


class Bump:
    def __init__(self, base, limit):
        self.p = base
        self.limit = limit

    def take(self, nbytes):
        nbytes = (nbytes + 63) // 64 * 64
        o = self.p
        self.p += nbytes
        assert self.p <= self.limit, ("SBUF arena overflow", self.p, self.limit)
        return o


def build_program():
    nc = bass.Bass("TRN2", target_bir_lowering=False)

    def din(name, shape):
        return nc.dram_tensor(name, list(shape), F32, kind="ExternalInput").ap()

    def dout(name, shape):
        return nc.dram_tensor(name, list(shape), F32, kind="ExternalOutput").ap()

    xp = din("xp", (NPT, D)); xh = din("xh", (NHALO, D)); xs = din("xs", (NSAMP, D))
    pp = din("pp", (NPT, 256)); psm = din("psm", (NSAMP, 256))
    sconv = din("sconv", (2, 512)); spool = din("spool", (15, 512))
    invcnt = din("invcnt", (128, 64))
    w_in = din("w_in", (D, 2048)); w_out = din("w_out", (D, D)); w_up = din("w_up", (D, DFF))
    w_down = din("w_down", (DFF, D)); w_ple = din("w_ple", (256, D)); w_gate = din("w_gate", (D, D))
    w_pool = din("w_pool", (4, 128, 128))
    g_mix = din("g_mix", (D,)); g_mlp = din("g_mlp", (D,)); g_ple = din("g_ple", (D,)); g_final = din("g_final", (D,))
    w_conv = din("w_conv", (3, 512)); pool_scale = din("pool_scale", (512,))
    y_p = dout("y_p", (NPT, D)); y_s = dout("y_s", (NSAMP, D))
    ncp = dout("ncp", (2, 512)); npp = dout("npp", (15, 512)); ncs = dout("ncs", (2, 512)); nps = dout("nps", (15, 512))

    arena_cm = nc.sbuf_tensor("arena", [128, ARENA_BYTES // 4], F32)
    arena = arena_cm.__enter__()
    ps_cms = [nc.psum_tensor("ps%d" % i, [128, 512], F32) for i in range(8)]
    PS = [cm.__enter__() for cm in ps_cms]
    pg = Prog(nc)

    def vf32(off, n):
        return arena[:, off // 4: off // 4 + n]

    def vbf(off, n):
        return arena[:, off // 4: off // 4 + n // 2].bitcast(BF16)

    def ps_bf(bank):
        return PS[bank][:, 0:512].bitcast(BF16).rearrange("p (k n) -> p k n", k=2)

    pers = Bump(0, ARENA_BYTES)
    o = pers.take(17 * 4096); xr = vf32(o, 17 * 1024).rearrange("p (s d) -> p s d", s=17)
    o = pers.take(256); identb = vbf(o, 128)
    o = pers.take(512); identf = vf32(o, 128)
    o = pers.take(64); ccol = vf32(o, 16)
    o = pers.take(64); epsc = vf32(o, 1)
    o = pers.take(64); mhalf = vf32(o, 8)
    o = pers.take(256); invc = vf32(o, 64).rearrange("p (g n) -> p g n", g=4)
    NSTAT = 8
    o = pers.take(NSTAT * 96); stat = vf32(o, NSTAT * 24)
    o = pers.take(4096); gbc_a = vf32(o, 1024)
    o = pers.take(8192); up_b0 = vbf(o, 4096).rearrange("p (k n) -> p k n", k=8)
    o = pers.take(8192); dn_b0 = vbf(o, 4096).rearrange("p (k n) -> p k n", k=4)
    R0 = pers.p

    TILES = [("halo", NHALO, 1, NHALO, 15)] + [("p%d" % t, TS, 4, 128, 4 * t) for t in range(4)] + [("smp", NSAMP, 1, NSAMP, 16)]
    MAIN = TILES[1:]

    def xkey(slot):
        return ("x", slot)

    statflip = [0]

    def nstage_a_steps(tile, gbc, gkey, junk, hb, hbk="hb"):
        name, ntok, nsub, rows, xs0 = tile
        k = statflip[0]; statflip[0] = (statflip[0] + 1) % NSTAT
        ssq = stat[:, k * 24: k * 24 + 8]; std = stat[:, k * 24 + 8: k * 24 + 16]; rstd = stat[:, k * 24 + 16: k * 24 + 24]
        skey = ("stat", k)
        steps = []
        for s in range(nsub):
            steps.append(lambda s=s: pg.op("act", lambda e: e.activation(out=junk[:rows, :], in_=xr[:rows, xs0 + s, :], func=AF.Square,
                                                                        accum_out=ssq[:rows, s:s + 1]),
                                           reads=[xkey(xs0 + s)], writes=[("junk",), skey]))

        def powstep():
            pg.op("pool", lambda e: e.tensor_scalar(out=std[:rows, 0:nsub], in0=ssq[:rows, 0:nsub], scalar1=1.0 / D, scalar2=EPS,
                                                    op0=ALU.mult, op1=ALU.add), reads=[skey], writes=[skey])
            pg.op("pool", lambda e: e.tensor_tensor(out=rstd[:rows, 0:nsub], in0=std[:rows, 0:nsub], in1=mhalf[:rows, 0:nsub],
                                                    op=ALU.pow), reads=[skey, ("const",)], writes=[skey])
        steps.append(powstep)
        for s in range(nsub):
            steps.append(lambda s=s: pg.op("dve", lambda e: e.scalar_tensor_tensor(out=hb[:rows, s, :], in0=xr[:rows, xs0 + s, :],
                                                                                  scalar=rstd[:rows, s:s + 1], in1=gbc[:rows, :],
                                                                                  op0=ALU.mult, op1=ALU.mult),
                                           reads=[xkey(xs0 + s), skey, gkey], writes=[(hbk, s)]))
        return steps

    def nstage_a(tile, gbc, gkey, junk, hb, hbk="hb"):
        for st in nstage_a_steps(tile, gbc, gkey, junk, hb, hbk):
            st()

    def nstage_b_steps(tile, hb, hT, hTkey, tbanks, hbk="hb"):
        name, ntok, nsub, rows, xs0 = tile
        steps = []
        for cg in range(4):
            def step(cg=cg):
                bank = tbanks[cg % 2]
                tv = ps_bf(bank)
                fns = []
                for s in range(nsub):
                    for c in (2 * cg, 2 * cg + 1):
                        fns.append(lambda e, s=s, c=c, tv=tv: e.transpose(out=tv[:, c % 2, s * 128: s * 128 + rows],
                                                                          in_=hb[:rows, s, c * 128:(c + 1) * 128],
                                                                          identity=identb[:rows, :rows]))
                pg.op("pe", fns, reads=[(hbk, s) for s in range(nsub)] + [("const",)], writes=[("ps", bank)])
                pg.op("act", lambda e, tv=tv: e.activation(out=hT[:, 2 * cg:2 * cg + 2, 0:ntok], in_=tv[:, :, 0:ntok], func=AF.Copy),
                      reads=[("ps", bank)], writes=[hTkey])
            steps.append(step)
        return steps

    def nstage_b(tile, hb, hT, hTkey, tbanks, hbk="hb", split_evac=False):
        for st in nstage_b_steps(tile, hb, hT, hTkey, tbanks, hbk):
            st()

    def nstage(tile, gbc, gkey, junk, hb, hT, hTkey, tbanks):
        nstage_a(tile, gbc, gkey, junk, hb)
        nstage_b(tile, hb, hT, hTkey, tbanks)

    def setup_consts(crow):
        pg.op("pool", lambda e: e.memset(epsc, EPS), writes=[("const",)])
        pg.op("pool", lambda e: e.memset(mhalf, -0.5), writes=[("const",)])
        pg.op("pool", lambda e: e.iota(identf, pattern=[[1, 128]], base=0, channel_multiplier=-1,
                                       allow_small_or_imprecise_dtypes=True), writes=[("identf",)])
        pg.op("dve", lambda e: e.tensor_scalar(out=identf, in0=identf, scalar1=0.0, scalar2=None, op0=ALU.is_equal),
              reads=[("identf",)], writes=[("identf",)])
        pg.op("dve", lambda e: e.tensor_copy(out=identb, in_=identf), reads=[("identf",)], writes=[("const",)])
        pg.dma("sp", lambda e: e.dma_start(out=invc, in_=invcnt.rearrange("p (g n) -> p g n", g=4)), writes=[("invc",)])
        pg.dma("sp", lambda e: e.dma_start(out=crow[0:12, :], in_=w_conv.rearrange("k (c p) -> (k c) p", p=128)),
               writes=[("crow",)])
        pg.dma("sp", lambda e: e.dma_start(out=crow[12:16, :], in_=pool_scale.rearrange("(c p) -> c p", p=128)),
               writes=[("crow",)])
        pg.op("pe", lambda e: e.transpose(out=PS[7][:, 0:16], in_=crow[0:16, :], identity=identf[0:16, 0:16]),
              reads=[("crow",), ("identf",)], writes=[("ps", 7)])
        pg.op("dve", lambda e: e.tensor_copy(out=ccol, in_=PS[7][:, 0:16]), reads=[("ps", 7)], writes=[("ccol",)])

    r2 = Bump(R0, ARENA_BYTES)
    o = r2.take(4096); wple = vbf(o, 2048).rearrange("p (k n) -> p k n", k=2)
    o = r2.take(16384); wgate = vbf(o, 8192).rearrange("p (k n) -> p k n", k=8)
    W3END = r2.p
    o = r2.take(8192); up_b1 = vbf(o, 4096).rearrange("p (k n) -> p k n", k=8)
    o = r2.take(8192); dn_b1 = vbf(o, 4096).rearrange("p (k n) -> p k n", k=4)
    o = r2.take(8192); fT = vbf(o, 4096).rearrange("p (w k n) -> p w k n", w=2, k=4)
    o = r2.take(4096); rtmp = vf32(o, 1024).rearrange("p (s n) -> p s n", s=2)
    o = r2.take(1024)
    o = r2.take(2048); junk2 = vbf(o, 1024); JUNK2_OFF = o
    o = r2.take(16384); hb2 = vbf(o, 8192).rearrange("p (w s d) -> p w s d", w=2, s=4); HB2_OFF = o
    o = r2.take(8 * 2080 * 2); h2T = vbf(o, 8 * 2080).rearrange("p (k n) -> p k n", k=8)
    R2END = r2.p

    r3 = Bump(W3END, ARENA_BYTES)
    o = r3.take(4096); gbc_b = vf32(o, 1024)
    o = r3.take(16384); hT3 = vbf(o, 8192).rearrange("p (w k n) -> p w k n", w=2, k=8)
    o = r3.take(4096); pb = vbf(o, 2048).rearrange("p (w s n) -> p w s n", w=2, s=4)
    o = r3.take(2048); pT = vbf(o, 1024).rearrange("p (k n) -> p k n", k=2)
    assert r3.p <= JUNK2_OFF
    r3.p = JUNK2_OFF
    o = r3.take(2048); junk3 = vbf(o, 1024)
    o = r3.take(16384); hb3 = vbf(o, 8192).rearrange("p (w s d) -> p w s d", w=2, s=4); assert o == HB2_OFF
    o = r3.take(6144); sg = vf32(o, 1536).rearrange("p (s n) -> p s n", s=3)
    o = r3.take(6144); ptmp = vf32(o, 1536).rearrange("p (s n) -> p s n", s=3)
    o = r3.take(16384); ytile = vf32(o, 4096).rearrange("p (s n) -> p s n", s=4)

    r1 = Bump(R0, ARENA_BYTES)
    o = r1.take(32768); win = vbf(o, 16384).rearrange("p (k n) -> p k n", k=8)
    o = r1.take(16384); wout = vbf(o, 8192).rearrange("p (k n) -> p k n", k=8)
    o = r1.take(1024); wpl = vbf(o, 512).rearrange("p (g n) -> p g n", g=4)
    o = r1.take(2048); junk1 = vbf(o, 1024); assert o == JUNK2_OFF
    o = r1.take(8192); hb1 = vbf(o, 4096).rearrange("p (s d) -> p s d", s=4); assert o == HB2_OFF
    o = r1.take(8192); hT1 = vbf(o, 4096).rearrange("p (k n) -> p k n", k=8)
    o = r1.take(4096); vS = vf32(o, 1024).rearrange("p (s n) -> p s n", s=2)
    o = r1.take(4096); acc = vf32(o, 1024).rearrange("p (s n) -> p s n", s=2)
    o = r1.take(4 * 514 * 4); hcv = vf32(o, 4 * 514).rearrange("p (c n) -> p c n", c=4)
    o = r1.take(4 * 527 * 4); U = vf32(o, 4 * 527).rearrange("p (c n) -> p c n", c=4)
    o = r1.take(2 * 528 * 4); Stmp = vf32(o, 2 * 528).rearrange("p (s n) -> p s n", s=2)
    o = r1.take(4096); dT = vbf(o, 2048).rearrange("p (s n) -> p s n", s=4)
    o = r1.take(8192); mix = vbf(o, 4096).rearrange("p (k n) -> p k n", k=8)
    o = r1.take(64); d16 = vf32(o, 16)
    o = r1.take(256); hT_halo = vbf(o, 128).rearrange("p (k n) -> p k n", k=8)
    o = r1.take(4 * 17 * 4); so_col = vf32(o, 68).rearrange("p (c n) -> p c n", c=4)
    o = r1.take(2048); so_row = vf32(o, 512)
    o = r1.take(2048); srow = vf32(o, 512)
    o = r1.take(512); crow = vf32(o, 128)

    pg.dma("sp", lambda e: e.dma_start(out=xr[0:NHALO, 15, :], in_=xh[:, :]), writes=[xkey(15)])
    pg.dma("sp", lambda e: e.dma_start(out=gbc_a, in_=g_mix.partition_broadcast(128)), writes=[("gbc_a",)])
    for s_ in range(4):
        pg.dma("sp", lambda e, s_=s_: e.dma_start(out=xr[:, s_, :], in_=xp[128 * s_:128 * (s_ + 1), :]), writes=[xkey(s_)])
    setup_consts(crow)
    pg.dma("pool", lambda e: e.dma_start(out=wpl, in_=w_pool.rearrange("g c d -> c g d")), writes=[("wpl",)])
    for kk in range(4):
        pg.dma("pool", lambda e, kk=kk: e.dma_start(out=win[:, 2 * kk:2 * kk + 2, :],
                                                    in_=w_in[256 * kk:256 * (kk + 1), :].rearrange("(k p) n -> p k n", p=128)),
               writes=[("win", kk)])
    pg.op("pool", lambda e: e.memset(srow[0:32, :], 0.0), writes=[("srow",)])
    pg.dma("sp", lambda e: e.dma_start(out=srow[0:2, :], in_=sconv[:, :]), writes=[("srow",)])
    pg.dma("sp", lambda e: e.dma_start(out=srow[2:17, :], in_=spool[:, :]), writes=[("srow",)])

    WIN_KEYS = [("win", kk) for kk in range(4)]
    WOUT_KEYS = [("wout", kk) for kk in range(2)]
    ZB = [2, 3, 4, 0, 1]
    zrot = [0]

    def zbank():
        b = ZB[zrot[0] % len(ZB)]
        zrot[0] += 1
        return b

    def zmm(tile, col0, bank):
        ntok = tile[1]
        hT, hTkey = (hT_halo, ("hT_halo",)) if tile[0] == "halo" else (hT1, ("hT1",))
        fns = []
        for k in range(8):
            fns.append(lambda e, k=k: e.matmul(PS[bank][:, 0:ntok], lhsT=win[:, k, col0:col0 + 128], rhs=hT[:, k, 0:ntok],
                                               start=(k == 0), stop=(k == 7)))
        pg.op("pe", fns, reads=[("win", kk) for kk in range(4)] + [hTkey], writes=[("ps", bank)])

    def pool_sums(tile, c):
        ntok = tile[1]
        nlev = c + 1
        L = ntok + 15
        src = U[:, c, :]
        skey_src = ("U", c)
        for lev in range(nlev):
            sh = 1 << lev
            lo = (1 << (lev + 1)) - 1
            if lev == nlev - 1:
                lo = 15
            dst = Stmp[:, lev % 2, :]
            pg.op("pool", lambda e, dst=dst, src=src, lo=lo, sh=sh, L=L: e.tensor_tensor(
                out=dst[:, lo:L], in0=src[:, lo:L], in1=src[:, lo - sh:L - sh], op=ALU.add),
                reads=[skey_src], writes=[("Stmp", lev % 2)])
            src = dst
            skey_src = ("Stmp", lev % 2)
        return src[:, 15:15 + ntok], skey_src

    def zstage(tile, full=True, first_prompt=False):
        name, ntok, nsub, rows, xs0 = tile
        for c in range(4):
            b = zbank()
            zmm(tile, 1536 + 128 * c, b)
            pg.op("act", lambda e, c=c, b=b: e.activation(out=U[:, c, 15:15 + ntok], in_=PS[b][:, 0:ntok], func=AF.Copy),
                  reads=[("ps", b)], writes=[("U", c)])
            if not full:
                continue
            sums, sk = pool_sums(tile, c)
            oth = 1 - (c % 2)
            tmpd = Stmp[:, oth, 15:15 + ntok]
            invw = 1.0 / WINDOWS[c]
            pg.op("pool", lambda e, sums=sums, tmpd=tmpd, invw=invw: e.tensor_scalar(out=tmpd, in0=sums, scalar1=invw, scalar2=0.0,
                                                                                     op0=ALU.mult, op1=ALU.add),
                  reads=[sk], writes=[("Stmp", oth)])
            if first_prompt:
                pg.op("pool", lambda e, c=c, sums=sums, tmpd=tmpd: e.tensor_tensor(out=tmpd[:, 0:16], in0=sums[:, 0:16],
                                                                                   in1=invc[:, c, :], op=ALU.mult),
                      reads=[sk, ("invc",), ("Stmp", oth)], writes=[("Stmp", oth)])
            pg.op("pool", lambda e, c=c, tmpd=tmpd: e.tensor_tensor(out=dT[:, c, 0:ntok], in0=tmpd, in1=U[:, c, 15:15 + ntok],
                                                                    op=ALU.subtract),
                  reads=[("Stmp", oth), ("U", c)], writes=[("dT", c)])
        for c in range(4):
            slot = c % 2
            b = zbank()
            zmm(tile, 1024 + 128 * c, b)
            pg.op("act", lambda e, b=b, slot=slot: e.activation(out=vS[:, slot, 0:ntok], in_=PS[b][:, 0:ntok], func=AF.Copy),
                  reads=[("ps", b)], writes=[("vS", slot)])
            b = zbank()
            zmm(tile, 512 + 128 * c, b)
            pg.op("dve", lambda e, b=b, c=c, slot=slot: e.tensor_tensor(out=hcv[:, c, 2:2 + ntok], in0=PS[b][:, 0:ntok],
                                                                        in1=vS[:, slot, 0:ntok], op=ALU.mult),
                  reads=[("ps", b), ("vS", slot)], writes=[("hcv", c)])
            if not full:
                continue
            pg.op("dve", lambda e, c=c, slot=slot: e.tensor_scalar(out=acc[:, slot, 0:ntok], in0=hcv[:, c, 0:ntok],
                                                                   scalar1=ccol[:, c:c + 1], scalar2=None, op0=ALU.mult),
                  reads=[("hcv", c), ("ccol",)], writes=[("acc", slot)])
            for tap in (1, 2):
                pg.op("dve", lambda e, c=c, slot=slot, tap=tap: e.scalar_tensor_tensor(
                    out=acc[:, slot, 0:ntok], in0=hcv[:, c, tap:tap + ntok], scalar=ccol[:, 4 * tap + c:4 * tap + c + 1],
                    in1=acc[:, slot, 0:ntok], op0=ALU.mult, op1=ALU.add),
                    reads=[("hcv", c), ("ccol",), ("acc", slot)], writes=[("acc", slot)])
            if c == 3 and full:
                for cc in range(4):
                    b2 = zbank()
                    pg.op("pe", lambda e, cc=cc, b2=b2: e.matmul(PS[b2][:, 0:ntok], lhsT=wpl[:, cc, :], rhs=dT[:, cc, 0:ntok],
                                                                 start=True, stop=True),
                          reads=[("wpl",), ("dT", cc)], writes=[("ps", b2)])
                    pg.op("act", lambda e, cc=cc, b2=b2: e.activation(out=mix[:, 4 + cc, 0:ntok], in_=PS[b2][:, 0:ntok], func=AF.Copy,
                                                                      scale=ccol[:, 12 + cc:13 + cc]),
                          reads=[("ps", b2), ("ccol",)], writes=[("mix", 4 + cc)])
            b = zbank()
            zmm(tile, 128 * c, b)
            pg.op("dve", lambda e, b=b, c=c, slot=slot: e.tensor_tensor(out=mix[:, c, 0:ntok], in0=PS[b][:, 0:ntok],
                                                                        in1=acc[:, slot, 0:ntok], op=ALU.mult),
                  reads=[("ps", b), ("acc", slot)], writes=[("mix", c)])

    def hist_roll(tile):
        ntok = tile[1]
        pg.op("dve", lambda e: e.tensor_copy(out=hcv[:, :, 0:2], in_=hcv[:, :, ntok:ntok + 2]),
              reads=[("hcv", c) for c in range(4)], writes=[("hcv", c) for c in range(4)])
        pg.op("pool", lambda e: e.tensor_copy(out=U[:, :, 0:15], in_=U[:, :, ntok:ntok + 15]),
              reads=[("U", c) for c in range(4)], writes=[("U", c) for c in range(4)])

    OB = [5, 6, 7, 2, 3, 4]
    orot = [0]

    def wout_stage(tile, pre_group=None, post_group=None, tail=()):
        name, ntok, nsub, rows, xs0 = tile
        pre_group = pre_group or {}
        post_group = post_group or {}
        gi = -1
        for s in range(nsub):
            for hf in range(2):
                gi += 1
                for st in pre_group.get(gi, ()):
                    st()
                b = OB[orot[0] % len(OB)]; orot[0] += 1
                fns = []
                korder = (4, 5, 6, 7, 0, 1, 2, 3)
                for ki, k in enumerate(korder):
                    fns.append(lambda e, k=k, ki=ki, s=s, hf=hf, b=b: e.matmul(PS[b][0:rows, :], lhsT=mix[:, k, s * 128:s * 128 + rows],
                                                                               rhs=wout[:, k, hf * 512:(hf + 1) * 512],
                                                                               start=(ki == 0), stop=(ki == 7)))
                pg.op("pe", fns, reads=[("mix", k) for k in range(8)] + WOUT_KEYS, writes=[("ps", b)])
                pg.op("dve", lambda e, s=s, hf=hf, b=b: e.tensor_tensor(out=xr[:rows, xs0 + s, hf * 512:(hf + 1) * 512],
                                                                        in0=xr[:rows, xs0 + s, hf * 512:(hf + 1) * 512],
                                                                        in1=PS[b][0:rows, :], op=ALU.add),
                      reads=[("ps", b), xkey(xs0 + s)], writes=[xkey(xs0 + s)])
                for st in post_group.get(gi, ()):
                    st()
        for st in tail:
            st()

    def state_out(tile, dconv, dpool):
        ntok = tile[1]
        pg.op("dve", lambda e: e.tensor_copy(out=so_col[:, :, 0:2], in_=hcv[:, :, ntok:ntok + 2]),
              reads=[("hcv", c) for c in range(4)], writes=[("so_col",)])
        pg.op("dve", lambda e: e.tensor_copy(out=so_col[:, :, 2:17], in_=U[:, :, ntok:ntok + 15]),
              reads=[("U", c) for c in range(4)], writes=[("so_col",)])
        fns = [lambda e, c=c: e.transpose(out=PS[7][0:17, c * 128:(c + 1) * 128], in_=so_col[:, c, :], identity=identf[:, :])
               for c in range(4)]
        pg.op("pe", fns, reads=[("so_col",), ("identf",)], writes=[("ps", 7)])
        pg.op("act", lambda e: e.activation(out=so_row[0:17, :], in_=PS[7][0:17, :], func=AF.Copy),
              reads=[("ps", 7)], writes=[("so_row",)])
        pg.dma("sp", lambda e: e.dma_start(out=dconv[:, :], in_=so_row[0:2, :]), reads=[("so_row",)], store=True)
        pg.dma("sp", lambda e: e.dma_start(out=dpool[:, :], in_=so_row[2:17, :]), reads=[("so_row",)], store=True)

    def state_in():
        fns = [lambda e, c=c: e.transpose(out=PS[7][:, c * 18:(c + 1) * 18], in_=srow[0:18, c * 128:(c + 1) * 128],
                                          identity=identf[0:18, 0:18]) for c in range(4)]
        pg.op("pe", fns, reads=[("srow",), ("identf",)], writes=[("ps", 7)])
        pv = PS[7][:, 0:72].rearrange("p (c n) -> p c n", c=4)
        pg.op("dve", lambda e: e.tensor_copy(out=hcv[:, :, 0:2], in_=pv[:, :, 0:2]),
              reads=[("ps", 7)] + [("hcv", c) for c in range(4)], writes=[("hcv", c) for c in range(4)])
        pg.op("dve", lambda e: e.tensor_copy(out=U[:, :, 0:15], in_=pv[:, :, 2:17]),
              reads=[("ps", 7)] + [("U", c) for c in range(4)], writes=[("U", c) for c in range(4)])

    def nst1_a(tile):
        nstage_a(tile, gbc_a, ("gbc_a",), junk1, hb1)

    def nst1_a_steps(tile):
        return nstage_a_steps(tile, gbc_a, ("gbc_a",), junk1, hb1)

    def nst1_b(tile):
        nstage_b(tile, hb1, hT1, ("hT1",), (0, 1))

    def nst3_a_steps(i, tile):
        return nstage_a_steps(tile, gbc_a, ("gbc_a",), junk3, hb3[:, i % 2, :, :], hbk="hb2_%d" % (i % 2))

    def nst3_b(i, tile):
        nstage_b(tile, hb3[:, i % 2, :, :], hT3[:, i % 2, :, :], ("hT3", i % 2), (0, 1), hbk="hb2_%d" % (i % 2))

    nsteps = {}

    def get_n(k):
        if k not in nsteps:
            st = nst3_a_steps(k, MAIN[k]); n = MAIN[k][2]
            nsteps[k] = (st[:n], st[n], st[n + 1:])
        return nsteps[k]

    halo = TILES[0]
    nst1_a(halo)
    nstage_b(halo, hb1, hT_halo, ("hT_halo",), (0, 1))
    nst1_a(MAIN[0])
    nst1_b(MAIN[0])
    for kk in range(2):
        pg.dma("pool", lambda e, kk=kk: e.dma_start(out=wout[:, 4 * kk:4 * kk + 4, :],
                                                    in_=w_out[512 * kk:512 * (kk + 1), :].rearrange("(k p) n -> p k n", p=128)),
               writes=[("wout", kk)])
    zstage(halo, full=False)
    halo_done = (pg.sem["pe"], pg.cnt["pe"], "pe")
    hist_roll(halo)
    for t in (1, 2):
        pg.dma("sp", lambda e, t=t: e.dma_start(out=xr[:, 4 * t:4 * t + 4, :],
                                                in_=xp[TS * t:TS * (t + 1), :].rearrange("(s p) d -> p s d", p=128)),
               writes=[xkey(4 * t + s) for s in range(4)], after=[halo_done])
    pg.dma("sp", lambda e: e.dma_start(out=xr[0:NSAMP, 16, :], in_=xs[:, :]), writes=[xkey(16)], after=[halo_done])
    pg.dma("sp", lambda e: e.dma_start(out=xr[:, 12:16, :], in_=xp[TS * 3:TS * 4, :].rearrange("(s p) d -> p s d", p=128)),
           writes=[xkey(12 + s) for s in range(4)])
    pg.dma("pool", lambda e: e.dma_start(out=up_b0, in_=w_up[:, 0:FB].rearrange("(k p) n -> p k n", p=128)), writes=[("up", 0)])
    pg.dma("pool", lambda e: e.dma_start(out=dn_b0, in_=w_down[0:FB, :].rearrange("(k p) n -> p k n", p=128)), writes=[("dn", 0)])

    st1 = {}

    def get1(k):
        if k not in st1:
            st = nst1_a_steps(MAIN[k]); n = MAIN[k][2]
            st1[k] = (st[:n], st[n], st[n + 1:])
        return st1[k]

    for i, tile in enumerate(MAIN):
        nxt = MAIN[i + 1] if i + 1 < len(MAIN) else None
        if nxt is not None and i >= 1:
            for st in get1(i + 1)[2]:
                st()
        if tile[0] == "smp":
            state_in()
        if tile[0] == "p3":
            pre0 = nstage_a_steps(MAIN[0], gbc_a, ("gbc_a",), junk1, hb1, hbk="hb")
            pre1 = nstage_a_steps(MAIN[1], gbc_a, ("gbc_a",), junk1, hb2[:, 1, :, :], hbk="hb2_1")
        zstage(tile, full=True, first_prompt=(i == 0))
        if tile[0] == "smp":
            for st in pre0[MAIN[0][2] + 1:]:
                st()
            pre2 = nstage_a_steps(MAIN[2], gbc_a, ("gbc_a",), junk1, hb2[:, 0, :, :], hbk="hb2_0")
            pre3 = nstage_a_steps(MAIN[3], gbc_a, ("gbc_a",), junk1, hb2[:, 1, :, :], hbk="hb2_1")
            for st in pre2[:MAIN[2][2] + 1] + pre3[:MAIN[3][2] + 1]:
                st()
        pre_g, post_g, tail = {}, {}, []
        ngrp = 2 * tile[2]
        if nxt is not None and i == 0:
            sq, pw, sc = get1(1)
            for q, st in enumerate(sq):
                pre_g.setdefault(min(ngrp - 1, q // 2), []).append(st)
            post_g.setdefault(min(ngrp - 1, 1), []).append(pw)
            for q, st in enumerate(sc):
                post_g.setdefault(min(ngrp - 1, 2 + q), []).append(st)
            tsteps = nstage_b_steps(nxt, hb1, hT1, ("hT1",), (0, 1))
            post_g.setdefault(min(ngrp - 1, 6), []).append(tsteps[0])
            post_g.setdefault(min(ngrp - 1, 7), []).append(tsteps[1])
            tail += tsteps[2:]
        elif nxt is not None:
            tsteps = nstage_b_steps(nxt, hb1, hT1, ("hT1",), (0, 1))
            for q, st in enumerate(tsteps):
                post_g.setdefault(min(ngrp - 1, 2 * q + 1), []).append(st)
        if i + 2 < len(MAIN):
            sq, pw, _ = get1(i + 2)
            for q, st in enumerate(sq):
                pre_g.setdefault(min(ngrp - 1, 2 * q), []).append(st)
            tail.append(pw)
        if tile[0] == "p3":
            n0, n1 = MAIN[0][2], MAIN[1][2]
            for q, st in enumerate(pre0[:n0] + pre1[:n1]):
                pre_g.setdefault(min(ngrp - 1, q), []).append(st)
            tail += [pre0[n0], pre1[n1]]
        wout_stage(tile, pre_g, post_g, tail)
        if nxt is not None and nxt[0] == "smp":
            pg.dma("sp", lambda e: e.dma_start(out=gbc_a, in_=g_mlp.partition_broadcast(128)), writes=[("gbc_a",)])
        if tile[0] == "p3":
            state_out(tile, ncp, npp)
        if tile[0] == "smp":
            state_out(tile, ncs, nps)
        else:
            hist_roll(tile)

    if STOP_AFTER_PHASE >= 2:
        pg.barrier()
        UPB = [up_b0, up_b1]; DNB = [dn_b0, dn_b1]

        tokoff = {}
        off = 0
        for tile in MAIN:
            tokoff[tile[0]] = off
            off += tile[1]

        def load_w2(j):
            pg.dma("pool", lambda e, j=j: e.dma_start(out=UPB[j % 2], in_=w_up[:, j * FB:(j + 1) * FB].rearrange("(k p) n -> p k n", p=128)),
                   writes=[("up", j % 2)])
            pg.dma("pool", lambda e, j=j: e.dma_start(out=DNB[j % 2], in_=w_down[j * FB:(j + 1) * FB, :].rearrange("(k p) n -> p k n", p=128)),
                   writes=[("dn", j % 2)])

        UB = [2, 3, 4]; DB = [5, 6, 7]
        UB4 = [2, 3, 4, 0]; DB4 = [5, 6, 7, 1]
        urot = [0]; drot = [0]

        def up_stage(j, tile, w, post=()):
            name, ntok, nsub, rows, xs0 = tile
            t0 = tokoff[name]
            post = list(post)
            for c in range(4):
                ub = UB if j == 0 else UB4
                b = ub[urot[0] % len(ub)]; urot[0] += 1
                fns = [lambda e, k=k, c=c, b=b: e.matmul(PS[b][:, 0:ntok], lhsT=UPB[j % 2][:, k, c * 128:(c + 1) * 128],
                                                         rhs=h2T[:, k, t0:t0 + ntok], start=(k == 0), stop=(k == 7))
                       for k in range(8)]
                pg.op("pe", fns, reads=[("up", j % 2), ("h2T", name)], writes=[("ps", b)])
                rs = c % 2
                pg.op("act", lambda e, b=b, rs=rs: e.activation(out=rtmp[:, rs, 0:ntok], in_=PS[b][:, 0:ntok], func=AF.Relu),
                      reads=[("ps", b)], writes=[("rtmp", rs)])
                pg.op("dve", lambda e, c=c, rs=rs: e.tensor_tensor(out=fT[:, w % 2, c, 0:ntok], in0=rtmp[:, rs, 0:ntok],
                                                                   in1=rtmp[:, rs, 0:ntok], op=ALU.mult),
                      reads=[("rtmp", rs)], writes=[("fT", w % 2, c)])
                if c < len(post):
                    post[c]()

        def down_stage(j, tile, w):
            name, ntok, nsub, rows, xs0 = tile
            for s in range(nsub):
                for hf in range(2):
                    db = DB if j == 0 else DB4
                    b = db[drot[0] % len(db)]; drot[0] += 1
                    fns = [lambda e, k=k, s=s, hf=hf, b=b: e.matmul(PS[b][0:rows, :], lhsT=fT[:, w % 2, k, s * 128:s * 128 + rows],
                                                                    rhs=DNB[j % 2][:, k, hf * 512:(hf + 1) * 512],
                                                                    start=(k == 0), stop=(k == 3))
                           for k in range(4)]
                    pg.op("pe", fns, reads=[("fT", w % 2, k) for k in range(4)] + [("dn", j % 2)], writes=[("ps", b)])
                    pg.op("dve", lambda e, s=s, hf=hf, b=b: e.tensor_tensor(out=xr[:rows, xs0 + s, hf * 512:(hf + 1) * 512],
                                                                            in0=xr[:rows, xs0 + s, hf * 512:(hf + 1) * 512],
                                                                            in1=PS[b][0:rows, :], op=ALU.add),
                          reads=[("ps", b), xkey(xs0 + s)], writes=[xkey(xs0 + s)])

        pg.dma("pool", lambda e: e.dma_start(out=wple, in_=w_ple.rearrange("(k p) n -> p k n", p=128)), writes=[("wple",)])

        def n2a(ti):
            nstage_a(MAIN[ti], gbc_a, ("gbc_a",), junk2, hb2[:, ti % 2, :, :], hbk="hb2_%d" % (ti % 2))

        def n2b(ti):
            tile = MAIN[ti]
            t0 = tokoff[tile[0]]
            nstage_b(tile, hb2[:, ti % 2, :, :], h2T[:, :, t0:t0 + tile[1]], ("h2T", tile[0]), (0, 1), hbk="hb2_%d" % (ti % 2))

        prev = None
        w = 0
        for j in range(NB):
            if j == 2:
                for kk in range(2):
                    pg.dma("pool", lambda e, kk=kk: e.dma_start(out=wgate[:, 4 * kk:4 * kk + 4, :],
                                                                in_=w_gate[512 * kk:512 * (kk + 1), :].rearrange("(k p) n -> p k n", p=128)),
                           writes=[("wgate", kk)])
            for ti, tile in enumerate(MAIN):
                if j == 0 and ti == 0:
                    for st in pre1[MAIN[1][2] + 1:]:
                        st()
                    n2b(0)
                if j == 0 and ti + 1 < len(MAIN):
                    nt_ = MAIN[ti + 1]
                    t0_ = tokoff[nt_[0]]
                    k_ = (ti + 1) % 2
                    tsteps = nstage_b_steps(nt_, hb2[:, k_, :, :], h2T[:, :, t0_:t0_ + nt_[1]], ("h2T", nt_[0]), (0, 1),
                                            hbk="hb2_%d" % k_)
                    if ti + 2 == 2:
                        ssteps = pre2[MAIN[2][2] + 1:]
                    elif ti + 2 == 3:
                        ssteps = pre3[MAIN[3][2] + 1:]
                    elif ti + 2 < len(MAIN):
                        ssteps = [lambda: n2a(ti + 2)]
                    else:
                        ssteps = []
                    post_ = []
                    for c_ in range(4):
                        def both(c_=c_, tsteps=tsteps, ssteps=ssteps):
                            tsteps[c_]()
                            if c_ < len(ssteps):
                                ssteps[c_]()
                        post_.append(both)
                    up_stage(j, tile, w, post=post_)
                else:
                    up_stage(j, tile, w)
                if prev is not None:
                    down_stage(*prev)
                    if prev[0] == NB - 1 and prev[1] in (MAIN[0], MAIN[1]):
                        kk = 0 if prev[1] is MAIN[0] else 1
                        sq, pw, sc = get_n(kk)
                        for st in sq:
                            st()
                        pw()
                        for st in sc:
                            st()
                if j == 1 and ti == 0:
                    pg.dma("sp", lambda e: e.dma_start(out=gbc_a, in_=g_ple.partition_broadcast(128)), writes=[("gbc_a",)])

                if tile is MAIN[0] and j + 1 < NB:
                    load_w2(j + 1)
                prev = (j, tile, w)
                w += 1
        down_stage(*prev)

    if STOP_AFTER_PHASE >= 3:
        pg.barrier()
        pg.dma("sp", lambda e: e.dma_start(out=gbc_b, in_=g_final.partition_broadcast(128)), writes=[("gbc_b",)])

        def load_p(i, tile):
            name, ntok, nsub, rows, xs0 = tile
            if name == "smp":
                pg.dma("pool", lambda e: e.dma_start(out=pb[0:NSAMP, i % 2, 0, :], in_=psm[:, :]), writes=[("pb", i % 2)])
            else:
                t = int(name[1:])
                pg.dma("pool", lambda e, t=t: e.dma_start(out=pb[:, i % 2, :, :],
                                                          in_=pp[TS * t:TS * (t + 1), :].rearrange("(s p) n -> p s n", p=128)),
                       writes=[("pb", i % 2)])


        yrot = [0]; srot = [0]

        def ple_pre(i, tile):
            name, ntok, nsub, rows, xs0 = tile
            tv = ps_bf(0)
            fns = []
            for s in range(nsub):
                for k in range(2):
                    fns.append(lambda e, s=s, k=k: e.transpose(out=tv[:, k, s * 128:s * 128 + rows],
                                                               in_=pb[:rows, i % 2, s, k * 128:(k + 1) * 128],
                                                               identity=identb[:rows, :rows]))
            pg.op("pe", fns, reads=[("pb", i % 2), ("const",)], writes=[("ps", 0)])
            pg.op("act", lambda e: e.activation(out=pT[:, :, 0:ntok], in_=tv[:, :, 0:ntok], func=AF.Copy),
                  reads=[("ps", 0)], writes=[("pT",)])

        def ple_main(i, tile, bg_act, bg_dve, post_group=None):
            name, ntok, nsub, rows, xs0 = tile
            post_group = post_group or {}
            ngrp = 2 * nsub
            pa = -(-len(bg_act) // ngrp) if bg_act else 0
            pd = -(-len(bg_dve) // ngrp) if bg_dve else 0
            gi = 0
            for s in range(nsub):
                for hf in range(2):
                    for st in bg_act[gi * pa:(gi + 1) * pa]:
                        st()
                    for st in bg_dve[gi * pd:(gi + 1) * pd]:
                        st()
                    gi += 1
                    q = srot[0] % 3; srot[0] += 1
                    bp = (2, 3, 7)[q]; bg = (4, 5, 6)[q]
                    fns = [lambda e, k=k, s=s, hf=hf, bp=bp: e.matmul(PS[bp][0:rows, :], lhsT=pT[:, k, s * 128:s * 128 + rows],
                                                                      rhs=wple[:, k, hf * 512:(hf + 1) * 512],
                                                                      start=(k == 0), stop=(k == 1)) for k in range(2)]
                    pg.op("pe", fns, reads=[("pT",), ("wple",)], writes=[("ps", bp)])
                    fns = [lambda e, k=k, s=s, hf=hf, bg=bg: e.matmul(PS[bg][0:rows, :], lhsT=hT3[:, i % 2, k, s * 128:s * 128 + rows],
                                                                      rhs=wgate[:, k, hf * 512:(hf + 1) * 512],
                                                                      start=(k == 0), stop=(k == 7)) for k in range(8)]
                    pg.op("pe", fns, reads=[("hT3", i % 2), ("wgate", 0), ("wgate", 1)], writes=[("ps", bg)])
                    pg.op("act", lambda e, q=q, bg=bg: e.activation(out=sg[:rows, q, :], in_=PS[bg][0:rows, :], func=AF.Sigmoid),
                          reads=[("ps", bg)], writes=[("sg", q)])
                    pg.op("dve", lambda e, q=q, bp=bp: e.tensor_tensor(out=ptmp[:rows, q, :], in0=PS[bp][0:rows, :],
                                                                       in1=sg[:rows, q, :], op=ALU.mult),
                          reads=[("ps", bp), ("sg", q)], writes=[("ptmp", q)])
                    pg.op("pool", lambda e, s=s, hf=hf, q=q: e.tensor_tensor(out=xr[:rows, xs0 + s, hf * 512:(hf + 1) * 512],
                                                                             in0=xr[:rows, xs0 + s, hf * 512:(hf + 1) * 512],
                                                                             in1=ptmp[:rows, q, :], op=ALU.add),
                          reads=[("ptmp", q), xkey(xs0 + s)], writes=[xkey(xs0 + s)])
                    for st in post_group.get(gi - 1, ()):
                        st()

        def final_steps(i, tile):
            name, ntok, nsub, rows, xs0 = tile
            k = statflip[0]; statflip[0] = (statflip[0] + 1) % NSTAT
            ssq = stat[:, k * 24: k * 24 + 8]; std = stat[:, k * 24 + 8: k * 24 + 16]; rstd = stat[:, k * 24 + 16: k * 24 + 24]
            skey = ("stat", k)
            steps = []
            for s in range(nsub):
                steps.append(lambda s=s: pg.op("act", lambda e: e.activation(out=junk3[:rows, :], in_=xr[:rows, xs0 + s, :], func=AF.Square,
                                                                            accum_out=ssq[:rows, s:s + 1]),
                                               reads=[xkey(xs0 + s)], writes=[("junk",), skey]))

            def powstep():
                pg.op("pool", lambda e: e.tensor_scalar(out=std[:rows, 0:nsub], in0=ssq[:rows, 0:nsub], scalar1=1.0 / D, scalar2=EPS,
                                                        op0=ALU.mult, op1=ALU.add), reads=[skey], writes=[skey])
                pg.op("pool", lambda e: e.tensor_tensor(out=rstd[:rows, 0:nsub], in0=std[:rows, 0:nsub], in1=mhalf[:rows, 0:nsub],
                                                        op=ALU.pow), reads=[skey, ("const",)], writes=[skey])
            steps.append(powstep)

            def ystep(s):
                ys = yrot[0] % 4; yrot[0] += 1
                pg.op("dve", lambda e: e.scalar_tensor_tensor(out=ytile[:rows, ys, :], in0=xr[:rows, xs0 + s, :],
                                                              scalar=rstd[:rows, s:s + 1], in1=gbc_b[:rows, :],
                                                              op0=ALU.mult, op1=ALU.mult),
                      reads=[xkey(xs0 + s), skey, ("gbc_b",)], writes=[("ytile", ys)])
                if name == "smp":
                    pg.dma("sp", lambda e: e.dma_start(out=y_s[:, :], in_=ytile[0:NSAMP, ys, :]), reads=[("ytile", ys)], store=True)
                else:
                    r0 = (xs0 + s) * 128
                    pg.dma("sp", lambda e: e.dma_start(out=y_p[r0:r0 + 128, :], in_=ytile[:, ys, :]), reads=[("ytile", ys)], store=True)
            for s in range(nsub):
                steps.append(lambda s=s: ystep(s))
            return steps

        def merge(a, b):
            out = []
            for j in range(max(len(a), len(b))):
                if j < len(a):
                    out.append(a[j])
                if j < len(b):
                    out.append(b[j])
            return out

        nT = len(MAIN)
        NSUB = [t[2] for t in MAIN]
        fsteps = {}

        def get_f(k):
            if k not in fsteps:
                st = final_steps(k, MAIN[k]); n = NSUB[k]
                fsteps[k] = (st[:n], st[n], st[n + 1:])
            return fsteps[k]

        def run(steps):
            for st in steps:
                st()

        load_p(0, MAIN[0])
        nst3_b(0, MAIN[0]); ple_pre(0, MAIN[0])
        for i, tile in enumerate(MAIN):
            if i + 1 < nT:
                load_p(i + 1, MAIN[i + 1])
            bg_act, bg_dve, pows = [], [], []
            post_g = {}
            ngrp = 2 * tile[2]
            nF = 0
            if i >= 1:
                sq, pwF, yF = get_f(i - 1); bg_act += sq; nF = len(sq)
            if i + 2 < nT:
                sq, pw, _ = get_n(i + 2); bg_act += sq; pows.append(pw)
            if i >= 1 and i + 1 < nT:
                bg_dve += get_n(i + 1)[2]
            tail = []
            if i >= 1 and i == nT - 1:
                bg_act.append(pwF)
                tail += list(yF)
            elif i >= 1:
                pa = -(-len(bg_act) // ngrp)
                gF = (nF - 1) // pa
                post_g.setdefault(gF, []).append(pwF)
                for q, st in enumerate(yF):
                    if gF + 1 + q < ngrp:
                        post_g.setdefault(gF + 1 + q, []).append(st)
                    else:
                        tail.append(st)
            if i + 1 < nT:
                k1 = (i + 1) % 2
                tsteps = nstage_b_steps(MAIN[i + 1], hb3[:, k1, :, :], hT3[:, k1, :, :], ("hT3", k1), (0, 1), hbk="hb2_%d" % k1)
                for q, st in enumerate(tsteps):
                    post_g.setdefault(min(ngrp - 1, max(0, ngrp - 4) + q), []).append(st)
            ple_main(i, tile, bg_act, bg_dve, post_g)
            for pw in pows:
                pw()
            run(tail)
            if i + 1 < nT:
                ple_pre(i + 1, MAIN[i + 1])
        sq, pw, sc = get_f(nT - 1)
        run(sq); pw()
        run(sc)

    if STOP_AFTER_PHASE < 3:
        for t in range(4):
            pg.dma("sp", lambda e, t=t: e.dma_start(out=y_p[TS * t:TS * (t + 1), :].rearrange("(s p) d -> p s d", p=128),
                                                    in_=xr[:, 4 * t:4 * t + 4, :]),
                   reads=[xkey(4 * t + s) for s in range(4)], store=True)
        pg.dma("sp", lambda e: e.dma_start(out=y_s[:, :], in_=xr[0:NSAMP, 16, :]), reads=[xkey(16)], store=True)

    pg.final_wait()
    with nc.Block() as block:
        pg.emit(block)
    pg.close()
    for cm in reversed(ps_cms):
        cm.__exit__(None, None, None)
    arena_cm.__exit__(None, None, None)
    return nc


def _invcnt(core):
    out = np.empty((128, 4, 16), np.float32)
    for g, w in enumerate(WINDOWS):
        if core == 0:
            cnt = np.minimum(np.arange(16) + 1, w).astype(np.float32)
        else:
            cnt = np.full(16, float(w), np.float32)
        out[:, g, :] = (1.0 / cnt)[None, :]
    return out.reshape(128, 64)


def kernel(x_prompt, x_sample, state_conv, state_pool, p_prompt, p_sample, g_mix, w_in, w_conv,
           w_pool, pool_scale, w_out, g_mlp, w_up, w_down, g_ple, w_ple, w_ple_gate, g_final):
    f = lambda a: np.ascontiguousarray(np.asarray(a, dtype=np.float32))
    xpr = f(x_prompt)[0]; xsm = f(x_sample); ppr = f(p_prompt)[0, 0]; psmp = f(p_sample)[0]
    sc = f(state_conv)[0]; spl = f(state_pool)[0]
    shared = {
        "w_in": f(w_in)[0], "w_out": f(w_out)[0], "w_up": f(w_up)[0], "w_down": f(w_down)[0],
        "w_ple": f(w_ple)[0], "w_gate": f(w_ple_gate)[0], "w_pool": f(w_pool)[0],
        "g_mix": f(g_mix)[0], "g_mlp": f(g_mlp)[0], "g_ple": f(g_ple)[0], "g_final": f(g_final),
        "w_conv": f(w_conv)[0], "pool_scale": f(pool_scale)[0],
    }
    in_maps = []
    for c in range(NCORES):
        m = dict(shared)
        m["xp"] = xpr[c * NPT:(c + 1) * NPT]
        m["xh"] = xpr[c * NPT - NHALO:c * NPT] if c > 0 else np.zeros((NHALO, D), np.float32)
        m["xs"] = xsm[c]
        m["pp"] = ppr[c * NPT:(c + 1) * NPT]
        m["psm"] = psmp[c]
        m["sconv"] = sc[c]
        m["spool"] = spl[c]
        m["invcnt"] = _invcnt(c)
        in_maps.append(m)
    nc = build_program()
    res = run_bass_kernel_spmd(nc, in_maps, core_ids=list(range(NCORES)))
    rs = res.results
    y_prompt = np.concatenate([rs[c]["y_p"] for c in range(NCORES)], axis=0)[None]
    y_sample = np.stack([rs[c]["y_s"] for c in range(NCORES)], axis=0)
    ncp = rs[NCORES - 1]["ncp"][None, None]
    npp = rs[NCORES - 1]["npp"][None, None]
    ncs = np.stack([rs[c]["ncs"] for c in range(NCORES)], axis=0)[None]
    nps = np.stack([rs[c]["nps"] for c in range(NCORES)], axis=0)[None]
    return (y_prompt.astype(np.float32), y_sample.astype(np.float32), ncp.astype(np.float32),
            npp.astype(np.float32), ncs.astype(np.float32), nps.astype(np.float32))
```
